# Optimizing a Trainium2 kernel written in Bass

```python
import jax, jax.numpy as jnp
from jax import lax
import numpy as np

D_MODEL = 1024
BATCH = 8
SEQ = 8192
DEPTH = 1

D_POOL = D_MODEL // 4
POOL_WINDOWS = (2, 4, 8, 16)
N_POOL_GROUPS = len(POOL_WINDOWS)
POOL_GROUP = D_POOL // N_POOL_GROUPS
V_HEAD = 128
QK_NOPE = 128
QK_ROPE = 64
N_HEADS = (D_MODEL - D_POOL) // V_HEAD
D_ATT = N_HEADS * V_HEAD
Q_LORA = 384
KV_LORA = 256
D_IN = D_POOL + Q_LORA + KV_LORA + QK_ROPE
ROPE_THETA = 10000.0
Q_BLOCK = 128
PEER_HEADS = 8
PEER_NKEYS = 128
PEER_EXPERTS = PEER_NKEYS * PEER_NKEYS
PEER_DQ = 256
PEER_HALF = PEER_DQ // 2
PEER_TOPK = 16
TOK_CHUNK = 128
ALPHA = (2.0 * DEPTH) ** 0.25
BETA = (8.0 * DEPTH) ** -0.25
LN_EPS = 1e-5
RMS_EPS = 1e-6

kernel_name = 'hybrid_pool_mla_peer_deepnorm_adaln'


def layer_norm(x, g, b):
    xf = x.astype(jnp.float32)
    mu = jnp.mean(xf, axis=-1, keepdims=True)
    var = jnp.mean(jnp.square(xf - mu), axis=-1, keepdims=True)
    return ((xf - mu) * lax.rsqrt(var + LN_EPS) * g.astype(jnp.float32) + b.astype(jnp.float32)).astype(x.dtype)


def rms_norm(x, g):
    xf = x.astype(jnp.float32)
    y = xf * lax.rsqrt(jnp.mean(jnp.square(xf), axis=-1, keepdims=True) + RMS_EPS)
    return (y * g.astype(jnp.float32)).astype(x.dtype)


def rope(x, cos, sin):
    h = x.shape[-1] // 2
    x1, x2 = x[..., :h], x[..., h:]
    return jnp.concatenate([x1 * cos - x2 * sin, x2 * cos + x1 * sin], axis=-1)


def causal_multiscale_pool(p, w_pool, pool_scale):
    B, S, _ = p.shape
    g = p.astype(jnp.float32).reshape(B, S, N_POOL_GROUPS, POOL_GROUP)
    cs = jnp.concatenate([jnp.zeros_like(g[:, :1]), jnp.cumsum(g, axis=1)], axis=1)
    t = jnp.arange(S)[:, None]
    win = jnp.array(POOL_WINDOWS, dtype=jnp.int32)[None, :]
    start = jnp.maximum(t + 1 - win, 0)
    gid = jnp.arange(N_POOL_GROUPS)[None, :]
    win_sum = cs[:, 1:] - cs[:, start, gid]
    count = jnp.minimum(t + 1, win).astype(jnp.float32)
    mixed = win_sum / count[None, :, :, None] - g
    y = jnp.einsum('bsgc,gcd->bsgd', mixed.astype(p.dtype), w_pool)
    y = y * pool_scale.reshape(N_POOL_GROUPS, POOL_GROUP)
    return y.reshape(B, S, D_POOL)


def mla_attention(cq, ckv, kr, cos, sin, q_norm_g, w_uq, kv_norm_g, w_ukv):
    B, S, _ = cq.shape
    q = (rms_norm(cq, q_norm_g) @ w_uq).reshape(B, S, N_HEADS, QK_NOPE + QK_ROPE)
    q_nope = q[..., :QK_NOPE]
    q_rope = rope(q[..., QK_NOPE:], cos[:, :, None], sin[:, :, None])
    kv = (rms_norm(ckv, kv_norm_g) @ w_ukv).reshape(B, S, N_HEADS, QK_NOPE + V_HEAD)
    k_nope, v = kv[..., :QK_NOPE], kv[..., QK_NOPE:]
    k_rope = rope(kr, cos, sin)
    scale = (QK_NOPE + QK_ROPE) ** -0.5
    nb = S // Q_BLOCK
    k_idx = jnp.arange(S)

    def attend(args):
        qn, qr, i = args
        s = (jnp.einsum('bqhd,bkhd->bhqk', qn, k_nope).astype(jnp.float32)
             + jnp.einsum('bqhr,bkr->bhqk', qr, k_rope).astype(jnp.float32)) * scale
        q_idx = i * Q_BLOCK + jnp.arange(Q_BLOCK)
        s = jnp.where(q_idx[:, None] >= k_idx[None, :], s, -jnp.inf)
        pr = jax.nn.softmax(s, axis=-1)
        return jnp.einsum('bhqk,bkhd->bqhd', pr.astype(v.dtype), v)

    qn_b = q_nope.reshape(B, nb, Q_BLOCK, N_HEADS, QK_NOPE).transpose(1, 0, 2, 3, 4)
    qr_b = q_rope.reshape(B, nb, Q_BLOCK, N_HEADS, QK_ROPE).transpose(1, 0, 2, 3, 4)
    out = lax.map(attend, (qn_b, qr_b, jnp.arange(nb)))
    return out.transpose(1, 0, 2, 3, 4).reshape(B, S, D_ATT)


def peer(h, w_peer_q, peer_keys, peer_u, peer_v):
    B, S, D = h.shape
    tokens = h.reshape((B * S) // TOK_CHUNK, TOK_CHUNK, D)

    def chunk(xc):
        q = (xc @ w_peer_q).reshape(TOK_CHUNK, PEER_HEADS, 2, PEER_HALF)
        sc = jnp.einsum('thpd,hpnd->thpn', q, peer_keys).astype(jnp.float32)
        s1, i1 = lax.top_k(sc[:, :, 0], PEER_TOPK)
        s2, i2 = lax.top_k(sc[:, :, 1], PEER_TOPK)
        cand = (s1[..., :, None] + s2[..., None, :]).reshape(TOK_CHUNK, PEER_HEADS, PEER_TOPK * PEER_TOPK)
        cidx = (i1[..., :, None] * PEER_NKEYS + i2[..., None, :]).reshape(TOK_CHUNK, PEER_HEADS, PEER_TOPK * PEER_TOPK)
        s, j = lax.top_k(cand, PEER_TOPK)
        idx = jnp.take_along_axis(cidx, j, axis=-1)
        gate = jax.nn.softmax(s, axis=-1)
        u = peer_u[idx]
        a = jax.nn.gelu(jnp.einsum('thkd,td->thk', u, xc).astype(jnp.float32), approximate=False)
        w = (gate * a).astype(xc.dtype)
        return jnp.einsum('thk,thkd->td', w, peer_v[idx])

    out = lax.map(chunk, tokens)
    return out.reshape(B, S, D)


def setup_inputs(seed: int = 0) -> dict:
    key = jax.random.key(seed)
    ks = jax.random.split(key, 24)
    L = DEPTH

    def nrm(k, shape, scale):
        return jax.random.normal(k, shape, jnp.float32) * scale

    x = nrm(ks[0], (BATCH, SEQ, D_MODEL), 1.0)
    c = nrm(ks[1], (BATCH, D_MODEL), 1.0)
    positions = jnp.broadcast_to(jnp.arange(SEQ, dtype=jnp.int32)[None, :], (BATCH, SEQ))
    w_ada = nrm(ks[2], (L, D_MODEL, 6 * D_MODEL), D_MODEL ** -0.5)
    b_ada = nrm(ks[3], (L, 6 * D_MODEL), 0.02)
    w_in = nrm(ks[4], (L, D_MODEL, D_IN), D_MODEL ** -0.5)
    pool_w = nrm(ks[5], (L, N_POOL_GROUPS, POOL_GROUP, POOL_GROUP), BETA * POOL_GROUP ** -0.5)
    pool_scale = 1.0 + nrm(ks[6], (L, D_POOL), 0.02)
    q_norm_g = 1.0 + nrm(ks[7], (L, Q_LORA), 0.02)
    w_uq = nrm(ks[8], (L, Q_LORA, N_HEADS * (QK_NOPE + QK_ROPE)), Q_LORA ** -0.5)
    kv_norm_g = 1.0 + nrm(ks[9], (L, KV_LORA), 0.02)
    w_uk = nrm(ks[10], (L, KV_LORA, N_HEADS, QK_NOPE), KV_LORA ** -0.5)
    w_uv = nrm(ks[11], (L, KV_LORA, N_HEADS, V_HEAD), BETA * KV_LORA ** -0.5)
    w_ukv = jnp.concatenate([w_uk, w_uv], axis=-1).reshape(L, KV_LORA, N_HEADS * (QK_NOPE + V_HEAD))
    w_out = nrm(ks[12], (L, D_POOL + D_ATT, D_MODEL), BETA * (D_POOL + D_ATT) ** -0.5)
    ln1_g = 1.0 + nrm(ks[13], (L, D_MODEL), 0.02)
    ln1_b = nrm(ks[14], (L, D_MODEL), 0.02)
    w_peer_q = nrm(ks[15], (L, D_MODEL, PEER_HEADS * PEER_DQ), D_MODEL ** -0.5)
    peer_keys = nrm(ks[16], (L, PEER_HEADS, 2, PEER_NKEYS, PEER_HALF), PEER_HALF ** -0.5)
    peer_u = nrm(ks[17], (L, PEER_EXPERTS, D_MODEL), D_MODEL ** -0.5)
    peer_v = nrm(ks[18], (L, PEER_EXPERTS, D_MODEL), BETA * (PEER_HEADS * PEER_TOPK) ** -0.5)
    ln2_g = 1.0 + nrm(ks[19], (L, D_MODEL), 0.02)
    ln2_b = nrm(ks[20], (L, D_MODEL), 0.02)
    return {'x': x, 'c': c, 'positions': positions, 'w_ada': w_ada, 'b_ada': b_ada,
            'w_in': w_in, 'pool_w': pool_w, 'pool_scale': pool_scale, 'q_norm_g': q_norm_g,
            'w_uq': w_uq, 'kv_norm_g': kv_norm_g, 'w_ukv': w_ukv, 'w_out': w_out,
            'ln1_g': ln1_g, 'ln1_b': ln1_b, 'w_peer_q': w_peer_q, 'peer_keys': peer_keys,
            'peer_u': peer_u, 'peer_v': peer_v, 'ln2_g': ln2_g, 'ln2_b': ln2_b}


def reference(x, c, positions, w_ada, b_ada, w_in, pool_w, pool_scale, q_norm_g, w_uq,
              kv_norm_g, w_ukv, w_out, ln1_g, ln1_b, w_peer_q, peer_keys, peer_u, peer_v,
              ln2_g, ln2_b):
    inv_freq = ROPE_THETA ** (-jnp.arange(0, QK_ROPE, 2, dtype=jnp.float32) / QK_ROPE)
    ang = positions.astype(jnp.float32)[..., None] * inv_freq
    cos = jnp.cos(ang).astype(x.dtype)
    sin = jnp.sin(ang).astype(x.dtype)
    c_act = jax.nn.silu(c)
    o1 = D_POOL
    o2 = o1 + Q_LORA
    o3 = o2 + KV_LORA
    for l in range(DEPTH):
        mod = c_act @ w_ada[l] + b_ada[l]
        sh1, sc1, g1, sh2, sc2, g2 = [m[:, None, :] for m in jnp.split(mod, 6, axis=-1)]
        h = x * (1.0 + sc1) + sh1
        z = h @ w_in[l]
        pool_out = causal_multiscale_pool(z[..., :o1], pool_w[l], pool_scale[l])
        att_out = mla_attention(z[..., o1:o2], z[..., o2:o3], z[..., o3:], cos, sin,
                                q_norm_g[l], w_uq[l], kv_norm_g[l], w_ukv[l])
        mix = jnp.concatenate([pool_out, att_out], axis=-1) @ w_out[l]
        x = layer_norm(ALPHA * x + g1 * mix, ln1_g[l], ln1_b[l])
        h2 = x * (1.0 + sc2) + sh2
        ffn = peer(h2, w_peer_q[l], peer_keys[l], peer_u[l], peer_v[l])
        x = layer_norm(ALPHA * x + g2 * ffn, ln2_g[l], ln2_b[l])
    return x
```

```python
import math
from contextlib import ExitStack
import numpy as np
import concourse.bass as bass
import concourse.mybir as mybir
from concourse.bass_utils import run_bass_kernel_spmd

F32 = mybir.dt.float32
BF16 = mybir.dt.bfloat16
I32 = mybir.dt.int32
AF = mybir.ActivationFunctionType
ALU = mybir.AluOpType
AX = mybir.AxisListType

SEQ = 8192
D = 1024
NH = 6
ALPHA = 2.0 ** 0.25
LN_EPS = 1e-5
RMS_EPS = 1e-6
QSCALE = 192.0 ** -0.5
T1 = 256
TG = 256
TWO_PI = 2.0 * math.pi
C1 = 6.28125
C2 = TWO_PI - C1
NEG = -3.0e38


class Sched:
    EPOCH = 24000
    NDS = 12

    def __init__(self, nc, st):
        self.nc, self.st = nc, st
        self.eng = dict(pe=nc.tensor, act=nc.scalar, dve=nc.vector, pool=nc.gpsimd, sp=nc.sync)
        self.sem, self.cnt, self.nsem = {}, {}, 0
        for e in self.eng:
            self._new_sem(e)
        self.waited = {e: {} for e in self.eng}
        self.lastw, self.readers = {}, {}
        self.dsem = {q: [[st.enter_context(nc.semaphore(f"d{q}{i}")), 0] for i in range(self.NDS)]
                     for q in ("sp", "pool", "act")}
        self.drr = {q: 0 for q in self.dsem}
        self.nops = 0

    def _new_sem(self, e):
        self.nsem += 1
        self.sem[e] = self.st.enter_context(self.nc.semaphore(f"s{e}{self.nsem}"))
        self.cnt[e] = 0

    def _wait(self, e, tok):
        sem, val = tok
        if val <= 0:
            return
        w = self.waited[e]
        if w.get(sem.num, 0) >= val:
            return
        self.eng[e].wait_ge(sem, val)
        w[sem.num] = val

    def _deps(self, e, r, w):
        deps = []
        for k in r:
            if k in self.lastw:
                deps.append(self.lastw[k])
        for k in w:
            if k in self.lastw:
                deps.append(self.lastw[k])
            deps.extend(self.readers.get(k, ()))
        for d in deps:
            if e == "pe" and d[2] == "pe":
                continue
            self._wait(e, (d[0], d[1]))

    def _commit(self, tok, r, w):
        for k in w:
            self.lastw[k] = tok
            self.readers[k] = []
        for k in r:
            self.readers.setdefault(k, []).append(tok)

    def op(self, e, fn, r=(), w=()):
        self._deps(e, r, w)
        if self.cnt[e] >= self.EPOCH:
            self._new_sem(e)
        ins = fn(self.eng[e])
        self.cnt[e] += 1
        ins.then_inc(self.sem[e], 1)
        tok = (self.sem[e], self.cnt[e], e)
        self._commit(tok, r, w)
        self.nops += 1
        return tok

    def dma(self, out, in_, r=(), w=(), q="sp"):
        slot = self.dsem[q][self.drr[q]]
        self.drr[q] = (self.drr[q] + 1) % self.NDS
        self._wait(q, (slot[0], slot[1]))
        self._deps(q, r, w)
        ins = self.eng[q].dma_start(out=out, in_=in_)
        slot[1] += 16
        ins.then_inc(slot[0], 16)
        tok = (slot[0], slot[1], "dma")
        self._commit(tok, r, w)
        return tok

    def snapshot(self, skip=()):
        toks = [(self.sem[e], self.cnt[e]) for e in self.eng if e not in skip]
        for q in self.dsem:
            if q not in skip:
                toks += [(s[0], s[1]) for s in self.dsem[q]]
        return toks

    def barrier_on(self, toks):
        for e in self.eng:
            for t in toks:
                self._wait(e, t)

    def barrier(self):
        toks = [(self.sem[e], self.cnt[e]) for e in self.eng]
        for q in self.dsem:
            toks += [(s[0], s[1]) for s in self.dsem[q]]
        for e in self.eng:
            for t in toks:
                self._wait(e, t)
        self.lastw.clear()
        self.readers.clear()

    def mm(self, out, lhsT, rhs, start, stop, r, w):
        return self.op("pe", lambda e: e.matmul(out, lhsT, rhs, start=start, stop=stop), r, w)


def build(debug=False, nst=64):
    nc = bass.Bass("TRN2", target_bir_lowering=False)

    def din(name, shape, dt=F32):
        return nc.dram_tensor(name, list(shape), dt, kind="ExternalInput").ap()

    scr_kind = "ExternalOutput" if debug else "Internal"

    def dscr(name, shape, dt):
        return nc.dram_tensor(name, list(shape), dt, kind=scr_kind).ap()

    xT_d = din("xT", [D, SEQ])
    x_d = din("x", [SEQ, D])
    c_d = din("c", [128, 8])
    pos_d = din("pos", [1, SEQ], I32)
    invf_d = din("invf", [64, 2])
    wada_d = din("w_ada", [D, 6 * D])
    bada_fm_d = din("b_ada_fm", [128, 48])
    bada_row_d = din("b_ada_row", [1, 6 * D])
    win_d = din("w_in_ext", [D, 1024])
    poolw_d = din("pool_w", [4, 64, 64])
    pools_d = din("pool_scale_fm", [128, 2])
    invw_d = din("invw", [128, 2])
    invc0_d = din("invc0", [128, 2, 512])
    qg_d = din("q_norm_g_fm", [128, 3])
    kvg_d = din("kv_norm_g_fm", [128, 2])
    wuq_d = din("w_uq_ext", [384, NH * 256])
    wukv_d = din("w_ukv", [256, NH * 256])
    wout_d = din("w_out", [D, D])
    ln_rows_d = din("ln_rows", [4, D])
    ln1_fm_d = din("ln1_fm", [128, 16])
    wpq_d = din("w_peer_q", [D, 2048])
    keysT_d = din("keysT", [128, 16 * 128])
    uT_d = din("uT", [D, 16384])
    v_d = din("v", [16384, D])
    mask_d = din("mask", [128, 4 * 512])
    ident_d = din("ident", [128, 128])
    iotan_d = din("iota_n", [128, 128], I32)
    iotaj_d = din("iota_j", [128, 256], I32)
    iota16_d = din("iota16", [128, 16])
    iota128_d = din("iota128", [128, 128])
    out_d = nc.dram_tensor("out", [SEQ, D], F32, kind="ExternalOutput").ap()

    cos_d = dscr("cos_scr", [64, SEQ], F32)
    sin_d = dscr("sin_scr", [64, SEQ], F32)
    apT_d = dscr("apT_scr", [D, SEQ], BF16)
    ubf_d = dscr("ubf_scr", [128, 8, 16384], BF16)
    vbf_d = dscr("vbf_scr", [128, 128, D], BF16)
    dbg = {}
    if debug:
        dbg["modT"] = nc.dram_tensor("dbg_modT", [128, 48], F32, kind="ExternalOutput").ap()
        dbg["lat"] = nc.dram_tensor("dbg_lat", [128, 6, SEQ], BF16, kind="ExternalOutput").ap()
        dbg["sel"] = nc.dram_tensor("dbg_sel", [128, 3, 128], F32, kind="ExternalOutput").ap()
        dbg["x1"] = nc.dram_tensor("dbg_x1", [128, D], F32, kind="ExternalOutput").ap()
        dbg["ffn"] = nc.dram_tensor("dbg_ffn", [128, D], F32, kind="ExternalOutput").ap()

    with ExitStack() as st:
        S = Sched(nc, st)

        def sb(name, shape, dt=F32, stack=st):
            return stack.enter_context(nc.sbuf_tensor("sb_" + name, list(shape), dt))

        psum = st.enter_context(nc.psum_tensor("psum", [128, 4096], F32))

        def bank(i, n=1):
            return psum[:, i * 512:(i + n) * 512]

        def bk(i, n=1):
            return [f"ps{j}" for j in range(i, i + n)]

        modT = sb("modT", [128, 48])
        sc1p = sb("sc1p", [128, 8])
        A2 = sb("A2", [128, 8])
        B2 = sb("B2", [128, 8])
        ident = sb("ident", [128, 128])
        ones_bf = sb("ones_bf", [128, 128], BF16)
        ones_f = sb("ones_f", [128, 128])
        invf = sb("invf", [64, 2])
        st12 = ExitStack()
        stg1 = ExitStack()
        stg2 = ExitStack()
        stw1 = ExitStack()
        g1_bc = sb("g1_bc", [128, D], stack=stg1)
        w_uq_bf = sb("w_uq_bf", [128, 3, NH * 256], BF16, stack=st12)
        w_ukv_bf = sb("w_ukv_bf", [128, 2, NH * 256], BF16, stack=st12)
        wbd = sb("wbd", [128, 2, 128], BF16, stack=st12)
        pools = sb("pools", [128, 2], stack=st12)
        invw = sb("invw", [128, 2], stack=st12)
        maskb = sb("maskb", [128, 4, 512], BF16, stack=st12)
        cqn = sb("cqn", [128, 3, SEQ], BF16, stack=st12)
        ckvn = sb("ckvn", [128, 2, SEQ], BF16, stack=st12)
        krT = sb("krT", [64, SEQ], BF16, stack=st12)
        g2_bc = sb("g2_bc", [128, D], stack=st12)
        cst = [sb(f"cst{i}", [128, 512], stack=st12) for i in range(2)]
        cbf = [sb(f"cbf{i}", [128, 512], BF16, stack=st12) for i in range(2)]
        w_in_bf = sb("w_in_bf", [128, 8, 1024], BF16, stack=stw1)
        invc0 = sb("invc0", [128, 2, T1], stack=stw1)
        S.dma(ident[:], ident_d, w=["ident"])
        S.dma(invf[:], invf_d, w=["invf"])
        S.op("dve", lambda e: e.memset(ones_f[:], 1.0), w=["ones_f"])
        S.op("dve", lambda e: e.memset(ones_bf[:], 1.0), w=["ones_bf"])

        with ExitStack() as s0:
            c_sb = sb("c_sb", [128, 8], stack=s0)
            c_act = sb("c_act", [128, 8], stack=s0)
            c_act2 = sb("c_act2", [128, 8, 2], stack=s0)
            c_bc = sb("c_bc", [128, 8, 128], stack=s0)
            bfm = sb("bfm", [128, 48], stack=s0)
            ln1fm = sb("ln1fm", [128, 16], stack=s0)
            wa = [sb(f"wa{i}", [128, 8, 256], stack=s0) for i in range(2)]
            S.dma(c_sb[:], c_d, w=["c_sb"])
            S.dma(bfm[:], bada_fm_d, w=["bfm"])
            S.dma(ln1fm[:], ln1_fm_d, w=["ln1fm"])
            S.dma(g1_bc[:], bada_row_d[0:1, 2048:3072].to_broadcast([128, D]), w=["g1_bc"])
            S.dma(g2_bc[:], bada_row_d[0:1, 5120:6144].to_broadcast([128, D]), w=["g2_bc"])
            S.op("act", lambda e: e.activation(out=c_act[:], in_=c_sb[:], func=AF.Silu), r=["c_sb"], w=["c_act"])
            S.op("dve", lambda e: e.tensor_copy(out=c_act2[:], in_=c_act[:].unsqueeze(2).to_broadcast([128, 8, 2])),
                 r=["c_act"], w=["c_act2"])
            S.op("dve", lambda e: e.tensor_copy(out=c_bc[:], in_=c_act[:].unsqueeze(2).to_broadcast([128, 8, 128])),
                 r=["c_act"], w=["c_bc"])
            wada_v = wada_d.rearrange("(k p) c -> p k c", p=128)
            psm = bank(7)[:, 0:96].rearrange("p (j two) -> p j two", two=2)
            for n in range(24):
                wt = wa[n % 2]
                S.dma(wt[:], wada_v[:, :, n * 256:(n + 1) * 256], w=[f"wa{n % 2}"])
                if n in (8, 9, 10, 11, 20, 21, 22, 23):
                    gt = g1_bc if n < 12 else g2_bc
                    gk = "g1_bc" if n < 12 else "g2_bc"
                    off = (n % 4) * 256
                    for k in range(8):
                        S.mm(bank(n % 2)[:, 0:256], c_bc[:, k, :], wt[:, k, :], k == 0, k == 7,
                             r=["c_bc", f"wa{n % 2}"], w=bk(n % 2))
                    S.op("dve", lambda e, gt=gt, off=off, n=n: e.tensor_tensor(
                        out=gt[:, off:off + 256], in0=bank(n % 2)[:, 0:256], in1=gt[:, off:off + 256], op=ALU.add),
                        r=bk(n % 2) + [gk], w=[gk])
                else:
                    for cb in range(2):
                        j = n * 2 + cb
                        for k in range(8):
                            S.mm(psm[:, j, :], wt[:, k, cb * 128:(cb + 1) * 128], c_act2[:, k, :], k == 0, k == 7,
                                 r=["c_act2", f"wa{n % 2}"], w=bk(7))
            S.op("dve", lambda e: e.tensor_tensor(out=modT[:], in0=psm[:, :, 0], in1=bfm[:], op=ALU.add),
                 r=bk(7) + ["bfm"], w=["modT"])
            S.op("dve", lambda e: e.tensor_scalar_add(out=sc1p[:], in0=modT[:, 8:16], scalar1=1.0), r=["modT"], w=["sc1p"])
            S.op("dve", lambda e: e.scalar_tensor_tensor(out=A2[:], in0=modT[:, 32:40], scalar=1.0, in1=ln1fm[:, 0:8],
                                                         op0=ALU.add, op1=ALU.mult), r=["modT", "ln1fm"], w=["A2"])
            S.op("dve", lambda e: e.scalar_tensor_tensor(out=B2[:], in0=modT[:, 32:40], scalar=1.0, in1=ln1fm[:, 8:16],
                                                         op0=ALU.add, op1=ALU.mult), r=["modT", "ln1fm"], w=["B2"])
            S.op("dve", lambda e: e.tensor_tensor(out=B2[:], in0=B2[:], in1=modT[:, 24:32], op=ALU.add),
                 r=["B2", "modT"], w=["B2"])
            if debug:
                S.dma(dbg["modT"], modT[:], r=["modT"])

            pos_i = sb("pos_i", [64, 512], I32, stack=s0)
            ang = sb("ang", [64, 512], stack=s0)
            ang2 = sb("ang2", [64, 512], stack=s0)
            ki = sb("ki", [64, 512], I32, stack=s0)
            kf = sb("kf", [64, 512], stack=s0)
            rr = sb("rr", [64, 512], stack=s0)
            tb = [sb(f"tb{i}", [64, 512], stack=s0) for i in range(2)]
            for ch in range(16):
                sl = slice(ch * 512, (ch + 1) * 512)
                S.dma(pos_i[:], pos_d[0:1, sl].to_broadcast([64, 512]), w=["pos_i"])
                S.op("dve", lambda e: e.tensor_copy(out=ang[:], in_=pos_i[:]), r=["pos_i"], w=["ang"])
                S.op("dve", lambda e: e.tensor_scalar(out=ang[:], in0=ang[:], scalar1=invf[:, 0:1], scalar2=None,
                                                      op0=ALU.mult), r=["ang", "invf"], w=["ang"])
                for which in range(2):
                    src = ang
                    if which == 0:
                        S.op("dve", lambda e: e.tensor_scalar_add(out=ang2[:], in0=ang[:], scalar1=math.pi / 2),
                             r=["ang"], w=["ang2"])
                        src = ang2
                    sn = "ang2" if which == 0 else "ang"
                    S.op("dve", lambda e, src=src: e.tensor_scalar(out=ki[:], in0=src[:], scalar1=1.0 / TWO_PI,
                                                                   scalar2=None, op0=ALU.mult), r=[sn], w=["ki"])
                    S.op("dve", lambda e: e.tensor_copy(out=kf[:], in_=ki[:]), r=["ki"], w=["kf"])
                    S.op("dve", lambda e, src=src: e.scalar_tensor_tensor(out=rr[:], in0=kf[:], scalar=-C1, in1=src[:],
                                                                          op0=ALU.mult, op1=ALU.add),
                         r=["kf", sn], w=["rr"])
                    S.op("dve", lambda e: e.scalar_tensor_tensor(out=rr[:], in0=kf[:], scalar=-C2, in1=rr[:],
                                                                 op0=ALU.mult, op1=ALU.add), r=["kf", "rr"], w=["rr"])
                    S.op("dve", lambda e: e.tensor_scalar(out=rr[:], in0=rr[:], scalar1=-3.14159, scalar2=3.14159,
                                                          op0=ALU.max, op1=ALU.min), r=["rr"], w=["rr"])
                    if which == 0:
                        S.op("act", lambda e: e.activation(out=tb[0][:], in_=rr[:], func=AF.Sin), r=["rr"], w=["tb0"])
                        S.dma(cos_d[:, sl], tb[0][:], r=["tb0"], w=["cos_d"])
                    else:
                        S.op("act", lambda e: e.activation(out=tb[1][:], in_=rr[:], func=AF.Sin, scale=invf[:, 1:2]),
                             r=["rr", "invf"], w=["tb1"])
                        S.dma(sin_d[:, sl], tb[1][:], r=["tb1"], w=["sin_d"])
        S.barrier()

        with ExitStack() as sp_:
            stg = [sb(f"stg{i}", [128, 2048], stack=sp_) for i in range(2)]
            qg = sb("qg", [128, 3], stack=sp_)
            kvg = sb("kvg", [128, 2], stack=sp_)
            wbdf = sb("wbdf", [128, 2, 128], stack=sp_)
            S.dma(qg[:], qg_d, w=["qg"])
            S.dma(kvg[:], kvg_d, w=["kvg"])
            S.dma(pools[:], pools_d, w=["pools"])
            S.dma(invw[:], invw_d, w=["invw"])
            S.dma(invc0[:], invc0_d[:, :, 0:T1], w=["invc0"])
            nstg = [0]

            def staged(src_ap, width, fn, rk=(), wk=()):
                i = nstg[0] % 2
                nstg[0] += 1
                S.dma(stg[i][:, 0:width], src_ap, w=[f"stg{i}"])
                S.op("dve" if i == 0 else "pool", lambda e: fn(e, stg[i][:, 0:width]), r=[f"stg{i}"] + list(rk), w=list(wk))

            win_v = win_d.rearrange("(k p) c -> p k c", p=128)
            for k in range(8):
                staged(win_v[:, k, :], 1024, lambda e, s, k=k: e.tensor_copy(out=w_in_bf[:, k, :], in_=s), wk=["w_in_bf"])
            wuq_v = wuq_d.rearrange("(k p) c -> p k c", p=128)
            for k in range(3):
                S.dma(stg[0][:, 0:1536], wuq_v[:, k, :], w=["stg0"])
                S.op("dve", lambda e, k=k: e.tensor_scalar(out=w_uq_bf[:, k, :], in0=stg[0][:, 0:1536], scalar1=qg[:, k:k + 1],
                                                           scalar2=QSCALE, op0=ALU.mult, op1=ALU.mult),
                     r=["stg0", "qg"], w=["w_uq_bf"])
            wukv_v = wukv_d.rearrange("(k p) c -> p k c", p=128)
            for k in range(2):
                S.dma(stg[1][:, 0:1536], wukv_v[:, k, :], w=["stg1"])
                S.op("dve", lambda e, k=k: e.tensor_scalar(out=w_ukv_bf[:, k, :], in0=stg[1][:, 0:1536],
                                                           scalar1=kvg[:, k:k + 1], scalar2=None, op0=ALU.mult),
                     r=["stg1", "kvg"], w=["w_ukv_bf"])
            staged(mask_d, 2048, lambda e, s: e.tensor_copy(out=maskb[:].rearrange("p a b -> p (a b)"), in_=s), wk=["maskb"])
            S.op("dve", lambda e: e.memset(wbdf[:], 0.0), w=["wbdf"])
            for g in range(4):
                S.dma(wbdf[(g % 2) * 64:(g % 2) * 64 + 64, g // 2, (g % 2) * 64:(g % 2) * 64 + 64], poolw_d[g],
                      w=["wbdf"])
            S.op("dve", lambda e: e.tensor_copy(out=wbd[:], in_=wbdf[:]), r=["wbdf"], w=["wbd"])

            snap = S.snapshot()
            n = 0
            for k in range(8):
                for ec in range(32):
                    i = n % 2
                    n += 1
                    S.dma(cst[i][:], uT_d[k * 128:(k + 1) * 128, ec * 512:(ec + 1) * 512], w=[f"cst{i}"], q="pool")
                    S.op("pool", lambda e, i=i: e.tensor_copy(out=cbf[i][:], in_=cst[i][:]), r=[f"cst{i}"], w=[f"cbf{i}"])
                    S.dma(ubf_d[:, k, ec * 512:(ec + 1) * 512], cbf[i][:], r=[f"cbf{i}"], w=["ubf_d"], q="pool")
            v_v = v_d.rearrange("(a p) f -> p a f", p=128)
            for a2 in range(128):
                for hf in range(2):
                    i = n % 2
                    n += 1
                    fs = slice(hf * 512, (hf + 1) * 512)
                    S.dma(cst[i][:], v_v[:, a2, fs], w=[f"cst{i}"], q="pool")
                    S.op("pool", lambda e, i=i, fs=fs: e.tensor_tensor(out=cbf[i][:], in0=cst[i][:], in1=g2_bc[:, fs], op=ALU.mult),
                         r=[f"cst{i}", "g2_bc"], w=[f"cbf{i}"])
                    S.dma(vbf_d[:, a2, fs], cbf[i][:], r=[f"cbf{i}"], w=["vbf_d"], q="pool")
            S.barrier_on(snap)

        with ExitStack() as s1:
            xTt = [sb(f"xTt{i}", [128, 8, T1], stack=s1) for i in range(2)]
            hT = [sb(f"hT{i}", [128, 8, T1], BF16, stack=s1) for i in range(2)]
            pbuf = [sb(f"pbuf{i}", [128, 2, T1 + 16], stack=s1) for i in range(2)]
            t2 = sb("t2", [128, 2, T1 + 16], stack=s1)
            t4 = sb("t4", [128, 2, T1 + 16], stack=s1)
            t8 = sb("t8", [128, T1 + 16], stack=s1)
            t16 = sb("t16", [128, T1 + 16], stack=s1)
            tmpf = sb("tmpf", [128, 2, T1], stack=s1)
            mixed = sb("mixed", [128, 2, T1], BF16, stack=s1)
            yT = [sb(f"yT{i}", [128, T1], BF16, stack=s1) for i in range(2)]
            sqb = [sb(f"sqb{i}", [128, T1], BF16, stack=s1) for i in range(2)]
            sqr = sb("sqr", [128, T1], stack=s1)
            rbc = sb("rbc", [128, T1], stack=s1)
            cst_ = [sb(f"cs{i}", [64, 2, T1], stack=s1) for i in range(2)]
            r1 = sb("r1", [64, T1], stack=s1)
            r2 = sb("r2", [64, T1], stack=s1)
            xT_v = xT_d.rearrange("(k p) t -> p k t", p=128)
            S.op("dve", lambda e: e.memset(pbuf[0][:, :, 0:16], 0.0), w=["pbuf0"])
            for tt in range(SEQ // T1):
                i = tt % 2
                ts = slice(tt * T1, (tt + 1) * T1)
                S.dma(xTt[i][:], xT_v[:, :, ts], w=[f"xTt{i}"])
                S.dma(cst_[i][:, 0, :], cos_d[:, ts], w=[f"cs{i}"])
                S.dma(cst_[i][:, 1, :], sin_d[:, ts], w=[f"cs{i}"])
                for k in range(8):
                    if k % 2 == 0:
                        S.op("act", lambda e, k=k: e.activation(out=hT[i][:, k, :], in_=xTt[i][:, k, :], func=AF.Identity,
                                                                scale=sc1p[:, k:k + 1], bias=modT[:, k:k + 1]),
                             r=[f"xTt{i}", "sc1p", "modT"], w=[f"hT{i}"])
                    else:
                        S.op("dve", lambda e, k=k: e.tensor_scalar(out=hT[i][:, k, :], in0=xTt[i][:, k, :],
                                                                   scalar1=sc1p[:, k:k + 1], scalar2=modT[:, k:k + 1],
                                                                   op0=ALU.mult, op1=ALU.add),
                             r=[f"xTt{i}", "sc1p", "modT"], w=[f"hT{i}"])

                def zmm(pb, col0, m, i=i):
                    for k in range(8):
                        S.mm(bank(pb)[0:m, 0:T1], w_in_bf[:, k, col0:col0 + m], hT[i][:, k, :], k == 0, k == 7,
                             r=[f"hT{i}", "w_in_bf"], w=bk(pb))
                pb_ = pbuf[i]
                for oc in range(2):
                    zmm(7, oc * 128, 128)
                    S.op("act", lambda e, oc=oc: e.copy(out=pb_[:, oc, 16:T1 + 16], in_=bank(7)[:, 0:T1]), r=bk(7), w=[f"pbuf{i}"])
                S.op("dve", lambda e: e.tensor_tensor(out=t2[:, :, 1:T1 + 16], in0=pb_[:, :, 1:T1 + 16], in1=pb_[:, :, 0:T1 + 15], op=ALU.add),
                     r=[f"pbuf{i}"], w=["t2"])
                S.op("dve", lambda e: e.tensor_tensor(out=t4[:, :, 3:T1 + 16], in0=t2[:, :, 3:T1 + 16], in1=t2[:, :, 1:T1 + 14], op=ALU.add),
                     r=["t2"], w=["t4"])
                S.op("dve", lambda e: e.tensor_tensor(out=t8[:, 7:T1 + 16], in0=t4[:, 1, 7:T1 + 16], in1=t4[:, 1, 3:T1 + 12], op=ALU.add),
                     r=["t4"], w=["t8"])
                S.op("dve", lambda e: e.tensor_tensor(out=t16[:, 15:T1 + 16], in0=t8[:, 15:T1 + 16], in1=t8[:, 7:T1 + 8], op=ALU.add),
                     r=["t8"], w=["t16"])
                srcs = [(t2, 0, 0), (t4, 0, 64), (None, 1, 0), (None, 1, 64)]
                for (tsrc, ch, p0) in srcs:
                    if tsrc is None:
                        win_ap = (t8 if p0 == 0 else t16)[p0:p0 + 64, 16:T1 + 16]
                    else:
                        win_ap = tsrc[p0:p0 + 64, ch, 16:T1 + 16]
                    if tt == 0:
                        S.op("dve", lambda e, win_ap=win_ap, ch=ch, p0=p0: e.tensor_tensor(
                            out=tmpf[p0:p0 + 64, ch, :], in0=win_ap, in1=invc0[p0:p0 + 64, ch, :], op=ALU.mult),
                            r=["t2", "t4", "t8", "t16", "invc0"], w=["tmpf"])
                        S.op("dve", lambda e, ch=ch, p0=p0: e.tensor_tensor(
                            out=mixed[p0:p0 + 64, ch, :], in0=tmpf[p0:p0 + 64, ch, :], in1=pb_[p0:p0 + 64, ch, 16:T1 + 16],
                            op=ALU.subtract), r=["tmpf", f"pbuf{i}"], w=["mixed"])
                    else:
                        S.op("dve", lambda e, win_ap=win_ap, ch=ch, p0=p0: e.scalar_tensor_tensor(
                            out=mixed[p0:p0 + 64, ch, :], in0=win_ap, scalar=invw[p0:p0 + 64, ch:ch + 1],
                            in1=pb_[p0:p0 + 64, ch, 16:T1 + 16], op0=ALU.mult, op1=ALU.subtract),
                            r=["t2", "t4", "t8", "t16", "invw", f"pbuf{i}"], w=["mixed"])
                S.op("dve", lambda e: e.tensor_copy(out=pbuf[1 - i][:, :, 0:16], in_=pb_[:, :, T1:T1 + 16]),
                     r=[f"pbuf{i}"], w=[f"pbuf{1 - i}"])
                for ch in range(2):
                    S.mm(bank(7)[:, 0:T1], wbd[:, ch, :], mixed[:, ch, :], True, True, r=["wbd", "mixed"], w=bk(7))
                    S.op("act", lambda e, ch=ch: e.activation(out=yT[ch][:], in_=bank(7)[:, 0:T1], func=AF.Identity,
                                                              scale=pools[:, ch:ch + 1]), r=bk(7) + ["pools"], w=[f"yT{ch}"])
                    S.dma(apT_d[ch * 128:(ch + 1) * 128, ts], yT[ch][:], r=[f"yT{ch}"], w=["apT_d"])
                for (name, dst, nchunk, col0, pb0, pss, nfeat) in (("q", cqn, 3, 256, 0, 3, 384.0),
                                                                    ("kv", ckvn, 2, 640, 4, 6, 256.0)):
                    for j in range(nchunk):
                        zmm(pb0 + j, col0 + j * 128, 128)
                        S.op("act", lambda e, j=j, pb0=pb0: e.activation(out=sqb[j % 2][:], in_=bank(pb0 + j)[:, 0:T1], func=AF.Square),
                             r=bk(pb0 + j), w=[f"sqb{j % 2}"])
                        S.mm(bank(pss)[:, 0:T1], ones_bf[:], sqb[j % 2][:], j == 0, j == nchunk - 1, r=["ones_bf", f"sqb{j % 2}"],
                             w=bk(pss))
                    S.op("dve", lambda e, pss=pss, nfeat=nfeat: e.tensor_scalar(out=sqr[:], in0=bank(pss)[:, 0:T1], scalar1=1.0 / nfeat,
                                                                               scalar2=RMS_EPS, op0=ALU.mult, op1=ALU.add),
                         r=bk(pss), w=["sqr"])
                    S.op("act", lambda e: e.activation(out=sqr[:], in_=sqr[:], func=AF.Sqrt), r=["sqr"], w=["sqr"])
                    S.op("dve", lambda e: e.reciprocal(out=rbc[:], in_=sqr[:]), r=["sqr"], w=["rbc"])
                    for j in range(nchunk):
                        S.op("dve", lambda e, j=j, dst=dst, pb0=pb0: e.tensor_tensor(out=dst[:, j, ts], in0=bank(pb0 + j)[:, 0:T1],
                                                                                     in1=rbc[:], op=ALU.mult),
                             r=bk(pb0 + j) + ["rbc"], w=[name + "n"])
                zmm(0, 896, 64)
                zmm(1, 960, 64)
                S.op("dve", lambda e: e.tensor_tensor(out=r1[:], in0=bank(0)[0:64, 0:T1], in1=cst_[i][:, 0, :], op=ALU.mult),
                     r=bk(0) + [f"cs{i}"], w=["r1"])
                S.op("dve", lambda e: e.tensor_tensor(out=r2[:], in0=bank(1)[0:64, 0:T1], in1=cst_[i][:, 1, :], op=ALU.mult),
                     r=bk(1) + [f"cs{i}"], w=["r2"])
                S.op("dve", lambda e: e.tensor_tensor(out=krT[:, ts], in0=r1[:], in1=r2[:], op=ALU.add),
                     r=["r1", "r2"], w=["krT"])
            if debug:
                S.dma(dbg["lat"][:, 0:3, :], cqn[:], r=["qn"])
                S.dma(dbg["lat"][:, 3:5, :], ckvn[:], r=["kvn"])
                S.dma(dbg["lat"][0:64, 5, :], krT[:], r=["krT"])
        S.barrier_on(S.snapshot(skip=("pool",)))
        stw1.close()

        with ExitStack() as s2:
            qnT2 = [sb(f"qnT{i}", [128, 512], BF16, stack=s2) for i in range(2)]
            qrT2 = [sb(f"qrT{i}", [64, 512], BF16, stack=s2) for i in range(2)]
            knT = sb("knT", [128, SEQ], BF16, stack=s2)
            V = sb("V", [128, 64, 128], BF16, stack=s2)
            cs2 = [sb(f"cs2{i}", [64, 2, TG], stack=s2) for i in range(2)]
            r1 = sb("r1b", [64, TG], stack=s2)
            r2 = sb("r2b", [64, TG], stack=s2)
            pT = [sb(f"pT{i}", [128, 512], BF16, stack=s2) for i in range(3)]
            acc = [sb(f"acc{i}", [128, 512], stack=s2) for i in range(2)]
            rl = sb("rl", [128, 512], stack=s2)
            att = [sb(f"att{i}", [128, 512], BF16, stack=s2) for i in range(2)]
            for h in range(NH):
                wq0 = h * 256
                for tt in range(SEQ // TG):
                    ts = slice(tt * TG, (tt + 1) * TG)
                    for k in range(2):
                        S.mm(bank(6)[:, 0:TG], w_ukv_bf[:, k, wq0:wq0 + 128], ckvn[:, k, ts], k == 0, k == 1, r=["w_ukv_bf", "kvn"], w=bk(6))
                    S.op("act", lambda e, ts=ts: e.copy(out=knT[:, ts], in_=bank(6)[:, 0:TG]), r=bk(6), w=["knT"])
                    for sub in range(TG // 128):
                        tsub = slice(tt * TG + sub * 128, tt * TG + sub * 128 + 128)
                        for k in range(2):
                            S.mm(bank(7)[:, sub * 128:(sub + 1) * 128], ckvn[:, k, tsub], w_ukv_bf[:, k, wq0 + 128:wq0 + 256],
                                 k == 0, k == 1, r=["w_ukv_bf", "kvn"], w=bk(7))
                    S.op("dve", lambda e, tt=tt: e.tensor_copy(out=V[:, tt * (TG // 128):(tt + 1) * (TG // 128), :],
                                                              in_=bank(7)[:, 0:TG].rearrange("p (a b) -> p a b", b=128)),
                         r=bk(7), w=["V"])
                def qgen(j):
                    qnT = qnT2[j % 2]
                    qrT = qrT2[j % 2]
                    qnk, qrk = f"qnT{j % 2}", f"qrT{j % 2}"
                    for sub in range(512 // TG):
                        i = (j * (512 // TG) + sub) % 2
                        ts = slice(j * 512 + sub * TG, j * 512 + (sub + 1) * TG)
                        so = slice(sub * TG, (sub + 1) * TG)
                        S.dma(cs2[i][:, 0, :], cos_d[:, ts], w=[f"cs2{i}"])
                        S.dma(cs2[i][:, 1, :], sin_d[:, ts], w=[f"cs2{i}"])
                        for k in range(3):
                            S.mm(bank(6)[:, 0:TG], w_uq_bf[:, k, wq0:wq0 + 128], cqn[:, k, ts], k == 0, k == 2, r=["w_uq_bf", "qn"], w=bk(6))
                        S.op("act", lambda e, so=so, qnT=qnT: e.copy(out=qnT[:, so], in_=bank(6)[:, 0:TG]), r=bk(6), w=[qnk])
                        for (pb, c0) in ((6, 128), (7, 192)):
                            for k in range(3):
                                S.mm(bank(pb)[0:64, 0:TG], w_uq_bf[:, k, wq0 + c0:wq0 + c0 + 64], cqn[:, k, ts], k == 0, k == 2,
                                     r=["w_uq_bf", "qn"], w=bk(pb))
                        S.op("dve", lambda e, i=i: e.tensor_tensor(out=r1[:], in0=bank(6)[0:64, 0:TG], in1=cs2[i][:, 0, :], op=ALU.mult),
                             r=bk(6) + [f"cs2{i}"], w=["r1"])
                        S.op("dve", lambda e, i=i: e.tensor_tensor(out=r2[:], in0=bank(7)[0:64, 0:TG], in1=cs2[i][:, 1, :], op=ALU.mult),
                             r=bk(7) + [f"cs2{i}"], w=["r2"])
                        S.op("dve", lambda e, so=so, qrT=qrT: e.tensor_tensor(out=qrT[:, so], in0=r1[:], in1=r2[:], op=ALU.add),
                             r=["r1", "r2"], w=[qrk])

                tiles = [(j, kt) for j in range(16) for kt in range(4 * j + 4)]

                def emit_S(n):
                    j, kt = tiles[n]
                    sbk = n % 3
                    ks = slice(kt * 128, (kt + 1) * 128)
                    S.mm(bank(sbk), knT[:, ks], qnT2[j % 2][:], True, False, r=["knT", f"qnT{j % 2}"], w=bk(sbk))
                    S.mm(bank(sbk), krT[:, ks], qrT2[j % 2][:], False, True, r=["krT", f"qrT{j % 2}"], w=bk(sbk))

                def epilogue(j):
                    ob = 3 + (j % 2)
                    ac = acc[j % 2]
                    S.mm(bank(5), ones_f[:], ac[:], True, True, r=["ones_f", f"acc{j % 2}"], w=bk(5))
                    S.op("dve", lambda e: e.reciprocal(out=rl[:], in_=bank(5)), r=bk(5), w=["rl"])
                    S.op("dve", lambda e, ob=ob, j=j: e.tensor_tensor(out=att[j % 2][:], in0=bank(ob), in1=rl[:], op=ALU.mult),
                         r=bk(ob) + ["rl"], w=[f"att{j % 2}"])
                    S.dma(apT_d[256 + h * 128:256 + (h + 1) * 128, j * 512:(j + 1) * 512], att[j % 2][:], r=[f"att{j % 2}"], w=["apT_d"])

                qgen(0)
                qgen(1)
                emit_S(0)
                emit_S(1)
                pend = None
                for n, (j, kt) in enumerate(tiles):
                    nk = 4 * j + 4
                    sbk = n % 3
                    ob = 3 + (j % 2)
                    ac = acc[j % 2]
                    S.op("act", lambda e, sbk=sbk: e.activation(out=pT[sbk][:], in_=bank(sbk), func=AF.Exp),
                         r=bk(sbk), w=[f"pT{sbk}"])
                    if kt >= 4 * j:
                        S.op("dve", lambda e, sbk=sbk, m=kt - 4 * j: e.tensor_tensor(
                            out=pT[sbk][:], in0=pT[sbk][:], in1=maskb[:, m, :], op=ALU.mult),
                            r=[f"pT{sbk}", "maskb"], w=[f"pT{sbk}"])
                    S.mm(bank(ob), V[:, kt, :], pT[sbk][:], kt == 0, kt == nk - 1, r=["V", f"pT{sbk}"], w=bk(ob))
                    if n + 2 < len(tiles):
                        emit_S(n + 2)
                    if kt == 0:
                        S.op("dve", lambda e, sbk=sbk, ac=ac: e.tensor_copy(out=ac[:], in_=pT[sbk][:]),
                             r=[f"pT{sbk}"], w=[f"acc{j % 2}"])
                    else:
                        S.op("dve", lambda e, sbk=sbk, ac=ac: e.tensor_tensor(out=ac[:], in0=ac[:], in1=pT[sbk][:], op=ALU.add),
                             r=[f"pT{sbk}", f"acc{j % 2}"], w=[f"acc{j % 2}"])
                    if pend is not None and kt == 1:
                        epilogue(pend)
                        pend = None
                    if kt == 2 and j + 1 < 16 and j >= 1:
                        qgen(j + 1)
                    if kt == nk - 1:
                        pend = j
                epilogue(pend)
        S.barrier()
        st12.close()

        with ExitStack() as s3:
            wq_bf = sb("wq_bf", [128, 8, 2048], BF16, stack=s3)
            w_out_bf = sb("w_out_bf", [128, 8, D], BF16, stack=s3)
            keysT = sb("keysT", [128, 16, 128], BF16, stack=s3)
            lnb = sb("lnb", [128, 4, D], stack=s3)
            iota_n = sb("iota_n", [128, 128], I32, stack=s3)
            iota_j = sb("iota_j", [128, 256], I32, stack=s3)
            iota16 = sb("iota16", [128, 16], stack=s3)
            iota128 = sb("iota128", [128, 128], stack=s3)
            S.dma(iota_n[:], iotan_d, w=["iota_n"])
            S.dma(iota_j[:], iotaj_d, w=["iota_j"])
            S.dma(iota16[:], iota16_d, w=["iota16"])
            S.dma(iota128[:], iota128_d, w=["iota128"])
            for r_ in range(4):
                S.dma(lnb[:, r_, :], ln_rows_d[r_:r_ + 1, :].to_broadcast([128, D]), w=["lnb"])
            with ExitStack() as sp3:
                stg = [sb(f"stg3{i}", [128, 2048], stack=sp3) for i in range(2)]
                wpq_v = wpq_d.rearrange("(k p) c -> p k c", p=128)
                for k in range(8):
                    S.dma(stg[k % 2][:], wpq_v[:, k, :], w=[f"stg3{k % 2}"])
                    S.op("dve" if k % 2 == 0 else "pool", lambda e, k=k: e.tensor_copy(out=wq_bf[:, k, :], in_=stg[k % 2][:]),
                         r=[f"stg3{k % 2}"], w=["wq_bf"])
                wout_v = wout_d.rearrange("(k p) c -> p k c", p=128)
                for k in range(8):
                    S.dma(stg[k % 2][:, 0:1024], wout_v[:, k, :], w=[f"stg3{k % 2}"])
                    S.op("dve", lambda e, k=k: e.tensor_tensor(out=w_out_bf[:, k, :], in0=stg[k % 2][:, 0:1024], in1=g1_bc[:],
                                                                           op=ALU.mult), r=[f"stg3{k % 2}", "g1_bc"], w=["w_out_bf"])
                S.dma(stg[0][:], keysT_d, w=["stg30"])
                S.op("dve", lambda e: e.tensor_copy(out=keysT[:].rearrange("p a b -> p (a b)"), in_=stg[0][:]),
                     r=["stg30"], w=["keysT"])
                S.barrier()

            apT4 = [sb(f"apT4{i}", [128, 8, 128], BF16, stack=s3) for i in range(2)]
            xt = [sb(f"xt{i}", [128, D], stack=s3) for i in range(1)]
            y = sb("y", [128, D], stack=s3)
            n1 = y
            x1s = [sb(f"x1{i}", [128, D], stack=s3) for i in range(2)]
            ot = [sb(f"ot{i}", [128, D], stack=s3) for i in range(1)]
            stats = sb("stats", [128, 2, 6], stack=s3)
            mv = sb("mv", [128, 2], stack=s3)
            rstd = sb("rstd", [128, 1], stack=s3)
            h2T = sb("h2T", [128, 8, 256], BF16, stack=s3)
            qT = sb("qT", [128, 16, 128], BF16, stack=s3)
            sc = sb("sc", [128, 16, 128], stack=s3)
            scr = sb("scrx", [128, 16, 128], stack=s3)
            s1t = sb("s1t", [128, 16, 16], stack=s3)
            idx1i = sb("idx1i", [128, 16, 16], I32, stack=s3)
            idx1f = sb("idx1f", [128, 16, 16], stack=s3)
            cand = sb("cand", [128, 8, 256], stack=s3)
            stop_ = sb("stop", [128, 8, 16], stack=s3)
            ji = sb("ji", [128, 8, 16], I32, stack=s3)
            ai = sb("ai", [128, 8, 16], I32, stack=s3)
            bi = sb("bi", [128, 8, 16], I32, stack=s3)
            af = sb("af", [128, 8, 16], stack=s3)
            bf_ = sb("bf", [128, 8, 16], stack=s3)
            sel = sb("sel", [128, 3, 128], stack=s3)
            ex = sb("ex", [128, 8, 16], stack=s3)
            ssum = sb("ssum", [128, 8], stack=s3)
            selT = sb("selT", [128, 3, 256], stack=s3)
            Bh = [sb(f"Bh{i}", [128, 8, 128], BF16, stack=s3) for i in range(2)]
            Ae = [sb(f"Ae{i}", [128, 8, 64], BF16, stack=s3) for i in range(2)]
            iota_bf = sb("iota_bf", [128, 128], BF16, stack=s3)
            S.op("dve", lambda e: e.tensor_copy(out=iota_bf[:], in_=iota128[:]), r=["iota128"], w=["iota_bf"])
            G = sb("G", [128, 64, 256], BF16, stack=s3)
            ublk = [sb(f"ublk{i}", [128, 8, 256], BF16, stack=s3) for i in range(2)]
            vblk = [sb(f"vblk{i}", [128, 2, D], BF16, stack=s3) for i in range(2)]
            gel = [sb(f"gel{i}", [128, 512], BF16, stack=s3) for i in range(2)]
            W = [sb(f"W{i}", [128, 512], BF16, stack=s3) for i in range(2)]
            eq = cand[:].rearrange("p h (a b) -> p h a b", a=16)
            apT_v = apT_d.rearrange("(k p) t -> p k t", p=128)
            out_toks = []

            def layer_norm(src, dst_n, srck, dstk):
                for c_ in range(2):
                    S.op("dve", lambda e, c_=c_: e.bn_stats(out=stats[:, c_, :], in_=src[:, c_ * 512:(c_ + 1) * 512]),
                         r=[srck], w=["stats"])
                S.op("dve", lambda e: e.bn_aggr(out=mv[:], in_=stats[:].rearrange("p a b -> p (a b)")), r=["stats"], w=["mv"])
                S.op("dve", lambda e: e.tensor_scalar_add(out=rstd[:], in0=mv[:, 1:2], scalar1=LN_EPS), r=["mv"], w=["rstd"])
                S.op("act", lambda e: e.activation(out=rstd[:], in_=rstd[:], func=AF.Sqrt), r=["rstd"], w=["rstd"])
                S.op("dve", lambda e: e.reciprocal(out=rstd[:], in_=rstd[:]), r=["rstd"], w=["rstd"])
                S.op("dve", lambda e: e.tensor_scalar(out=dst_n[:], in0=src[:], scalar1=mv[:, 0:1], scalar2=rstd[:, 0:1],
                                                      op0=ALU.subtract, op1=ALU.mult), r=[srck, "mv", "rstd"], w=[dstk])

            NST = nst
            assert NST % 2 == 0
            for pr in range(NST // 2):
                for s in (2 * pr, 2 * pr + 1):
                    t0 = s * 128
                    i = 0
                    sub = s % 2
                    x1 = x1s[sub]
                    x1k = f"x1{sub}"
                    hs = slice(sub * 128, (sub + 1) * 128)
                    S.dma(apT4[sub][:], apT_v[:, :, t0:t0 + 128], w=[f"apT4{sub}"])
                    ap4 = apT4[sub]
                    ap4k = f"apT4{sub}"
                    toff = 0
                    S.dma(xt[i][:], x_d[t0:t0 + 128, :], w=[f"xt{i}"])
                    for half in range(2):
                        for k in range(8):
                            S.mm(bank(half), ap4[:, k, toff:toff + 128], w_out_bf[:, k, half * 512:(half + 1) * 512], k == 0, k == 7,
                                 r=[ap4k, "w_out_bf"], w=bk(half))
                    S.op("dve", lambda e, i=i: e.scalar_tensor_tensor(out=y[:], in0=xt[i][:], scalar=ALPHA, in1=bank(0, 2),
                                                                      op0=ALU.mult, op1=ALU.add), r=[f"xt{i}"] + bk(0, 2), w=["y"])
                    layer_norm(y, y, "y", "y")
                    S.op("pool", lambda e, x1=x1: e.tensor_tensor(out=x1[:], in0=n1[:], in1=lnb[:, 0, :], op=ALU.mult), r=["y", "lnb"], w=[x1k])
                    S.op("pool", lambda e, x1=x1: e.tensor_tensor(out=x1[:], in0=x1[:], in1=lnb[:, 1, :], op=ALU.add), r=[x1k, "lnb"], w=[x1k])
                    for k in range(8):
                        S.op("pe", lambda e, k=k: e.transpose(bank(2, 2)[:, k * 128:(k + 1) * 128], n1[:, k * 128:(k + 1) * 128], ident[:]),
                             r=["y", "ident"], w=bk(2 + k // 4))
                    for k in range(8):
                        S.op("act", lambda e, k=k, hs=hs: e.activation(out=h2T[:, k, hs], in_=bank(2, 2)[:, k * 128:(k + 1) * 128],
                                                                func=AF.Identity, scale=A2[:, k:k + 1], bias=B2[:, k:k + 1]),
                             r=bk(2 + k // 4) + ["A2", "B2"], w=["h2T"])
                    for oc in range(16):
                        pb = 4 + oc // 4
                        for k in range(8):
                            S.mm(bank(pb)[:, (oc % 4) * 128:(oc % 4 + 1) * 128], wq_bf[:, k, oc * 128:(oc + 1) * 128], h2T[:, k, hs],
                                 k == 0, k == 7, r=["wq_bf", "h2T"], w=bk(pb))
                    for b4 in range(4):
                        S.op("act" if b4 % 2 == 0 else "dve",
                             (lambda e, b4=b4: e.copy(out=qT[:, b4 * 4:b4 * 4 + 4, :], in_=bank(4 + b4).rearrange("p (a b) -> p a b", a=4)))
                             if b4 % 2 == 0 else
                             (lambda e, b4=b4: e.tensor_copy(out=qT[:, b4 * 4:b4 * 4 + 4, :], in_=bank(4 + b4).rearrange("p (a b) -> p a b", a=4))),
                             r=bk(4 + b4), w=["qT"])
                    for hp in range(16):
                        pb = 4 + hp // 4
                        S.mm(bank(pb)[:, (hp % 4) * 128:(hp % 4 + 1) * 128], qT[:, hp, :], keysT[:, hp, :], True, True,
                             r=["qT", "keysT"], w=bk(pb))
                    for b4 in range(4):
                        S.op("dve", lambda e, b4=b4: e.tensor_single_scalar(
                            out=sc[:, b4 * 4:b4 * 4 + 4, :].bitcast(I32),
                            in_=bank(4 + b4).rearrange("p (a b) -> p a b", a=4).bitcast(I32), scalar=-128, op=ALU.bitwise_and),
                            r=bk(4 + b4), w=["sc"])
                    S.op("dve", lambda e: e.tensor_tensor(out=sc[:].bitcast(I32), in0=sc[:].bitcast(I32),
                                                          in1=iota_n[:].unsqueeze(1).to_broadcast([128, 16, 128]), op=ALU.bitwise_or),
                         r=["sc", "iota_n"], w=["sc"])
                    for hp in range(16):
                        S.op("dve", lambda e, hp=hp: e.max(out=s1t[:, hp, 0:8], in_=sc[:, hp, :]), r=["sc"], w=["s1t"])
                        S.op("dve", lambda e, hp=hp: e.match_replace(out=scr[:, hp, :], in_to_replace=s1t[:, hp, 0:8],
                                                                     in_values=sc[:, hp, :], imm_value=NEG), r=["sc", "s1t"], w=["scr"])
                        S.op("dve", lambda e, hp=hp: e.max(out=s1t[:, hp, 8:16], in_=scr[:, hp, :]), r=["scr"], w=["s1t"])
                    S.op("dve", lambda e: e.tensor_single_scalar(out=idx1i[:], in_=s1t[:].bitcast(I32), scalar=127, op=ALU.bitwise_and),
                         r=["s1t"], w=["idx1i"])
                    S.op("dve", lambda e: e.tensor_copy(out=idx1f[:], in_=idx1i[:]), r=["idx1i"], w=["idx1f"])
                    s1v = s1t[:].rearrange("p (h two) a -> p h two a", two=2)
                    candv = cand[:].rearrange("p h (a b) -> p h a b", a=16)
                    S.op("dve", lambda e: e.tensor_tensor(out=candv, in0=s1v[:, :, 0, :].unsqueeze(3).to_broadcast([128, 8, 16, 16]),
                                                          in1=s1v[:, :, 1, :].unsqueeze(2).to_broadcast([128, 8, 16, 16]), op=ALU.add),
                         r=["s1t"], w=["cand"])
                    S.op("dve", lambda e: e.tensor_single_scalar(out=cand[:].bitcast(I32), in_=cand[:].bitcast(I32), scalar=-256,
                                                                 op=ALU.bitwise_and), r=["cand"], w=["cand"])
                    S.op("dve", lambda e: e.tensor_tensor(out=cand[:].bitcast(I32), in0=cand[:].bitcast(I32),
                                                          in1=iota_j[:].unsqueeze(1).to_broadcast([128, 8, 256]), op=ALU.bitwise_or),
                         r=["cand", "iota_j"], w=["cand"])
                    scr2 = scr[:].rearrange("p (h two) n -> p h (two n)", two=2)
                    for h in range(8):
                        S.op("dve", lambda e, h=h: e.max(out=stop_[:, h, 0:8], in_=cand[:, h, :]), r=["cand"], w=["stop"])
                        S.op("dve", lambda e, h=h: e.match_replace(out=scr2[:, h, :], in_to_replace=stop_[:, h, 0:8],
                                                                   in_values=cand[:, h, :], imm_value=NEG), r=["cand", "stop"], w=["scr"])
                        S.op("dve", lambda e, h=h: e.max(out=stop_[:, h, 8:16], in_=scr2[:, h, :]), r=["scr"], w=["stop"])
                    S.op("dve", lambda e: e.tensor_single_scalar(out=ji[:], in_=stop_[:].bitcast(I32), scalar=255, op=ALU.bitwise_and),
                         r=["stop"], w=["ji"])
                    S.op("dve", lambda e: e.tensor_single_scalar(out=ai[:], in_=ji[:], scalar=4, op=ALU.logical_shift_right),
                         r=["ji"], w=["ai"])
                    S.op("dve", lambda e: e.tensor_single_scalar(out=bi[:], in_=ji[:], scalar=15, op=ALU.bitwise_and),
                         r=["ji"], w=["bi"])
                    S.op("dve", lambda e: e.tensor_copy(out=af[:], in_=ai[:]), r=["ai"], w=["af"])
                    S.op("dve", lambda e: e.tensor_copy(out=bf_[:], in_=bi[:]), r=["bi"], w=["bf"])
                    idxv = idx1f[:].rearrange("p (h two) a -> p h two a", two=2)
                    for which, (srcf, srck) in enumerate(((af, "af"), (bf_, "bf"))):
                        S.op("dve", lambda e, srcf=srcf: e.tensor_tensor(
                            out=eq, in0=srcf[:].unsqueeze(3).to_broadcast([128, 8, 16, 16]),
                            in1=iota16[:].unsqueeze(1).unsqueeze(1).to_broadcast([128, 8, 16, 16]), op=ALU.is_equal),
                            r=[srck, "iota16"], w=["cand"])
                        S.op("dve", lambda e, which=which: e.tensor_tensor(
                            out=eq, in0=eq, in1=idxv[:, :, which, :].unsqueeze(2).to_broadcast([128, 8, 16, 16]), op=ALU.mult),
                            r=["cand", "idx1f"], w=["cand"])
                        S.op("dve", lambda e, which=which: e.tensor_reduce(
                            out=sel[:, which, :], in_=cand[:].rearrange("p h (k a) -> p (h k) a", a=16), axis=AX.X, op=ALU.add),
                            r=["cand"], w=["sel"])
                    S.op("dve", lambda e: e.tensor_tensor(out=ex[:], in0=stop_[:], in1=stop_[:, :, 0:1].to_broadcast([128, 8, 16]),
                                                          op=ALU.subtract), r=["stop"], w=["ex"])
                    S.op("act", lambda e: e.activation(out=ex[:], in_=ex[:], func=AF.Exp), r=["ex"], w=["ex"])
                    S.op("dve", lambda e: e.tensor_reduce(out=ssum[:], in_=ex[:], axis=AX.X, op=ALU.add), r=["ex"], w=["ssum"])
                    S.op("dve", lambda e: e.reciprocal(out=ssum[:], in_=ssum[:]), r=["ssum"], w=["ssum"])
                    S.op("dve", lambda e: e.tensor_tensor(out=sel[:, 2, :].rearrange("p (h k) -> p h k", h=8), in0=ex[:],
                                                          in1=ssum[:].unsqueeze(2).to_broadcast([128, 8, 16]), op=ALU.mult),
                         r=["ex", "ssum"], w=["sel"])
                    if debug and s == 0:
                        S.dma(dbg["sel"], sel[:], r=["sel"])
                        S.dma(dbg["x1"], x1[:], r=[x1k])
                    for w_ in range(3):
                        S.op("pe", lambda e, w_=w_: e.transpose(bank(2)[:, w_ * 128:(w_ + 1) * 128], sel[:, w_, :], ident[:]),
                             r=["sel", "ident"], w=bk(2))
                    S.op("dve", lambda e, hs=hs: e.tensor_copy(out=selT[:, :, hs], in_=bank(2)[:, 0:384].rearrange("p (a b) -> p a b", a=3)), r=bk(2), w=["selT"])
                def onehot_G(half):
                    ng = 0
                    for c0 in range(0, 256, 8):
                        ob_ = (c0 // 8) % 2
                        for tt_ in range(8):
                            t_ = c0 + tt_
                            S.op("dve", lambda e, t_=t_, tt_=tt_, ob_=ob_: e.tensor_scalar(
                                out=Bh[ob_][:, tt_, :], in0=iota_bf[:], scalar1=selT[:, 1, t_:t_ + 1], scalar2=None, op0=ALU.is_equal),
                                r=["selT", "iota_bf"], w=[f"Bh{ob_}"])
                            S.op("dve", lambda e, t_=t_, tt_=tt_, ob_=ob_: e.tensor_scalar(
                                out=Ae[ob_][:, tt_, :], in0=iota_bf[:, half * 64:(half + 1) * 64], scalar1=selT[:, 0, t_:t_ + 1],
                                scalar2=selT[:, 2, t_:t_ + 1], op0=ALU.is_equal, op1=ALU.mult), r=["selT", "iota_bf"], w=[f"Ae{ob_}"])
                        pb = 6 + (ng % 2)
                        ng += 1
                        for tt_ in range(8):
                            S.mm(bank(pb)[:, tt_ * 64:(tt_ + 1) * 64], Bh[ob_][:, tt_, :], Ae[ob_][:, tt_, :], True, True,
                                 r=[f"Bh{ob_}", f"Ae{ob_}"], w=bk(pb))
                        S.op("act", lambda e, pb=pb, c0=c0: e.copy(out=G[:, :, c0:c0 + 8],
                                                                   in_=bank(pb).rearrange("p (t i) -> p i t", t=8)),
                             r=bk(pb), w=["G"])

                def dense(half):
                    for g in range(32):
                        gg = half * 32 + g
                        bi_ = gg % 2
                        S.dma(ublk[bi_][:], ubf_d[:, :, gg * 256:(gg + 1) * 256], w=[f"ublk{bi_}"])
                        S.dma(vblk[bi_][:], vbf_d[:, gg * 2:gg * 2 + 2, :], w=[f"vblk{bi_}"])
                        pst = 4 + bi_
                        for i4 in range(2):
                            for k in range(8):
                                S.mm(bank(pst)[:, i4 * 256:(i4 + 1) * 256], ublk[bi_][:, k, i4 * 128:(i4 + 1) * 128], h2T[:, k, :],
                                     k == 0, k == 7, r=[f"ublk{bi_}", "h2T"], w=bk(pst))
                        S.op("act", lambda e, bi_=bi_, pst=pst: e.activation(out=gel[bi_][:], in_=bank(pst), func=AF.Gelu),
                             r=bk(pst), w=[f"gel{bi_}"])
                        S.op("dve", lambda e, bi_=bi_, g=g: e.tensor_tensor(
                            out=W[bi_][:], in0=gel[bi_][:], in1=G[:, g * 2:g * 2 + 2, :].rearrange("p a t -> p (a t)"), op=ALU.mult),
                            r=[f"gel{bi_}", "G"], w=[f"W{bi_}"])
                        for i4 in range(2):
                            for sub in range(2):
                                for hf in range(2):
                                    S.mm(bank(sub * 2 + hf), W[bi_][:, i4 * 256 + sub * 128:i4 * 256 + (sub + 1) * 128],
                                         vblk[bi_][:, i4, hf * 512:(hf + 1) * 512],
                                         half == 0 and g == 0 and i4 == 0, half == 1 and g == 31 and i4 == 1,
                                         r=[f"W{bi_}", f"vblk{bi_}"], w=bk(sub * 2 + hf))

                for half in range(2):
                    onehot_G(half)
                    dense(half)
                if debug and pr == 0:
                    S.op("dve", lambda e: e.tensor_copy(out=y[:], in_=bank(0, 2)), r=bk(0, 2), w=["y"])
                    S.dma(dbg["ffn"], y[:], r=["y"])
                for s in (2 * pr, 2 * pr + 1):
                    t0 = s * 128
                    i = 0
                    sub = s % 2
                    x1 = x1s[sub]
                    x1k = f"x1{sub}"
                    S.op("dve", lambda e, x1=x1, sub=sub: e.scalar_tensor_tensor(out=y[:], in0=x1[:], scalar=ALPHA, in1=bank(sub * 2, 2),
                                                                 op0=ALU.mult, op1=ALU.add), r=[x1k] + bk(sub * 2, 2), w=["y"])
                    layer_norm(y, y, "y", "y")
                    S.op("pool", lambda e, i=i: e.tensor_tensor(out=ot[i][:], in0=n1[:], in1=lnb[:, 2, :], op=ALU.mult),
                         r=["y", "lnb"], w=[f"ot{i}"])
                    S.op("pool", lambda e, i=i: e.tensor_tensor(out=ot[i][:], in0=ot[i][:], in1=lnb[:, 3, :], op=ALU.add),
                         r=[f"ot{i}", "lnb"], w=[f"ot{i}"])
                    out_toks.append(S.dma(out_d[t0:t0 + 128, :], ot[i][:], r=[f"ot{i}"], w=["out_d"]))
            S.barrier()
        stg1.close()
        print("bass ops:", S.nops, "sems:", S.nsem)
    return nc


def _host_inputs(inp, b):
    f32 = np.float32
    x = np.asarray(inp["x"][b], f32)
    fm = lambda v, k: np.ascontiguousarray(np.asarray(v, f32).reshape(k, 128).T)
    w_in = np.asarray(inp["w_in"][0], f32)
    w_in_ext = np.concatenate([w_in, w_in[:, 928:960], w_in[:, 896:928]], axis=1)
    wuq = np.asarray(inp["w_uq"][0], f32).reshape(384, NH, 192)
    wuq_ext = np.concatenate([wuq, wuq[:, :, 160:192], wuq[:, :, 128:160]], axis=2).reshape(384, NH * 256)
    inv_freq = (10000.0 ** (-np.arange(0, 64, 2, dtype=np.float32) / 64)).astype(f32)
    invf = np.zeros((64, 2), f32)
    invf[:, 0] = np.concatenate([inv_freq, inv_freq])
    invf[:, 1] = np.concatenate([-np.ones(32, f32), np.ones(32, f32)])
    wins = np.array([2, 4, 8, 16], f32)
    invw = np.zeros((128, 2), f32)
    invc0 = np.zeros((128, 2, 512), f32)
    t = np.arange(512, dtype=f32)
    for g in range(4):
        rows = slice((g % 2) * 64, (g % 2) * 64 + 64)
        invw[rows, g // 2] = 1.0 / wins[g]
        invc0[rows, g // 2, :] = 1.0 / np.minimum(t + 1, wins[g])
    kp = np.arange(128)[:, None]
    qf = np.arange(512)[None, :]
    mask = np.stack([(qf >= i * 128 + kp).astype(f32) for i in range(4)], axis=1).reshape(128, 2048)
    b_ada = np.asarray(inp["b_ada"][0], f32)
    ln_rows = np.stack([inp["ln1_g"][0], inp["ln1_b"][0], inp["ln2_g"][0], inp["ln2_b"][0]]).astype(f32)
    keysT = np.ascontiguousarray(np.asarray(inp["peer_keys"][0], f32).reshape(16, 128, 128).transpose(2, 0, 1)).reshape(128, 2048)
    return {
        "xT": np.ascontiguousarray(x.T), "x": x, "c": fm(inp["c"][b], 8),
        "pos": np.asarray(inp["positions"][b], np.int32).reshape(1, SEQ), "invf": invf,
        "w_ada": np.asarray(inp["w_ada"][0], f32), "b_ada_fm": fm(b_ada, 48), "b_ada_row": b_ada.reshape(1, -1),
        "w_in_ext": np.ascontiguousarray(w_in_ext), "pool_w": np.asarray(inp["pool_w"][0], f32),
        "pool_scale_fm": fm(inp["pool_scale"][0], 2), "invw": invw, "invc0": invc0,
        "q_norm_g_fm": fm(inp["q_norm_g"][0], 3), "kv_norm_g_fm": fm(inp["kv_norm_g"][0], 2),
        "w_uq_ext": np.ascontiguousarray(wuq_ext), "w_ukv": np.asarray(inp["w_ukv"][0], f32),
        "w_out": np.asarray(inp["w_out"][0], f32), "ln_rows": ln_rows,
        "ln1_fm": np.concatenate([fm(inp["ln1_g"][0], 8), fm(inp["ln1_b"][0], 8)], axis=1),
        "w_peer_q": np.asarray(inp["w_peer_q"][0], f32), "keysT": keysT,
        "uT": np.ascontiguousarray(np.asarray(inp["peer_u"][0], f32).T), "v": np.asarray(inp["peer_v"][0], f32),
        "mask": mask, "ident": np.eye(128, dtype=f32),
        "iota_n": np.tile(np.arange(128, dtype=np.int32), (128, 1)),
        "iota_j": np.tile(np.arange(256, dtype=np.int32), (128, 1)),
        "iota16": np.tile(np.arange(16, dtype=f32), (128, 1)),
        "iota128": np.tile(np.arange(128, dtype=f32), (128, 1)),
    }


def kernel(**inputs):
    nc = build(False)
    shared = None
    in_maps = []
    for b in range(8):
        m = _host_inputs(inputs, b)
        if shared is None:
            shared = m
        else:
            for k in m:
                if k not in ("xT", "x", "c", "pos"):
                    m[k] = shared[k]
        in_maps.append(m)
    res = run_bass_kernel_spmd(nc, in_maps, core_ids=list(range(8)))
    return np.stack([np.asarray(r["out"], np.float32) for r in res.results], axis=0)
```

```python
import math
from contextlib import ExitStack
import numpy as np
import concourse.bass as bass
import concourse.mybir as mybir
from concourse.bass_utils import run_bass_kernel_spmd

F32 = mybir.dt.float32
BF16 = mybir.dt.bfloat16
I32 = mybir.dt.int32
AF = mybir.ActivationFunctionType
ALU = mybir.AluOpType
AX = mybir.AxisListType

SEQ = 8192
D = 1024
NH = 6
ALPHA = 2.0 ** 0.25
LN_EPS = 1e-5
RMS_EPS = 1e-6
QSCALE = 192.0 ** -0.5
T1 = 256
TG = 256
TWO_PI = 2.0 * math.pi
C1 = 6.28125
C2 = TWO_PI - C1
NEG = -3.0e38


class Sched:
    EPOCH = 24000
    NDS = 12

    def __init__(self, nc, st):
        self.nc, self.st = nc, st
        self.eng = dict(pe=nc.tensor, act=nc.scalar, dve=nc.vector, pool=nc.gpsimd, sp=nc.sync)
        self.sem, self.cnt, self.nsem = {}, {}, 0
        for e in self.eng:
            self._new_sem(e)
        self.waited = {e: {} for e in self.eng}
        self.lastw, self.readers = {}, {}
        self.dsem = {q: [[st.enter_context(nc.semaphore(f"d{q}{i}")), 0] for i in range(self.NDS)]
                     for q in ("sp", "pool", "act")}
        self.drr = {q: 0 for q in self.dsem}
        self.nops = 0

    def _new_sem(self, e):
        self.nsem += 1
        self.sem[e] = self.st.enter_context(self.nc.semaphore(f"s{e}{self.nsem}"))
        self.cnt[e] = 0

    def _wait(self, e, tok):
        sem, val = tok
        if val <= 0:
            return
        w = self.waited[e]
        if w.get(sem.num, 0) >= val:
            return
        self.eng[e].wait_ge(sem, val)
        w[sem.num] = val

    def _deps(self, e, r, w):
        deps = []
        for k in r:
            if k in self.lastw:
                deps.append(self.lastw[k])
        for k in w:
            if k in self.lastw:
                deps.append(self.lastw[k])
            deps.extend(self.readers.get(k, ()))
        for d in deps:
            if e == "pe" and d[2] == "pe":
                continue
            self._wait(e, (d[0], d[1]))

    def _commit(self, tok, r, w):
        for k in w:
            self.lastw[k] = tok
            self.readers[k] = []
        for k in r:
            self.readers.setdefault(k, []).append(tok)

    def op(self, e, fn, r=(), w=()):
        self._deps(e, r, w)
        if self.cnt[e] >= self.EPOCH:
            self._new_sem(e)
        ins = fn(self.eng[e])
        self.cnt[e] += 1
        ins.then_inc(self.sem[e], 1)
        tok = (self.sem[e], self.cnt[e], e)
        self._commit(tok, r, w)
        self.nops += 1
        return tok

    def dma(self, out, in_, r=(), w=(), q="sp"):
        slot = self.dsem[q][self.drr[q]]
        self.drr[q] = (self.drr[q] + 1) % self.NDS
        self._wait(q, (slot[0], slot[1]))
        self._deps(q, r, w)
        ins = self.eng[q].dma_start(out=out, in_=in_)
        slot[1] += 16
        ins.then_inc(slot[0], 16)
        tok = (slot[0], slot[1], "dma")
        self._commit(tok, r, w)
        return tok

    def snapshot(self, skip=()):
        toks = [(self.sem[e], self.cnt[e]) for e in self.eng if e not in skip]
        for q in self.dsem:
            if q not in skip:
                toks += [(s[0], s[1]) for s in self.dsem[q]]
        return toks

    def barrier_on(self, toks):
        for e in self.eng:
            for t in toks:
                self._wait(e, t)

    def barrier(self):
        toks = [(self.sem[e], self.cnt[e]) for e in self.eng]
        for q in self.dsem:
            toks += [(s[0], s[1]) for s in self.dsem[q]]
        for e in self.eng:
            for t in toks:
                self._wait(e, t)
        self.lastw.clear()
        self.readers.clear()

    def mm(self, out, lhsT, rhs, start, stop, r, w):
        return self.op("pe", lambda e: e.matmul(out, lhsT, rhs, start=start, stop=stop), r, w)


def build(debug=False, nst=64):
    nc = bass.Bass("TRN2", target_bir_lowering=False)

    def din(name, shape, dt=F32):
        return nc.dram_tensor(name, list(shape), dt, kind="ExternalInput").ap()

    scr_kind = "ExternalOutput" if debug else "Internal"

    def dscr(name, shape, dt):
        return nc.dram_tensor(name, list(shape), dt, kind=scr_kind).ap()

    xT_d = din("xT", [D, SEQ])
    x_d = din("x", [SEQ, D])
    c_d = din("c", [128, 8])
    pos_d = din("pos", [1, SEQ], I32)
    invf_d = din("invf", [64, 2])
    wada_d = din("w_ada", [D, 6 * D])
    bada_fm_d = din("b_ada_fm", [128, 48])
    bada_row_d = din("b_ada_row", [1, 6 * D])
    win_d = din("w_in_ext", [D, 1024])
    poolw_d = din("pool_w", [4, 64, 64])
    pools_d = din("pool_scale_fm", [128, 2])
    invw_d = din("invw", [128, 2])
    invc0_d = din("invc0", [128, 2, 512])
    qg_d = din("q_norm_g_fm", [128, 3])
    kvg_d = din("kv_norm_g_fm", [128, 2])
    wuq_d = din("w_uq_ext", [384, NH * 256])
    wukv_d = din("w_ukv", [256, NH * 256])
    wout_d = din("w_out", [D, D])
    ln_rows_d = din("ln_rows", [4, D])
    ln1_fm_d = din("ln1_fm", [128, 16])
    wpq_d = din("w_peer_q", [D, 2048])
    keysT_d = din("keysT", [128, 16 * 128])
    uT_d = din("uT", [D, 16384])
    v_d = din("v", [16384, D])
    mask_d = din("mask", [128, 4 * 512])
    ident_d = din("ident", [128, 128])
    iotan_d = din("iota_n", [128, 128], I32)
    iotaj_d = din("iota_j", [128, 256], I32)
    iota16_d = din("iota16", [128, 16])
    iota128_d = din("iota128", [128, 128])
    out_d = nc.dram_tensor("out", [SEQ, D], F32, kind="ExternalOutput").ap()

    cos_d = dscr("cos_scr", [64, SEQ], F32)
    sin_d = dscr("sin_scr", [64, SEQ], F32)
    apT_d = dscr("apT_scr", [D, SEQ], BF16)
    ubf_d = dscr("ubf_scr", [128, 8, 16384], BF16)
    vbf_d = dscr("vbf_scr", [128, 128, D], BF16)
    dbg = {}
    if debug:
        dbg["modT"] = nc.dram_tensor("dbg_modT", [128, 48], F32, kind="ExternalOutput").ap()
        dbg["lat"] = nc.dram_tensor("dbg_lat", [128, 6, SEQ], BF16, kind="ExternalOutput").ap()
        dbg["sel"] = nc.dram_tensor("dbg_sel", [128, 3, 128], F32, kind="ExternalOutput").ap()
        dbg["x1"] = nc.dram_tensor("dbg_x1", [128, D], F32, kind="ExternalOutput").ap()
        dbg["ffn"] = nc.dram_tensor("dbg_ffn", [128, D], F32, kind="ExternalOutput").ap()

    with ExitStack() as st:
        S = Sched(nc, st)

        def sb(name, shape, dt=F32, stack=st):
            return stack.enter_context(nc.sbuf_tensor("sb_" + name, list(shape), dt))

        psum = st.enter_context(nc.psum_tensor("psum", [128, 4096], F32))

        def bank(i, n=1):
            return psum[:, i * 512:(i + n) * 512]

        def bk(i, n=1):
            return [f"ps{j}" for j in range(i, i + n)]

        modT = sb("modT", [128, 48])
        sc1p = sb("sc1p", [128, 8])
        A2 = sb("A2", [128, 8])
        B2 = sb("B2", [128, 8])
        ident = sb("ident", [128, 128])
        ones_bf = sb("ones_bf", [128, 128], BF16)
        ones_f = sb("ones_f", [128, 128])
        invf = sb("invf", [64, 2])
        st12 = ExitStack()
        stg1 = ExitStack()
        stg2 = ExitStack()
        stw1 = ExitStack()
        g1_bc = sb("g1_bc", [128, D], stack=stg1)
        w_uq_bf = sb("w_uq_bf", [128, 3, NH * 256], BF16, stack=st12)
        w_ukv_bf = sb("w_ukv_bf", [128, 2, NH * 256], BF16, stack=st12)
        wbd = sb("wbd", [128, 2, 128], BF16, stack=st12)
        pools = sb("pools", [128, 2], stack=st12)
        invw = sb("invw", [128, 2], stack=st12)
        maskb = sb("maskb", [128, 4, 512], BF16, stack=st12)
        cqn = sb("cqn", [128, 3, SEQ], BF16, stack=st12)
        ckvn = sb("ckvn", [128, 2, SEQ], BF16, stack=st12)
        krT = sb("krT", [128, SEQ], BF16, stack=st12)
        g2_bc = sb("g2_bc", [128, D], stack=st12)
        cst = [sb(f"cst{i}", [128, 512], stack=st12) for i in range(2)]
        cbf = [sb(f"cbf{i}", [128, 512], BF16, stack=st12) for i in range(2)]
        w_in_bf = sb("w_in_bf", [128, 8, 1024], BF16, stack=stw1)
        invc0 = sb("invc0", [128, 2, T1], stack=stw1)
        S.dma(ident[:], ident_d, w=["ident"])
        S.dma(invf[:], invf_d, w=["invf"])
        S.op("dve", lambda e: e.memset(ones_f[:], 1.0), w=["ones_f"])
        S.op("dve", lambda e: e.memset(ones_bf[:], 1.0), w=["ones_bf"])
        S.op("pool", lambda e: e.memset(krT[64:128, :], 0.0), w=["krT"])

        with ExitStack() as s0:
            c_sb = sb("c_sb", [128, 8], stack=s0)
            c_act = sb("c_act", [128, 8], stack=s0)
            c_act2 = sb("c_act2", [128, 8, 2], stack=s0)
            c_bc = sb("c_bc", [128, 8, 128], stack=s0)
            bfm = sb("bfm", [128, 48], stack=s0)
            ln1fm = sb("ln1fm", [128, 16], stack=s0)
            wa = [sb(f"wa{i}", [128, 8, 256], stack=s0) for i in range(2)]
            S.dma(c_sb[:], c_d, w=["c_sb"])
            S.dma(bfm[:], bada_fm_d, w=["bfm"])
            S.dma(ln1fm[:], ln1_fm_d, w=["ln1fm"])
            S.dma(g1_bc[:], bada_row_d[0:1, 2048:3072].to_broadcast([128, D]), w=["g1_bc"])
            S.dma(g2_bc[:], bada_row_d[0:1, 5120:6144].to_broadcast([128, D]), w=["g2_bc"])
            S.op("act", lambda e: e.activation(out=c_act[:], in_=c_sb[:], func=AF.Silu), r=["c_sb"], w=["c_act"])
            S.op("dve", lambda e: e.tensor_copy(out=c_act2[:], in_=c_act[:].unsqueeze(2).to_broadcast([128, 8, 2])),
                 r=["c_act"], w=["c_act2"])
            S.op("dve", lambda e: e.tensor_copy(out=c_bc[:], in_=c_act[:].unsqueeze(2).to_broadcast([128, 8, 128])),
                 r=["c_act"], w=["c_bc"])
            wada_v = wada_d.rearrange("(k p) c -> p k c", p=128)
            psm = bank(7)[:, 0:96].rearrange("p (j two) -> p j two", two=2)
            for n in range(24):
                wt = wa[n % 2]
                S.dma(wt[:], wada_v[:, :, n * 256:(n + 1) * 256], w=[f"wa{n % 2}"])
                if n in (8, 9, 10, 11, 20, 21, 22, 23):
                    gt = g1_bc if n < 12 else g2_bc
                    gk = "g1_bc" if n < 12 else "g2_bc"
                    off = (n % 4) * 256
                    for k in range(8):
                        S.mm(bank(n % 2)[:, 0:256], c_bc[:, k, :], wt[:, k, :], k == 0, k == 7,
                             r=["c_bc", f"wa{n % 2}"], w=bk(n % 2))
                    S.op("dve", lambda e, gt=gt, off=off, n=n: e.tensor_tensor(
                        out=gt[:, off:off + 256], in0=bank(n % 2)[:, 0:256], in1=gt[:, off:off + 256], op=ALU.add),
                        r=bk(n % 2) + [gk], w=[gk])
                else:
                    for cb in range(2):
                        j = n * 2 + cb
                        for k in range(8):
                            S.mm(psm[:, j, :], wt[:, k, cb * 128:(cb + 1) * 128], c_act2[:, k, :], k == 0, k == 7,
                                 r=["c_act2", f"wa{n % 2}"], w=bk(7))
            S.op("dve", lambda e: e.tensor_tensor(out=modT[:], in0=psm[:, :, 0], in1=bfm[:], op=ALU.add),
                 r=bk(7) + ["bfm"], w=["modT"])
            S.op("dve", lambda e: e.tensor_scalar_add(out=sc1p[:], in0=modT[:, 8:16], scalar1=1.0), r=["modT"], w=["sc1p"])
            S.op("dve", lambda e: e.scalar_tensor_tensor(out=A2[:], in0=modT[:, 32:40], scalar=1.0, in1=ln1fm[:, 0:8],
                                                         op0=ALU.add, op1=ALU.mult), r=["modT", "ln1fm"], w=["A2"])
            S.op("dve", lambda e: e.scalar_tensor_tensor(out=B2[:], in0=modT[:, 32:40], scalar=1.0, in1=ln1fm[:, 8:16],
                                                         op0=ALU.add, op1=ALU.mult), r=["modT", "ln1fm"], w=["B2"])
            S.op("dve", lambda e: e.tensor_tensor(out=B2[:], in0=B2[:], in1=modT[:, 24:32], op=ALU.add),
                 r=["B2", "modT"], w=["B2"])
            if debug:
                S.dma(dbg["modT"], modT[:], r=["modT"])

            pos_i = sb("pos_i", [64, 512], I32, stack=s0)
            ang = sb("ang", [64, 512], stack=s0)
            ang2 = sb("ang2", [64, 512], stack=s0)
            ki = sb("ki", [64, 512], I32, stack=s0)
            kf = sb("kf", [64, 512], stack=s0)
            rr = sb("rr", [64, 512], stack=s0)
            tb = [sb(f"tb{i}", [64, 512], stack=s0) for i in range(2)]
            for ch in range(16):
                sl = slice(ch * 512, (ch + 1) * 512)
                S.dma(pos_i[:], pos_d[0:1, sl].to_broadcast([64, 512]), w=["pos_i"])
                S.op("dve", lambda e: e.tensor_copy(out=ang[:], in_=pos_i[:]), r=["pos_i"], w=["ang"])
                S.op("dve", lambda e: e.tensor_scalar(out=ang[:], in0=ang[:], scalar1=invf[:, 0:1], scalar2=None,
                                                      op0=ALU.mult), r=["ang", "invf"], w=["ang"])
                for which in range(2):
                    src = ang
                    if which == 0:
                        S.op("dve", lambda e: e.tensor_scalar_add(out=ang2[:], in0=ang[:], scalar1=math.pi / 2),
                             r=["ang"], w=["ang2"])
                        src = ang2
                    sn = "ang2" if which == 0 else "ang"
                    S.op("dve", lambda e, src=src: e.tensor_scalar(out=ki[:], in0=src[:], scalar1=1.0 / TWO_PI,
                                                                   scalar2=None, op0=ALU.mult), r=[sn], w=["ki"])
                    S.op("dve", lambda e: e.tensor_copy(out=kf[:], in_=ki[:]), r=["ki"], w=["kf"])
                    S.op("dve", lambda e, src=src: e.scalar_tensor_tensor(out=rr[:], in0=kf[:], scalar=-C1, in1=src[:],
                                                                          op0=ALU.mult, op1=ALU.add),
                         r=["kf", sn], w=["rr"])
                    S.op("dve", lambda e: e.scalar_tensor_tensor(out=rr[:], in0=kf[:], scalar=-C2, in1=rr[:],
                                                                 op0=ALU.mult, op1=ALU.add), r=["kf", "rr"], w=["rr"])
                    S.op("dve", lambda e: e.tensor_scalar(out=rr[:], in0=rr[:], scalar1=-3.14159, scalar2=3.14159,
                                                          op0=ALU.max, op1=ALU.min), r=["rr"], w=["rr"])
                    if which == 0:
                        S.op("act", lambda e: e.activation(out=tb[0][:], in_=rr[:], func=AF.Sin), r=["rr"], w=["tb0"])
                        S.dma(cos_d[:, sl], tb[0][:], r=["tb0"], w=["cos_d"])
                    else:
                        S.op("act", lambda e: e.activation(out=tb[1][:], in_=rr[:], func=AF.Sin, scale=invf[:, 1:2]),
                             r=["rr", "invf"], w=["tb1"])
                        S.dma(sin_d[:, sl], tb[1][:], r=["tb1"], w=["sin_d"])
        S.barrier()

        with ExitStack() as sp_:
            stg = [sb(f"stg{i}", [128, 2048], stack=sp_) for i in range(2)]
            qg = sb("qg", [128, 3], stack=sp_)
            kvg = sb("kvg", [128, 2], stack=sp_)
            wbdf = sb("wbdf", [128, 2, 128], stack=sp_)
            S.dma(qg[:], qg_d, w=["qg"])
            S.dma(kvg[:], kvg_d, w=["kvg"])
            S.dma(pools[:], pools_d, w=["pools"])
            S.dma(invw[:], invw_d, w=["invw"])
            S.dma(invc0[:], invc0_d[:, :, 0:T1], w=["invc0"])
            nstg = [0]

            def staged(src_ap, width, fn, rk=(), wk=()):
                i = nstg[0] % 2
                nstg[0] += 1
                S.dma(stg[i][:, 0:width], src_ap, w=[f"stg{i}"])
                S.op("dve" if i == 0 else "pool", lambda e: fn(e, stg[i][:, 0:width]), r=[f"stg{i}"] + list(rk), w=list(wk))

            win_v = win_d.rearrange("(k p) c -> p k c", p=128)
            for k in range(8):
                staged(win_v[:, k, :], 1024, lambda e, s, k=k: e.tensor_copy(out=w_in_bf[:, k, :], in_=s), wk=["w_in_bf"])
            wuq_v = wuq_d.rearrange("(k p) c -> p k c", p=128)
            for k in range(3):
                S.dma(stg[0][:, 0:1536], wuq_v[:, k, :], w=["stg0"])
                S.op("dve", lambda e, k=k: e.tensor_scalar(out=w_uq_bf[:, k, :], in0=stg[0][:, 0:1536], scalar1=qg[:, k:k + 1],
                                                           scalar2=QSCALE, op0=ALU.mult, op1=ALU.mult),
                     r=["stg0", "qg"], w=["w_uq_bf"])
            wukv_v = wukv_d.rearrange("(k p) c -> p k c", p=128)
            for k in range(2):
                S.dma(stg[1][:, 0:1536], wukv_v[:, k, :], w=["stg1"])
                S.op("dve", lambda e, k=k: e.tensor_scalar(out=w_ukv_bf[:, k, :], in0=stg[1][:, 0:1536],
                                                           scalar1=kvg[:, k:k + 1], scalar2=None, op0=ALU.mult),
                     r=["stg1", "kvg"], w=["w_ukv_bf"])
            staged(mask_d, 2048, lambda e, s: e.tensor_copy(out=maskb[:].rearrange("p a b -> p (a b)"), in_=s), wk=["maskb"])
            S.op("dve", lambda e: e.memset(wbdf[:], 0.0), w=["wbdf"])
            for g in range(4):
                S.dma(wbdf[(g % 2) * 64:(g % 2) * 64 + 64, g // 2, (g % 2) * 64:(g % 2) * 64 + 64], poolw_d[g],
                      w=["wbdf"])
            S.op("dve", lambda e: e.tensor_copy(out=wbd[:], in_=wbdf[:]), r=["wbdf"], w=["wbd"])

            snap = S.snapshot()
            n = 0
            for k in range(8):
                for ec in range(32):
                    i = n % 2
                    n += 1
                    S.dma(cst[i][:], uT_d[k * 128:(k + 1) * 128, ec * 512:(ec + 1) * 512], w=[f"cst{i}"], q="pool")
                    S.op("pool", lambda e, i=i: e.tensor_copy(out=cbf[i][:], in_=cst[i][:]), r=[f"cst{i}"], w=[f"cbf{i}"])
                    S.dma(ubf_d[:, k, ec * 512:(ec + 1) * 512], cbf[i][:], r=[f"cbf{i}"], w=["ubf_d"], q="pool")
            v_v = v_d.rearrange("(a p) f -> p a f", p=128)
            for a2 in range(128):
                for hf in range(2):
                    i = n % 2
                    n += 1
                    fs = slice(hf * 512, (hf + 1) * 512)
                    S.dma(cst[i][:], v_v[:, a2, fs], w=[f"cst{i}"], q="pool")
                    S.op("pool", lambda e, i=i, fs=fs: e.tensor_tensor(out=cbf[i][:], in0=cst[i][:], in1=g2_bc[:, fs], op=ALU.mult),
                         r=[f"cst{i}", "g2_bc"], w=[f"cbf{i}"])
                    S.dma(vbf_d[:, a2, fs], cbf[i][:], r=[f"cbf{i}"], w=["vbf_d"], q="pool")
            S.barrier_on(snap)

        with ExitStack() as s1:
            xTt = [sb(f"xTt{i}", [128, 8, T1], stack=s1) for i in range(2)]
            hT = [sb(f"hT{i}", [128, 8, T1], BF16, stack=s1) for i in range(2)]
            pbuf = [sb(f"pbuf{i}", [128, 2, T1 + 16], stack=s1) for i in range(2)]
            t2 = sb("t2", [128, 2, T1 + 16], stack=s1)
            t4 = sb("t4", [128, 2, T1 + 16], stack=s1)
            t8 = sb("t8", [128, T1 + 16], stack=s1)
            t16 = sb("t16", [128, T1 + 16], stack=s1)
            tmpf = sb("tmpf", [128, 2, T1], stack=s1)
            mixed = sb("mixed", [128, 2, T1], BF16, stack=s1)
            yT = [sb(f"yT{i}", [128, T1], BF16, stack=s1) for i in range(2)]
            sqb = [sb(f"sqb{i}", [128, T1], BF16, stack=s1) for i in range(2)]
            sqr = sb("sqr", [128, T1], stack=s1)
            rbc = sb("rbc", [128, T1], stack=s1)
            cst_ = [sb(f"cs{i}", [64, 2, T1], stack=s1) for i in range(2)]
            r1 = sb("r1", [64, T1], stack=s1)
            r2 = sb("r2", [64, T1], stack=s1)
            xT_v = xT_d.rearrange("(k p) t -> p k t", p=128)
            S.op("dve", lambda e: e.memset(pbuf[0][:, :, 0:16], 0.0), w=["pbuf0"])
            for tt in range(SEQ // T1):
                i = tt % 2
                ts = slice(tt * T1, (tt + 1) * T1)
                S.dma(xTt[i][:], xT_v[:, :, ts], w=[f"xTt{i}"])
                S.dma(cst_[i][:, 0, :], cos_d[:, ts], w=[f"cs{i}"])
                S.dma(cst_[i][:, 1, :], sin_d[:, ts], w=[f"cs{i}"])
                for k in range(8):
                    if k % 2 == 0:
                        S.op("act", lambda e, k=k: e.activation(out=hT[i][:, k, :], in_=xTt[i][:, k, :], func=AF.Identity,
                                                                scale=sc1p[:, k:k + 1], bias=modT[:, k:k + 1]),
                             r=[f"xTt{i}", "sc1p", "modT"], w=[f"hT{i}"])
                    else:
                        S.op("dve", lambda e, k=k: e.tensor_scalar(out=hT[i][:, k, :], in0=xTt[i][:, k, :],
                                                                   scalar1=sc1p[:, k:k + 1], scalar2=modT[:, k:k + 1],
                                                                   op0=ALU.mult, op1=ALU.add),
                             r=[f"xTt{i}", "sc1p", "modT"], w=[f"hT{i}"])

                def zmm(pb, col0, m, i=i):
                    for k in range(8):
                        S.mm(bank(pb)[0:m, 0:T1], w_in_bf[:, k, col0:col0 + m], hT[i][:, k, :], k == 0, k == 7,
                             r=[f"hT{i}", "w_in_bf"], w=bk(pb))
                pb_ = pbuf[i]
                for oc in range(2):
                    zmm(7, oc * 128, 128)
                    S.op("act", lambda e, oc=oc: e.copy(out=pb_[:, oc, 16:T1 + 16], in_=bank(7)[:, 0:T1]), r=bk(7), w=[f"pbuf{i}"])
                S.op("dve", lambda e: e.tensor_tensor(out=t2[:, :, 1:T1 + 16], in0=pb_[:, :, 1:T1 + 16], in1=pb_[:, :, 0:T1 + 15], op=ALU.add),
                     r=[f"pbuf{i}"], w=["t2"])
                S.op("dve", lambda e: e.tensor_tensor(out=t4[:, :, 3:T1 + 16], in0=t2[:, :, 3:T1 + 16], in1=t2[:, :, 1:T1 + 14], op=ALU.add),
                     r=["t2"], w=["t4"])
                S.op("dve", lambda e: e.tensor_tensor(out=t8[:, 7:T1 + 16], in0=t4[:, 1, 7:T1 + 16], in1=t4[:, 1, 3:T1 + 12], op=ALU.add),
                     r=["t4"], w=["t8"])
                S.op("dve", lambda e: e.tensor_tensor(out=t16[:, 15:T1 + 16], in0=t8[:, 15:T1 + 16], in1=t8[:, 7:T1 + 8], op=ALU.add),
                     r=["t8"], w=["t16"])
                srcs = [(t2, 0, 0), (t4, 0, 64), (None, 1, 0), (None, 1, 64)]
                for (tsrc, ch, p0) in srcs:
                    if tsrc is None:
                        win_ap = (t8 if p0 == 0 else t16)[p0:p0 + 64, 16:T1 + 16]
                    else:
                        win_ap = tsrc[p0:p0 + 64, ch, 16:T1 + 16]
                    if tt == 0:
                        S.op("dve", lambda e, win_ap=win_ap, ch=ch, p0=p0: e.tensor_tensor(
                            out=tmpf[p0:p0 + 64, ch, :], in0=win_ap, in1=invc0[p0:p0 + 64, ch, :], op=ALU.mult),
                            r=["t2", "t4", "t8", "t16", "invc0"], w=["tmpf"])
                        S.op("dve", lambda e, ch=ch, p0=p0: e.tensor_tensor(
                            out=mixed[p0:p0 + 64, ch, :], in0=tmpf[p0:p0 + 64, ch, :], in1=pb_[p0:p0 + 64, ch, 16:T1 + 16],
                            op=ALU.subtract), r=["tmpf", f"pbuf{i}"], w=["mixed"])
                    else:
                        S.op("dve", lambda e, win_ap=win_ap, ch=ch, p0=p0: e.scalar_tensor_tensor(
                            out=mixed[p0:p0 + 64, ch, :], in0=win_ap, scalar=invw[p0:p0 + 64, ch:ch + 1],
                            in1=pb_[p0:p0 + 64, ch, 16:T1 + 16], op0=ALU.mult, op1=ALU.subtract),
                            r=["t2", "t4", "t8", "t16", "invw", f"pbuf{i}"], w=["mixed"])
                S.op("dve", lambda e: e.tensor_copy(out=pbuf[1 - i][:, :, 0:16], in_=pb_[:, :, T1:T1 + 16]),
                     r=[f"pbuf{i}"], w=[f"pbuf{1 - i}"])
                for ch in range(2):
                    S.mm(bank(7)[:, 0:T1], wbd[:, ch, :], mixed[:, ch, :], True, True, r=["wbd", "mixed"], w=bk(7))
                    S.op("act", lambda e, ch=ch: e.activation(out=yT[ch][:], in_=bank(7)[:, 0:T1], func=AF.Identity,
                                                              scale=pools[:, ch:ch + 1]), r=bk(7) + ["pools"], w=[f"yT{ch}"])
                    S.dma(apT_d[ch * 128:(ch + 1) * 128, ts], yT[ch][:], r=[f"yT{ch}"], w=["apT_d"])
                for (name, dst, nchunk, col0, pb0, pss, nfeat) in (("q", cqn, 3, 256, 0, 3, 384.0),
                                                                    ("kv", ckvn, 2, 640, 4, 6, 256.0)):
                    for j in range(nchunk):
                        zmm(pb0 + j, col0 + j * 128, 128)
                        S.op("act", lambda e, j=j, pb0=pb0: e.activation(out=sqb[j % 2][:], in_=bank(pb0 + j)[:, 0:T1], func=AF.Square),
                             r=bk(pb0 + j), w=[f"sqb{j % 2}"])
                        S.mm(bank(pss)[:, 0:T1], ones_bf[:], sqb[j % 2][:], j == 0, j == nchunk - 1, r=["ones_bf", f"sqb{j % 2}"],
                             w=bk(pss))
                    S.op("dve", lambda e, pss=pss, nfeat=nfeat: e.tensor_scalar(out=sqr[:], in0=bank(pss)[:, 0:T1], scalar1=1.0 / nfeat,
                                                                               scalar2=RMS_EPS, op0=ALU.mult, op1=ALU.add),
                         r=bk(pss), w=["sqr"])
                    S.op("act", lambda e: e.activation(out=sqr[:], in_=sqr[:], func=AF.Sqrt), r=["sqr"], w=["sqr"])
                    S.op("dve", lambda e: e.reciprocal(out=rbc[:], in_=sqr[:]), r=["sqr"], w=["rbc"])
                    for j in range(nchunk):
                        S.op("dve", lambda e, j=j, dst=dst, pb0=pb0: e.tensor_tensor(out=dst[:, j, ts], in0=bank(pb0 + j)[:, 0:T1],
                                                                                     in1=rbc[:], op=ALU.mult),
                             r=bk(pb0 + j) + ["rbc"], w=[name + "n"])
                zmm(0, 896, 64)
                zmm(1, 960, 64)
                S.op("dve", lambda e: e.tensor_tensor(out=r1[:], in0=bank(0)[0:64, 0:T1], in1=cst_[i][:, 0, :], op=ALU.mult),
                     r=bk(0) + [f"cs{i}"], w=["r1"])
                S.op("dve", lambda e: e.tensor_tensor(out=r2[:], in0=bank(1)[0:64, 0:T1], in1=cst_[i][:, 1, :], op=ALU.mult),
                     r=bk(1) + [f"cs{i}"], w=["r2"])
                S.op("dve", lambda e: e.tensor_tensor(out=krT[0:64, ts], in0=r1[:], in1=r2[:], op=ALU.add),
                     r=["r1", "r2"], w=["krT"])
            if debug:
                S.dma(dbg["lat"][:, 0:3, :], cqn[:], r=["qn"])
                S.dma(dbg["lat"][:, 3:5, :], ckvn[:], r=["kvn"])
                S.dma(dbg["lat"][0:64, 5, :], krT[0:64, :], r=["krT"])
        S.barrier_on(S.snapshot(skip=("pool",)))
        stw1.close()

        with ExitStack() as s2:
            qnT2 = [sb(f"qnT{i}", [128, 512], BF16, stack=s2) for i in range(2)]
            qrT2 = [sb(f"qrT{i}", [128, 512], BF16, stack=s2) for i in range(2)]
            for i_ in range(2):
                S.op("dve", lambda e, i_=i_: e.memset(qrT2[i_][64:128, :], 0.0), w=[f"qrT{i_}"])
            knT = sb("knT", [128, SEQ], BF16, stack=s2)
            V = sb("V", [128, 64, 128], BF16, stack=s2)
            cs2 = [sb(f"cs2{i}", [64, 2, TG], stack=s2) for i in range(2)]
            r1 = sb("r1b", [64, TG], stack=s2)
            r2 = sb("r2b", [64, TG], stack=s2)
            pT = [sb(f"pT{i}", [128, 512], BF16, stack=s2) for i in range(3)]
            acc = [[sb(f"acc{i}{p}", [128, 512], stack=s2) for p in range(2)] for i in range(2)]
            rl = sb("rl", [128, 512], stack=s2)
            att = [sb(f"att{i}", [128, 512], BF16, stack=s2) for i in range(2)]
            for h in range(NH):
                wq0 = h * 256
                for tt in range(SEQ // TG):
                    ts = slice(tt * TG, (tt + 1) * TG)
                    for k in range(2):
                        S.mm(bank(6)[:, 0:TG], w_ukv_bf[:, k, wq0:wq0 + 128], ckvn[:, k, ts], k == 0, k == 1, r=["w_ukv_bf", "kvn"], w=bk(6))
                    S.op("act", lambda e, ts=ts: e.copy(out=knT[:, ts], in_=bank(6)[:, 0:TG]), r=bk(6), w=["knT"])
                    for sub in range(TG // 128):
                        tsub = slice(tt * TG + sub * 128, tt * TG + sub * 128 + 128)
                        for k in range(2):
                            S.mm(bank(7)[:, sub * 128:(sub + 1) * 128], ckvn[:, k, tsub], w_ukv_bf[:, k, wq0 + 128:wq0 + 256],
                                 k == 0, k == 1, r=["w_ukv_bf", "kvn"], w=bk(7))
                    S.op("dve", lambda e, tt=tt: e.tensor_copy(out=V[:, tt * (TG // 128):(tt + 1) * (TG // 128), :],
                                                              in_=bank(7)[:, 0:TG].rearrange("p (a b) -> p a b", b=128)),
                         r=bk(7), w=["V"])
                def qgen(j):
                    qnT = qnT2[j % 2]
                    qrT = qrT2[j % 2]
                    qnk, qrk = f"qnT{j % 2}", f"qrT{j % 2}"
                    for sub in range(512 // TG):
                        i = (j * (512 // TG) + sub) % 2
                        ts = slice(j * 512 + sub * TG, j * 512 + (sub + 1) * TG)
                        so = slice(sub * TG, (sub + 1) * TG)
                        S.dma(cs2[i][:, 0, :], cos_d[:, ts], w=[f"cs2{i}"])
                        S.dma(cs2[i][:, 1, :], sin_d[:, ts], w=[f"cs2{i}"])
                        for k in range(3):
                            S.mm(bank(6)[:, 0:TG], w_uq_bf[:, k, wq0:wq0 + 128], cqn[:, k, ts], k == 0, k == 2, r=["w_uq_bf", "qn"], w=bk(6))
                        S.op("act", lambda e, so=so, qnT=qnT: e.copy(out=qnT[:, so], in_=bank(6)[:, 0:TG]), r=bk(6), w=[qnk])
                        for (pb, c0) in ((6, 128), (7, 192)):
                            for k in range(3):
                                S.mm(bank(pb)[0:64, 0:TG], w_uq_bf[:, k, wq0 + c0:wq0 + c0 + 64], cqn[:, k, ts], k == 0, k == 2,
                                     r=["w_uq_bf", "qn"], w=bk(pb))
                        S.op("dve", lambda e, i=i: e.tensor_tensor(out=r1[:], in0=bank(6)[0:64, 0:TG], in1=cs2[i][:, 0, :], op=ALU.mult),
                             r=bk(6) + [f"cs2{i}"], w=["r1"])
                        S.op("dve", lambda e, i=i: e.tensor_tensor(out=r2[:], in0=bank(7)[0:64, 0:TG], in1=cs2[i][:, 1, :], op=ALU.mult),
                             r=bk(7) + [f"cs2{i}"], w=["r2"])
                        S.op("dve", lambda e, so=so, qrT=qrT: e.tensor_tensor(out=qrT[0:64, so], in0=r1[:], in1=r2[:], op=ALU.add),
                             r=["r1", "r2"], w=[qrk])

                tiles = [(j, kt) for j in range(16) for kt in range(4 * j + 4)]

                def emit_S(n):
                    j, kt = tiles[n]
                    sbk = n % 3
                    ks = slice(kt * 128, (kt + 1) * 128)
                    S.mm(bank(sbk), knT[:, ks], qnT2[j % 2][:], True, False, r=["knT", f"qnT{j % 2}"], w=bk(sbk))
                    S.mm(bank(sbk), krT[:, ks], qrT2[j % 2][:], False, True, r=["krT", f"qrT{j % 2}"], w=bk(sbk))

                def epilogue(j):
                    ob = 3 + (j % 2)
                    for p_ in range(2):
                        S.mm(bank(5), ones_f[:], acc[j % 2][p_][:], p_ == 0, p_ == 1, r=["ones_f", f"acc{j % 2}{p_}"], w=bk(5))
                    S.op("dve", lambda e: e.reciprocal(out=rl[:], in_=bank(5)), r=bk(5), w=["rl"])
                    S.op("dve", lambda e, ob=ob, j=j: e.tensor_tensor(out=att[j % 2][:], in0=bank(ob), in1=rl[:], op=ALU.mult),
                         r=bk(ob) + ["rl"], w=[f"att{j % 2}"])
                    S.dma(apT_d[256 + h * 128:256 + (h + 1) * 128, j * 512:(j + 1) * 512], att[j % 2][:], r=[f"att{j % 2}"], w=["apT_d"])

                qgen(0)
                qgen(1)
                emit_S(0)
                emit_S(1)
                pend = None
                for n, (j, kt) in enumerate(tiles):
                    nk = 4 * j + 4
                    sbk = n % 3
                    ob = 3 + (j % 2)
                    ac = acc[j % 2][kt % 2]
                    ack = f"acc{j % 2}{kt % 2}"
                    S.op("act", lambda e, sbk=sbk: e.activation(out=pT[sbk][:], in_=bank(sbk), func=AF.Exp),
                         r=bk(sbk), w=[f"pT{sbk}"])
                    if kt >= 4 * j:
                        S.op("dve", lambda e, sbk=sbk, m=kt - 4 * j: e.tensor_tensor(
                            out=pT[sbk][:], in0=pT[sbk][:], in1=maskb[:, m, :], op=ALU.mult),
                            r=[f"pT{sbk}", "maskb"], w=[f"pT{sbk}"])
                    S.mm(bank(ob), V[:, kt, :], pT[sbk][:], kt == 0, kt == nk - 1, r=["V", f"pT{sbk}"], w=bk(ob))
                    if n + 2 < len(tiles):
                        emit_S(n + 2)
                    if kt < 2:
                        S.op("dve", lambda e, sbk=sbk, ac=ac: e.tensor_copy(out=ac[:], in_=pT[sbk][:]),
                             r=[f"pT{sbk}"], w=[ack])
                    else:
                        S.op("dve", lambda e, sbk=sbk, ac=ac: e.tensor_tensor(out=ac[:], in0=ac[:], in1=pT[sbk][:], op=ALU.add),
                             r=[f"pT{sbk}", ack], w=[ack])
                    if pend is not None and kt == 1:
                        epilogue(pend)
                        pend = None
                    if kt == 2 and j + 1 < 16 and j >= 1:
                        qgen(j + 1)
                    if kt == nk - 1:
                        pend = j
                epilogue(pend)
        S.barrier()
        st12.close()

        with ExitStack() as s3:
            wq_bf = sb("wq_bf", [128, 8, 2048], BF16, stack=s3)
            w_out_bf = sb("w_out_bf", [128, 8, D], BF16, stack=s3)
            keysT = sb("keysT", [128, 16, 128], BF16, stack=s3)
            lnb = sb("lnb", [128, 4, D], stack=s3)
            iota_n = sb("iota_n", [128, 128], I32, stack=s3)
            iota_j = sb("iota_j", [128, 256], I32, stack=s3)
            iota16 = sb("iota16", [128, 16], stack=s3)
            iota128 = sb("iota128", [128, 128], stack=s3)
            S.dma(iota_n[:], iotan_d, w=["iota_n"])
            S.dma(iota_j[:], iotaj_d, w=["iota_j"])
            S.dma(iota16[:], iota16_d, w=["iota16"])
            S.dma(iota128[:], iota128_d, w=["iota128"])
            for r_ in range(4):
                S.dma(lnb[:, r_, :], ln_rows_d[r_:r_ + 1, :].to_broadcast([128, D]), w=["lnb"])
            with ExitStack() as sp3:
                stg = [sb(f"stg3{i}", [128, 2048], stack=sp3) for i in range(2)]
                wpq_v = wpq_d.rearrange("(k p) c -> p k c", p=128)
                for k in range(8):
                    S.dma(stg[k % 2][:], wpq_v[:, k, :], w=[f"stg3{k % 2}"])
                    S.op("dve" if k % 2 == 0 else "pool", lambda e, k=k: e.tensor_copy(out=wq_bf[:, k, :], in_=stg[k % 2][:]),
                         r=[f"stg3{k % 2}"], w=["wq_bf"])
                wout_v = wout_d.rearrange("(k p) c -> p k c", p=128)
                for k in range(8):
                    S.dma(stg[k % 2][:, 0:1024], wout_v[:, k, :], w=[f"stg3{k % 2}"])
                    S.op("dve", lambda e, k=k: e.tensor_tensor(out=w_out_bf[:, k, :], in0=stg[k % 2][:, 0:1024], in1=g1_bc[:],
                                                                           op=ALU.mult), r=[f"stg3{k % 2}", "g1_bc"], w=["w_out_bf"])
                S.dma(stg[0][:], keysT_d, w=["stg30"])
                S.op("dve", lambda e: e.tensor_copy(out=keysT[:].rearrange("p a b -> p (a b)"), in_=stg[0][:]),
                     r=["stg30"], w=["keysT"])
                S.barrier()

            apT4 = [sb(f"apT4{i}", [128, 8, 128], BF16, stack=s3) for i in range(2)]
            xt = [sb(f"xt{i}", [128, D], stack=s3) for i in range(1)]
            y = sb("y", [128, D], stack=s3)
            n1 = y
            x1s = [sb(f"x1{i}", [128, D], stack=s3) for i in range(2)]
            ot = [sb(f"ot{i}", [128, D], stack=s3) for i in range(1)]
            stats = sb("stats", [128, 2, 6], stack=s3)
            mv = sb("mv", [128, 2], stack=s3)
            rstd = sb("rstd", [128, 1], stack=s3)
            h2T = sb("h2T", [128, 8, 256], BF16, stack=s3)
            qT = sb("qT", [128, 16, 128], BF16, stack=s3)
            sc = sb("sc", [128, 16, 128], stack=s3)
            scr = sb("scrx", [128, 16, 128], stack=s3)
            s1t = sb("s1t", [128, 16, 16], stack=s3)
            idx1i = sb("idx1i", [128, 16, 16], I32, stack=s3)
            idx1f = sb("idx1f", [128, 16, 16], stack=s3)
            cand = sb("cand", [128, 8, 256], stack=s3)
            stop_ = sb("stop", [128, 8, 16], stack=s3)
            ji = sb("ji", [128, 8, 16], I32, stack=s3)
            ai = sb("ai", [128, 8, 16], I32, stack=s3)
            bi = sb("bi", [128, 8, 16], I32, stack=s3)
            af = sb("af", [128, 8, 16], stack=s3)
            bf_ = sb("bf", [128, 8, 16], stack=s3)
            sel = sb("sel", [128, 3, 128], stack=s3)
            ex = sb("ex", [128, 8, 16], stack=s3)
            ssum = sb("ssum", [128, 8], stack=s3)
            selT = sb("selT", [128, 3, 256], stack=s3)
            Bh = [sb(f"Bh{i}", [128, 8, 128], BF16, stack=s3) for i in range(2)]
            Ae = [sb(f"Ae{i}", [128, 8, 64], BF16, stack=s3) for i in range(2)]
            iota_bf = sb("iota_bf", [128, 128], BF16, stack=s3)
            S.op("dve", lambda e: e.tensor_copy(out=iota_bf[:], in_=iota128[:]), r=["iota128"], w=["iota_bf"])
            G = sb("G", [128, 64, 256], BF16, stack=s3)
            ublk = [sb(f"ublk{i}", [128, 8, 256], BF16, stack=s3) for i in range(2)]
            vblk = [sb(f"vblk{i}", [128, 2, D], BF16, stack=s3) for i in range(2)]
            gel = [sb(f"gel{i}", [128, 512], BF16, stack=s3) for i in range(2)]
            W = [sb(f"W{i}", [128, 512], BF16, stack=s3) for i in range(2)]
            eq = cand[:].rearrange("p h (a b) -> p h a b", a=16)
            apT_v = apT_d.rearrange("(k p) t -> p k t", p=128)
            out_toks = []

            def layer_norm(src, dst_n, srck, dstk):
                for c_ in range(2):
                    S.op("dve", lambda e, c_=c_: e.bn_stats(out=stats[:, c_, :], in_=src[:, c_ * 512:(c_ + 1) * 512]),
                         r=[srck], w=["stats"])
                S.op("dve", lambda e: e.bn_aggr(out=mv[:], in_=stats[:].rearrange("p a b -> p (a b)")), r=["stats"], w=["mv"])
                S.op("dve", lambda e: e.tensor_scalar_add(out=rstd[:], in0=mv[:, 1:2], scalar1=LN_EPS), r=["mv"], w=["rstd"])
                S.op("act", lambda e: e.activation(out=rstd[:], in_=rstd[:], func=AF.Sqrt), r=["rstd"], w=["rstd"])
                S.op("dve", lambda e: e.reciprocal(out=rstd[:], in_=rstd[:]), r=["rstd"], w=["rstd"])
                S.op("dve", lambda e: e.tensor_scalar(out=dst_n[:], in0=src[:], scalar1=mv[:, 0:1], scalar2=rstd[:, 0:1],
                                                      op0=ALU.subtract, op1=ALU.mult), r=[srck, "mv", "rstd"], w=[dstk])

            NST = nst
            assert NST % 2 == 0
            for pr in range(NST // 2):
                for s in (2 * pr, 2 * pr + 1):
                    t0 = s * 128
                    i = 0
                    sub = s % 2
                    x1 = x1s[sub]
                    x1k = f"x1{sub}"
                    hs = slice(sub * 128, (sub + 1) * 128)
                    S.dma(apT4[sub][:], apT_v[:, :, t0:t0 + 128], w=[f"apT4{sub}"])
                    ap4 = apT4[sub]
                    ap4k = f"apT4{sub}"
                    toff = 0
                    S.dma(xt[i][:], x_d[t0:t0 + 128, :], w=[f"xt{i}"])
                    for half in range(2):
                        for k in range(8):
                            S.mm(bank(half), ap4[:, k, toff:toff + 128], w_out_bf[:, k, half * 512:(half + 1) * 512], k == 0, k == 7,
                                 r=[ap4k, "w_out_bf"], w=bk(half))
                    S.op("dve", lambda e, i=i: e.scalar_tensor_tensor(out=y[:], in0=xt[i][:], scalar=ALPHA, in1=bank(0, 2),
                                                                      op0=ALU.mult, op1=ALU.add), r=[f"xt{i}"] + bk(0, 2), w=["y"])
                    layer_norm(y, y, "y", "y")
                    S.op("pool", lambda e, x1=x1: e.tensor_tensor(out=x1[:], in0=n1[:], in1=lnb[:, 0, :], op=ALU.mult), r=["y", "lnb"], w=[x1k])
                    S.op("pool", lambda e, x1=x1: e.tensor_tensor(out=x1[:], in0=x1[:], in1=lnb[:, 1, :], op=ALU.add), r=[x1k, "lnb"], w=[x1k])
                    for k in range(8):
                        S.op("pe", lambda e, k=k: e.transpose(bank(2, 2)[:, k * 128:(k + 1) * 128], n1[:, k * 128:(k + 1) * 128], ident[:]),
                             r=["y", "ident"], w=bk(2 + k // 4))
                    for k in range(8):
                        S.op("act", lambda e, k=k, hs=hs: e.activation(out=h2T[:, k, hs], in_=bank(2, 2)[:, k * 128:(k + 1) * 128],
                                                                func=AF.Identity, scale=A2[:, k:k + 1], bias=B2[:, k:k + 1]),
                             r=bk(2 + k // 4) + ["A2", "B2"], w=["h2T"])
                    for oc in range(16):
                        pb = 4 + oc // 4
                        for k in range(8):
                            S.mm(bank(pb)[:, (oc % 4) * 128:(oc % 4 + 1) * 128], wq_bf[:, k, oc * 128:(oc + 1) * 128], h2T[:, k, hs],
                                 k == 0, k == 7, r=["wq_bf", "h2T"], w=bk(pb))
                    for b4 in range(4):
                        S.op("act" if b4 % 2 == 0 else "dve",
                             (lambda e, b4=b4: e.copy(out=qT[:, b4 * 4:b4 * 4 + 4, :], in_=bank(4 + b4).rearrange("p (a b) -> p a b", a=4)))
                             if b4 % 2 == 0 else
                             (lambda e, b4=b4: e.tensor_copy(out=qT[:, b4 * 4:b4 * 4 + 4, :], in_=bank(4 + b4).rearrange("p (a b) -> p a b", a=4))),
                             r=bk(4 + b4), w=["qT"])
                    for hp in range(16):
                        pb = 4 + hp // 4
                        S.mm(bank(pb)[:, (hp % 4) * 128:(hp % 4 + 1) * 128], qT[:, hp, :], keysT[:, hp, :], True, True,
                             r=["qT", "keysT"], w=bk(pb))
                    for b4 in range(4):
                        S.op("dve", lambda e, b4=b4: e.tensor_single_scalar(
                            out=sc[:, b4 * 4:b4 * 4 + 4, :].bitcast(I32),
                            in_=bank(4 + b4).rearrange("p (a b) -> p a b", a=4).bitcast(I32), scalar=-128, op=ALU.bitwise_and),
                            r=bk(4 + b4), w=["sc"])
                    S.op("dve", lambda e: e.tensor_tensor(out=sc[:].bitcast(I32), in0=sc[:].bitcast(I32),
                                                          in1=iota_n[:].unsqueeze(1).to_broadcast([128, 16, 128]), op=ALU.bitwise_or),
                         r=["sc", "iota_n"], w=["sc"])
                    for hp in range(16):
                        S.op("dve", lambda e, hp=hp: e.max(out=s1t[:, hp, 0:8], in_=sc[:, hp, :]), r=["sc"], w=["s1t"])
                        S.op("dve", lambda e, hp=hp: e.match_replace(out=scr[:, hp, :], in_to_replace=s1t[:, hp, 0:8],
                                                                     in_values=sc[:, hp, :], imm_value=NEG), r=["sc", "s1t"], w=["scr"])
                        S.op("dve", lambda e, hp=hp: e.max(out=s1t[:, hp, 8:16], in_=scr[:, hp, :]), r=["scr"], w=["s1t"])
                    S.op("dve", lambda e: e.tensor_single_scalar(out=idx1i[:], in_=s1t[:].bitcast(I32), scalar=127, op=ALU.bitwise_and),
                         r=["s1t"], w=["idx1i"])
                    S.op("dve", lambda e: e.tensor_copy(out=idx1f[:], in_=idx1i[:]), r=["idx1i"], w=["idx1f"])
                    s1v = s1t[:].rearrange("p (h two) a -> p h two a", two=2)
                    candv = cand[:].rearrange("p h (a b) -> p h a b", a=16)
                    S.op("dve", lambda e: e.tensor_tensor(out=candv, in0=s1v[:, :, 0, :].unsqueeze(3).to_broadcast([128, 8, 16, 16]),
                                                          in1=s1v[:, :, 1, :].unsqueeze(2).to_broadcast([128, 8, 16, 16]), op=ALU.add),
                         r=["s1t"], w=["cand"])
                    S.op("dve", lambda e: e.tensor_single_scalar(out=cand[:].bitcast(I32), in_=cand[:].bitcast(I32), scalar=-256,
                                                                 op=ALU.bitwise_and), r=["cand"], w=["cand"])
                    S.op("dve", lambda e: e.tensor_tensor(out=cand[:].bitcast(I32), in0=cand[:].bitcast(I32),
                                                          in1=iota_j[:].unsqueeze(1).to_broadcast([128, 8, 256]), op=ALU.bitwise_or),
                         r=["cand", "iota_j"], w=["cand"])
                    scr2 = scr[:].rearrange("p (h two) n -> p h (two n)", two=2)
                    for h in range(8):
                        S.op("dve", lambda e, h=h: e.max(out=stop_[:, h, 0:8], in_=cand[:, h, :]), r=["cand"], w=["stop"])
                        S.op("dve", lambda e, h=h: e.match_replace(out=scr2[:, h, :], in_to_replace=stop_[:, h, 0:8],
                                                                   in_values=cand[:, h, :], imm_value=NEG), r=["cand", "stop"], w=["scr"])
                        S.op("dve", lambda e, h=h: e.max(out=stop_[:, h, 8:16], in_=scr2[:, h, :]), r=["scr"], w=["stop"])
                    S.op("dve", lambda e: e.tensor_single_scalar(out=ji[:], in_=stop_[:].bitcast(I32), scalar=255, op=ALU.bitwise_and),
                         r=["stop"], w=["ji"])
                    S.op("dve", lambda e: e.tensor_single_scalar(out=ai[:], in_=ji[:], scalar=4, op=ALU.logical_shift_right),
                         r=["ji"], w=["ai"])
                    S.op("dve", lambda e: e.tensor_single_scalar(out=bi[:], in_=ji[:], scalar=15, op=ALU.bitwise_and),
                         r=["ji"], w=["bi"])
                    S.op("dve", lambda e: e.tensor_copy(out=af[:], in_=ai[:]), r=["ai"], w=["af"])
                    S.op("dve", lambda e: e.tensor_copy(out=bf_[:], in_=bi[:]), r=["bi"], w=["bf"])
                    idxv = idx1f[:].rearrange("p (h two) a -> p h two a", two=2)
                    for which, (srcf, srck) in enumerate(((af, "af"), (bf_, "bf"))):
                        S.op("dve", lambda e, srcf=srcf: e.tensor_tensor(
                            out=eq, in0=srcf[:].unsqueeze(3).to_broadcast([128, 8, 16, 16]),
                            in1=iota16[:].unsqueeze(1).unsqueeze(1).to_broadcast([128, 8, 16, 16]), op=ALU.is_equal),
                            r=[srck, "iota16"], w=["cand"])
                        S.op("dve", lambda e, which=which: e.tensor_tensor(
                            out=eq, in0=eq, in1=idxv[:, :, which, :].unsqueeze(2).to_broadcast([128, 8, 16, 16]), op=ALU.mult),
                            r=["cand", "idx1f"], w=["cand"])
                        S.op("dve", lambda e, which=which: e.tensor_reduce(
                            out=sel[:, which, :], in_=cand[:].rearrange("p h (k a) -> p (h k) a", a=16), axis=AX.X, op=ALU.add),
                            r=["cand"], w=["sel"])
                    S.op("dve", lambda e: e.tensor_tensor(out=ex[:], in0=stop_[:], in1=stop_[:, :, 0:1].to_broadcast([128, 8, 16]),
                                                          op=ALU.subtract), r=["stop"], w=["ex"])
                    S.op("act", lambda e: e.activation(out=ex[:], in_=ex[:], func=AF.Exp), r=["ex"], w=["ex"])
                    S.op("dve", lambda e: e.tensor_reduce(out=ssum[:], in_=ex[:], axis=AX.X, op=ALU.add), r=["ex"], w=["ssum"])
                    S.op("dve", lambda e: e.reciprocal(out=ssum[:], in_=ssum[:]), r=["ssum"], w=["ssum"])
                    S.op("dve", lambda e: e.tensor_tensor(out=sel[:, 2, :].rearrange("p (h k) -> p h k", h=8), in0=ex[:],
                                                          in1=ssum[:].unsqueeze(2).to_broadcast([128, 8, 16]), op=ALU.mult),
                         r=["ex", "ssum"], w=["sel"])
                    if debug and s == 0:
                        S.dma(dbg["sel"], sel[:], r=["sel"])
                        S.dma(dbg["x1"], x1[:], r=[x1k])
                    for w_ in range(3):
                        S.op("pe", lambda e, w_=w_: e.transpose(bank(2)[:, w_ * 128:(w_ + 1) * 128], sel[:, w_, :], ident[:]),
                             r=["sel", "ident"], w=bk(2))
                    S.op("dve", lambda e, hs=hs: e.tensor_copy(out=selT[:, :, hs], in_=bank(2)[:, 0:384].rearrange("p (a b) -> p a b", a=3)), r=bk(2), w=["selT"])
                def onehot_G(half):
                    ng = 0
                    for c0 in range(0, 256, 8):
                        ob_ = (c0 // 8) % 2
                        for tt_ in range(8):
                            t_ = c0 + tt_
                            S.op("dve", lambda e, t_=t_, tt_=tt_, ob_=ob_: e.tensor_scalar(
                                out=Bh[ob_][:, tt_, :], in0=iota_bf[:], scalar1=selT[:, 1, t_:t_ + 1], scalar2=None, op0=ALU.is_equal),
                                r=["selT", "iota_bf"], w=[f"Bh{ob_}"])
                            S.op("dve", lambda e, t_=t_, tt_=tt_, ob_=ob_: e.tensor_scalar(
                                out=Ae[ob_][:, tt_, :], in0=iota_bf[:, half * 64:(half + 1) * 64], scalar1=selT[:, 0, t_:t_ + 1],
                                scalar2=selT[:, 2, t_:t_ + 1], op0=ALU.is_equal, op1=ALU.mult), r=["selT", "iota_bf"], w=[f"Ae{ob_}"])
                        pb = 6 + (ng % 2)
                        ng += 1
                        for tt_ in range(8):
                            S.mm(bank(pb)[:, tt_ * 64:(tt_ + 1) * 64], Bh[ob_][:, tt_, :], Ae[ob_][:, tt_, :], True, True,
                                 r=[f"Bh{ob_}", f"Ae{ob_}"], w=bk(pb))
                        S.op("act", lambda e, pb=pb, c0=c0: e.copy(out=G[:, :, c0:c0 + 8],
                                                                   in_=bank(pb).rearrange("p (t i) -> p i t", t=8)),
                             r=bk(pb), w=["G"])

                def dense(half):
                    def emit_ST(g):
                        gg = half * 32 + g
                        bi_ = gg % 2
                        S.dma(ublk[bi_][:], ubf_d[:, :, gg * 256:(gg + 1) * 256], w=[f"ublk{bi_}"])
                        S.dma(vblk[bi_][:], vbf_d[:, gg * 2:gg * 2 + 2, :], w=[f"vblk{bi_}"])
                        pst = 4 + bi_
                        for i4 in range(2):
                            for k in range(8):
                                S.mm(bank(pst)[:, i4 * 256:(i4 + 1) * 256], ublk[bi_][:, k, i4 * 128:(i4 + 1) * 128], h2T[:, k, :],
                                     k == 0, k == 7, r=[f"ublk{bi_}", "h2T"], w=bk(pst))
                    emit_ST(0)
                    for g in range(32):
                        gg = half * 32 + g
                        bi_ = gg % 2
                        pst = 4 + bi_
                        S.op("act", lambda e, bi_=bi_, pst=pst: e.activation(out=gel[bi_][:], in_=bank(pst), func=AF.Gelu),
                             r=bk(pst), w=[f"gel{bi_}"])
                        if g + 1 < 32:
                            emit_ST(g + 1)
                        S.op("dve", lambda e, bi_=bi_, g=g: e.tensor_tensor(
                            out=W[bi_][:], in0=gel[bi_][:], in1=G[:, g * 2:g * 2 + 2, :].rearrange("p a t -> p (a t)"), op=ALU.mult),
                            r=[f"gel{bi_}", "G"], w=[f"W{bi_}"])
                        for i4 in range(2):
                            for sub in range(2):
                                for hf in range(2):
                                    S.mm(bank(sub * 2 + hf), W[bi_][:, i4 * 256 + sub * 128:i4 * 256 + (sub + 1) * 128],
                                         vblk[bi_][:, i4, hf * 512:(hf + 1) * 512],
                                         half == 0 and g == 0 and i4 == 0, half == 1 and g == 31 and i4 == 1,
                                         r=[f"W{bi_}", f"vblk{bi_}"], w=bk(sub * 2 + hf))

                for half in range(2):
                    onehot_G(half)
                    dense(half)
                if debug and pr == 0:
                    S.op("dve", lambda e: e.tensor_copy(out=y[:], in_=bank(0, 2)), r=bk(0, 2), w=["y"])
                    S.dma(dbg["ffn"], y[:], r=["y"])
                for s in (2 * pr, 2 * pr + 1):
                    t0 = s * 128
                    i = 0
                    sub = s % 2
                    x1 = x1s[sub]
                    x1k = f"x1{sub}"
                    S.op("dve", lambda e, x1=x1, sub=sub: e.scalar_tensor_tensor(out=y[:], in0=x1[:], scalar=ALPHA, in1=bank(sub * 2, 2),
                                                                 op0=ALU.mult, op1=ALU.add), r=[x1k] + bk(sub * 2, 2), w=["y"])
                    layer_norm(y, y, "y", "y")
                    S.op("pool", lambda e, i=i: e.tensor_tensor(out=ot[i][:], in0=n1[:], in1=lnb[:, 2, :], op=ALU.mult),
                         r=["y", "lnb"], w=[f"ot{i}"])
                    S.op("pool", lambda e, i=i: e.tensor_tensor(out=ot[i][:], in0=ot[i][:], in1=lnb[:, 3, :], op=ALU.add),
                         r=[f"ot{i}", "lnb"], w=[f"ot{i}"])
                    out_toks.append(S.dma(out_d[t0:t0 + 128, :], ot[i][:], r=[f"ot{i}"], w=["out_d"]))
            S.barrier()
        stg1.close()
        print("bass ops:", S.nops, "sems:", S.nsem)
    return nc


def _host_inputs(inp, b):
    f32 = np.float32
    x = np.asarray(inp["x"][b], f32)
    fm = lambda v, k: np.ascontiguousarray(np.asarray(v, f32).reshape(k, 128).T)
    w_in = np.asarray(inp["w_in"][0], f32)
    w_in_ext = np.concatenate([w_in, w_in[:, 928:960], w_in[:, 896:928]], axis=1)
    wuq = np.asarray(inp["w_uq"][0], f32).reshape(384, NH, 192)
    wuq_ext = np.concatenate([wuq, wuq[:, :, 160:192], wuq[:, :, 128:160]], axis=2).reshape(384, NH * 256)
    inv_freq = (10000.0 ** (-np.arange(0, 64, 2, dtype=np.float32) / 64)).astype(f32)
    invf = np.zeros((64, 2), f32)
    invf[:, 0] = np.concatenate([inv_freq, inv_freq])
    invf[:, 1] = np.concatenate([-np.ones(32, f32), np.ones(32, f32)])
    wins = np.array([2, 4, 8, 16], f32)
    invw = np.zeros((128, 2), f32)
    invc0 = np.zeros((128, 2, 512), f32)
    t = np.arange(512, dtype=f32)
    for g in range(4):
        rows = slice((g % 2) * 64, (g % 2) * 64 + 64)
        invw[rows, g // 2] = 1.0 / wins[g]
        invc0[rows, g // 2, :] = 1.0 / np.minimum(t + 1, wins[g])
    kp = np.arange(128)[:, None]
    qf = np.arange(512)[None, :]
    mask = np.stack([(qf >= i * 128 + kp).astype(f32) for i in range(4)], axis=1).reshape(128, 2048)
    b_ada = np.asarray(inp["b_ada"][0], f32)
    ln_rows = np.stack([inp["ln1_g"][0], inp["ln1_b"][0], inp["ln2_g"][0], inp["ln2_b"][0]]).astype(f32)
    keysT = np.ascontiguousarray(np.asarray(inp["peer_keys"][0], f32).reshape(16, 128, 128).transpose(2, 0, 1)).reshape(128, 2048)
    return {
        "xT": np.ascontiguousarray(x.T), "x": x, "c": fm(inp["c"][b], 8),
        "pos": np.asarray(inp["positions"][b], np.int32).reshape(1, SEQ), "invf": invf,
        "w_ada": np.asarray(inp["w_ada"][0], f32), "b_ada_fm": fm(b_ada, 48), "b_ada_row": b_ada.reshape(1, -1),
        "w_in_ext": np.ascontiguousarray(w_in_ext), "pool_w": np.asarray(inp["pool_w"][0], f32),
        "pool_scale_fm": fm(inp["pool_scale"][0], 2), "invw": invw, "invc0": invc0,
        "q_norm_g_fm": fm(inp["q_norm_g"][0], 3), "kv_norm_g_fm": fm(inp["kv_norm_g"][0], 2),
        "w_uq_ext": np.ascontiguousarray(wuq_ext), "w_ukv": np.asarray(inp["w_ukv"][0], f32),
        "w_out": np.asarray(inp["w_out"][0], f32), "ln_rows": ln_rows,
        "ln1_fm": np.concatenate([fm(inp["ln1_g"][0], 8), fm(inp["ln1_b"][0], 8)], axis=1),
        "w_peer_q": np.asarray(inp["w_peer_q"][0], f32), "keysT": keysT,
        "uT": np.ascontiguousarray(np.asarray(inp["peer_u"][0], f32).T), "v": np.asarray(inp["peer_v"][0], f32),
        "mask": mask, "ident": np.eye(128, dtype=f32),
        "iota_n": np.tile(np.arange(128, dtype=np.int32), (128, 1)),
        "iota_j": np.tile(np.arange(256, dtype=np.int32), (128, 1)),
        "iota16": np.tile(np.arange(16, dtype=f32), (128, 1)),
        "iota128": np.tile(np.arange(128, dtype=f32), (128, 1)),
    }


def kernel(**inputs):
    nc = build(False)
    shared = None
    in_maps = []
    for b in range(8):
        m = _host_inputs(inputs, b)
        if shared is None:
            shared = m
        else:
            for k in m:
                if k not in ("xT", "x", "c", "pos"):
                    m[k] = shared[k]
        in_maps.append(m)
    res = run_bass_kernel_spmd(nc, in_maps, core_ids=list(range(8)))
    return np.stack([np.asarray(r["out"], np.float32) for r in res.results], axis=0)
```

```python
import math
from contextlib import ExitStack
import numpy as np
import concourse.bass as bass
import concourse.mybir as mybir
from concourse.bass_utils import run_bass_kernel_spmd

F32 = mybir.dt.float32
BF16 = mybir.dt.bfloat16
I32 = mybir.dt.int32
AF = mybir.ActivationFunctionType
ALU = mybir.AluOpType
AX = mybir.AxisListType

SEQ = 8192
D = 1024
NH = 6
ALPHA = 2.0 ** 0.25
LN_EPS = 1e-5
RMS_EPS = 1e-6
QSCALE = 192.0 ** -0.5
T1 = 256
TG = 256
TWO_PI = 2.0 * math.pi
C1 = 6.28125
C2 = TWO_PI - C1
NEG = -3.0e38


class Sched:
    EPOCH = 24000
    NDS = 12

    def __init__(self, nc, st):
        self.nc, self.st = nc, st
        self.eng = dict(pe=nc.tensor, act=nc.scalar, dve=nc.vector, pool=nc.gpsimd, sp=nc.sync)
        self.sem, self.cnt, self.nsem = {}, {}, 0
        for e in self.eng:
            self._new_sem(e)
        self.waited = {e: {} for e in self.eng}
        self.lastw, self.readers = {}, {}
        self.dsem = {q: [[st.enter_context(nc.semaphore(f"d{q}{i}")), 0] for i in range(self.NDS)]
                     for q in ("sp", "pool", "act")}
        self.drr = {q: 0 for q in self.dsem}
        self.nops = 0

    def _new_sem(self, e):
        self.nsem += 1
        self.sem[e] = self.st.enter_context(self.nc.semaphore(f"s{e}{self.nsem}"))
        self.cnt[e] = 0

    def _wait(self, e, tok):
        sem, val = tok
        if val <= 0:
            return
        w = self.waited[e]
        if w.get(sem.num, 0) >= val:
            return
        self.eng[e].wait_ge(sem, val)
        w[sem.num] = val

    def _deps(self, e, r, w):
        deps = []
        for k in r:
            if k in self.lastw:
                deps.append(self.lastw[k])
        for k in w:
            if k in self.lastw:
                deps.append(self.lastw[k])
            deps.extend(self.readers.get(k, ()))
        for d in deps:
            if e == "pe" and d[2] == "pe":
                continue
            self._wait(e, (d[0], d[1]))

    def _commit(self, tok, r, w):
        for k in w:
            self.lastw[k] = tok
            self.readers[k] = []
        for k in r:
            self.readers.setdefault(k, []).append(tok)

    def op(self, e, fn, r=(), w=()):
        self._deps(e, r, w)
        if self.cnt[e] >= self.EPOCH:
            self._new_sem(e)
        ins = fn(self.eng[e])
        self.cnt[e] += 1
        ins.then_inc(self.sem[e], 1)
        tok = (self.sem[e], self.cnt[e], e)
        self._commit(tok, r, w)
        self.nops += 1
        return tok

    def dma(self, out, in_, r=(), w=(), q="sp"):
        slot = self.dsem[q][self.drr[q]]
        self.drr[q] = (self.drr[q] + 1) % self.NDS
        self._wait(q, (slot[0], slot[1]))
        self._deps(q, r, w)
        ins = self.eng[q].dma_start(out=out, in_=in_)
        slot[1] += 16
        ins.then_inc(slot[0], 16)
        tok = (slot[0], slot[1], "dma")
        self._commit(tok, r, w)
        return tok

    def snapshot(self, skip=()):
        toks = [(self.sem[e], self.cnt[e]) for e in self.eng if e not in skip]
        for q in self.dsem:
            if q not in skip:
                toks += [(s[0], s[1]) for s in self.dsem[q]]
        return toks

    def barrier_on(self, toks):
        for e in self.eng:
            for t in toks:
                self._wait(e, t)

    def barrier(self):
        toks = [(self.sem[e], self.cnt[e]) for e in self.eng]
        for q in self.dsem:
            toks += [(s[0], s[1]) for s in self.dsem[q]]
        for e in self.eng:
            for t in toks:
                self._wait(e, t)
        self.lastw.clear()
        self.readers.clear()

    def mm(self, out, lhsT, rhs, start, stop, r, w):
        return self.op("pe", lambda e: e.matmul(out, lhsT, rhs, start=start, stop=stop), r, w)


def build(debug=False, nst=64):
    nc = bass.Bass("TRN2", target_bir_lowering=False)

    def din(name, shape, dt=F32):
        return nc.dram_tensor(name, list(shape), dt, kind="ExternalInput").ap()

    scr_kind = "ExternalOutput" if debug else "Internal"

    def dscr(name, shape, dt):
        return nc.dram_tensor(name, list(shape), dt, kind=scr_kind).ap()

    xT_d = din("xT", [D, SEQ])
    x_d = din("x", [SEQ, D])
    c_d = din("c", [128, 8])
    pos_d = din("pos", [1, SEQ], I32)
    invf_d = din("invf", [64, 2])
    wada_d = din("w_ada", [D, 6 * D])
    bada_fm_d = din("b_ada_fm", [128, 48])
    bada_row_d = din("b_ada_row", [1, 6 * D])
    win_d = din("w_in_ext", [D, 1024])
    poolw_d = din("pool_w", [4, 64, 64])
    pools_d = din("pool_scale_fm", [128, 2])
    invw_d = din("invw", [128, 2])
    invc0_d = din("invc0", [128, 2, 512])
    qg_d = din("q_norm_g_fm", [128, 3])
    kvg_d = din("kv_norm_g_fm", [128, 2])
    wuq_d = din("w_uq_ext", [384, NH * 256])
    wukv_d = din("w_ukv", [256, NH * 256])
    wout_d = din("w_out", [D, D])
    ln_rows_d = din("ln_rows", [4, D])
    ln1_fm_d = din("ln1_fm", [128, 16])
    wpq_d = din("w_peer_q", [D, 2048])
    keysT_d = din("keysT", [128, 16 * 128])
    uT_d = din("uT", [D, 16384])
    v_d = din("v", [16384, D])
    mask_d = din("mask", [128, 4 * 512])
    ident_d = din("ident", [128, 128])
    iotan_d = din("iota_n", [128, 128], I32)
    iotaj_d = din("iota_j", [128, 256], I32)
    iota16_d = din("iota16", [128, 16])
    iota128_d = din("iota128", [128, 128])
    out_d = nc.dram_tensor("out", [SEQ, D], F32, kind="ExternalOutput").ap()

    cos_d = dscr("cos_scr", [64, SEQ], F32)
    sin_d = dscr("sin_scr", [64, SEQ], F32)
    apT_d = dscr("apT_scr", [D, SEQ], BF16)
    ubf_d = dscr("ubf_scr", [128, 8, 16384], BF16)
    vbf_d = dscr("vbf_scr", [128, 128, D], BF16)
    dbg = {}
    if debug:
        dbg["modT"] = nc.dram_tensor("dbg_modT", [128, 48], F32, kind="ExternalOutput").ap()
        dbg["lat"] = nc.dram_tensor("dbg_lat", [128, 6, SEQ], BF16, kind="ExternalOutput").ap()
        dbg["sel"] = nc.dram_tensor("dbg_sel", [128, 3, 128], F32, kind="ExternalOutput").ap()
        dbg["x1"] = nc.dram_tensor("dbg_x1", [128, D], F32, kind="ExternalOutput").ap()
        dbg["ffn"] = nc.dram_tensor("dbg_ffn", [128, D], F32, kind="ExternalOutput").ap()

    with ExitStack() as st:
        S = Sched(nc, st)

        def sb(name, shape, dt=F32, stack=st):
            return stack.enter_context(nc.sbuf_tensor("sb_" + name, list(shape), dt))

        psum = st.enter_context(nc.psum_tensor("psum", [128, 4096], F32))

        def bank(i, n=1):
            return psum[:, i * 512:(i + n) * 512]

        def bk(i, n=1):
            return [f"ps{j}" for j in range(i, i + n)]

        modT = sb("modT", [128, 48])
        sc1p = sb("sc1p", [128, 8])
        A2 = sb("A2", [128, 8])
        B2 = sb("B2", [128, 8])
        ident = sb("ident", [128, 128])
        ones_bf = sb("ones_bf", [128, 128], BF16)
        ones_f = sb("ones_f", [128, 128])
        invf = sb("invf", [64, 2])
        st12 = ExitStack()
        stg1 = ExitStack()
        stg2 = ExitStack()
        stw1 = ExitStack()
        g1_bc = sb("g1_bc", [128, D], stack=stg1)
        w_uq_bf = sb("w_uq_bf", [128, 3, NH * 256], BF16, stack=st12)
        w_ukv_bf = sb("w_ukv_bf", [128, 2, NH * 256], BF16, stack=st12)
        wbd = sb("wbd", [128, 2, 128], BF16, stack=st12)
        pools = sb("pools", [128, 2], stack=st12)
        invw = sb("invw", [128, 2], stack=st12)
        maskb = sb("maskb", [128, 4, 512], BF16, stack=st12)
        cqn = sb("cqn", [128, 3, SEQ], BF16, stack=st12)
        ckvn = sb("ckvn", [128, 2, SEQ], BF16, stack=st12)
        krT = sb("krT", [128, SEQ], BF16, stack=st12)
        g2_bc = sb("g2_bc", [128, D], stack=st12)
        cst = [sb(f"cst{i}", [128, 512], stack=st12) for i in range(2)]
        cbf = [sb(f"cbf{i}", [128, 512], BF16, stack=st12) for i in range(2)]
        w_in_bf = sb("w_in_bf", [128, 8, 1024], BF16, stack=stw1)
        invc0 = sb("invc0", [128, 2, T1], stack=stw1)
        S.dma(ident[:], ident_d, w=["ident"])
        S.dma(invf[:], invf_d, w=["invf"])
        S.op("dve", lambda e: e.memset(ones_f[:], 1.0), w=["ones_f"])
        S.op("dve", lambda e: e.memset(ones_bf[:], 1.0), w=["ones_bf"])
        S.op("pool", lambda e: e.memset(krT[64:128, :], 0.0), w=["krT"])

        with ExitStack() as s0:
            c_sb = sb("c_sb", [128, 8], stack=s0)
            c_act = sb("c_act", [128, 8], stack=s0)
            c_act2 = sb("c_act2", [128, 8, 2], stack=s0)
            c_bc = sb("c_bc", [128, 8, 128], stack=s0)
            bfm = sb("bfm", [128, 48], stack=s0)
            ln1fm = sb("ln1fm", [128, 16], stack=s0)
            wa = [sb(f"wa{i}", [128, 8, 256], stack=s0) for i in range(2)]
            S.dma(c_sb[:], c_d, w=["c_sb"])
            S.dma(bfm[:], bada_fm_d, w=["bfm"])
            S.dma(ln1fm[:], ln1_fm_d, w=["ln1fm"])
            S.dma(g1_bc[:], bada_row_d[0:1, 2048:3072].to_broadcast([128, D]), w=["g1_bc"])
            S.dma(g2_bc[:], bada_row_d[0:1, 5120:6144].to_broadcast([128, D]), w=["g2_bc"])
            S.op("act", lambda e: e.activation(out=c_act[:], in_=c_sb[:], func=AF.Silu), r=["c_sb"], w=["c_act"])
            S.op("dve", lambda e: e.tensor_copy(out=c_act2[:], in_=c_act[:].unsqueeze(2).to_broadcast([128, 8, 2])),
                 r=["c_act"], w=["c_act2"])
            S.op("dve", lambda e: e.tensor_copy(out=c_bc[:], in_=c_act[:].unsqueeze(2).to_broadcast([128, 8, 128])),
                 r=["c_act"], w=["c_bc"])
            wada_v = wada_d.rearrange("(k p) c -> p k c", p=128)
            psm = bank(7)[:, 0:96].rearrange("p (j two) -> p j two", two=2)
            for n in range(24):
                wt = wa[n % 2]
                S.dma(wt[:], wada_v[:, :, n * 256:(n + 1) * 256], w=[f"wa{n % 2}"])
                if n in (8, 9, 10, 11, 20, 21, 22, 23):
                    gt = g1_bc if n < 12 else g2_bc
                    gk = "g1_bc" if n < 12 else "g2_bc"
                    off = (n % 4) * 256
                    for k in range(8):
                        S.mm(bank(n % 2)[:, 0:256], c_bc[:, k, :], wt[:, k, :], k == 0, k == 7,
                             r=["c_bc", f"wa{n % 2}"], w=bk(n % 2))
                    S.op("dve", lambda e, gt=gt, off=off, n=n: e.tensor_tensor(
                        out=gt[:, off:off + 256], in0=bank(n % 2)[:, 0:256], in1=gt[:, off:off + 256], op=ALU.add),
                        r=bk(n % 2) + [gk], w=[gk])
                else:
                    for cb in range(2):
                        j = n * 2 + cb
                        for k in range(8):
                            S.mm(psm[:, j, :], wt[:, k, cb * 128:(cb + 1) * 128], c_act2[:, k, :], k == 0, k == 7,
                                 r=["c_act2", f"wa{n % 2}"], w=bk(7))
            S.op("dve", lambda e: e.tensor_tensor(out=modT[:], in0=psm[:, :, 0], in1=bfm[:], op=ALU.add),
                 r=bk(7) + ["bfm"], w=["modT"])
            S.op("dve", lambda e: e.tensor_scalar_add(out=sc1p[:], in0=modT[:, 8:16], scalar1=1.0), r=["modT"], w=["sc1p"])
            S.op("dve", lambda e: e.scalar_tensor_tensor(out=A2[:], in0=modT[:, 32:40], scalar=1.0, in1=ln1fm[:, 0:8],
                                                         op0=ALU.add, op1=ALU.mult), r=["modT", "ln1fm"], w=["A2"])
            S.op("dve", lambda e: e.scalar_tensor_tensor(out=B2[:], in0=modT[:, 32:40], scalar=1.0, in1=ln1fm[:, 8:16],
                                                         op0=ALU.add, op1=ALU.mult), r=["modT", "ln1fm"], w=["B2"])
            S.op("dve", lambda e: e.tensor_tensor(out=B2[:], in0=B2[:], in1=modT[:, 24:32], op=ALU.add),
                 r=["B2", "modT"], w=["B2"])
            if debug:
                S.dma(dbg["modT"], modT[:], r=["modT"])

            pos_i = sb("pos_i", [64, 512], I32, stack=s0)
            ang = sb("ang", [64, 512], stack=s0)
            ang2 = sb("ang2", [64, 512], stack=s0)
            ki = sb("ki", [64, 512], I32, stack=s0)
            kf = sb("kf", [64, 512], stack=s0)
            rr = sb("rr", [64, 512], stack=s0)
            tb = [sb(f"tb{i}", [64, 512], stack=s0) for i in range(2)]
            for ch in range(16):
                sl = slice(ch * 512, (ch + 1) * 512)
                S.dma(pos_i[:], pos_d[0:1, sl].to_broadcast([64, 512]), w=["pos_i"])
                S.op("dve", lambda e: e.tensor_copy(out=ang[:], in_=pos_i[:]), r=["pos_i"], w=["ang"])
                S.op("dve", lambda e: e.tensor_scalar(out=ang[:], in0=ang[:], scalar1=invf[:, 0:1], scalar2=None,
                                                      op0=ALU.mult), r=["ang", "invf"], w=["ang"])
                for which in range(2):
                    src = ang
                    if which == 0:
                        S.op("dve", lambda e: e.tensor_scalar_add(out=ang2[:], in0=ang[:], scalar1=math.pi / 2),
                             r=["ang"], w=["ang2"])
                        src = ang2
                    sn = "ang2" if which == 0 else "ang"
                    S.op("dve", lambda e, src=src: e.tensor_scalar(out=ki[:], in0=src[:], scalar1=1.0 / TWO_PI,
                                                                   scalar2=None, op0=ALU.mult), r=[sn], w=["ki"])
                    S.op("dve", lambda e: e.tensor_copy(out=kf[:], in_=ki[:]), r=["ki"], w=["kf"])
                    S.op("dve", lambda e, src=src: e.scalar_tensor_tensor(out=rr[:], in0=kf[:], scalar=-C1, in1=src[:],
                                                                          op0=ALU.mult, op1=ALU.add),
                         r=["kf", sn], w=["rr"])
                    S.op("dve", lambda e: e.scalar_tensor_tensor(out=rr[:], in0=kf[:], scalar=-C2, in1=rr[:],
                                                                 op0=ALU.mult, op1=ALU.add), r=["kf", "rr"], w=["rr"])
                    S.op("dve", lambda e: e.tensor_scalar(out=rr[:], in0=rr[:], scalar1=-3.14159, scalar2=3.14159,
                                                          op0=ALU.max, op1=ALU.min), r=["rr"], w=["rr"])
                    if which == 0:
                        S.op("act", lambda e: e.activation(out=tb[0][:], in_=rr[:], func=AF.Sin), r=["rr"], w=["tb0"])
                        S.dma(cos_d[:, sl], tb[0][:], r=["tb0"], w=["cos_d"])
                    else:
                        S.op("act", lambda e: e.activation(out=tb[1][:], in_=rr[:], func=AF.Sin, scale=invf[:, 1:2]),
                             r=["rr", "invf"], w=["tb1"])
                        S.dma(sin_d[:, sl], tb[1][:], r=["tb1"], w=["sin_d"])
        S.barrier()

        with ExitStack() as sp_:
            stg = [sb(f"stg{i}", [128, 2048], stack=sp_) for i in range(2)]
            qg = sb("qg", [128, 3], stack=sp_)
            kvg = sb("kvg", [128, 2], stack=sp_)
            wbdf = sb("wbdf", [128, 2, 128], stack=sp_)
            S.dma(qg[:], qg_d, w=["qg"])
            S.dma(kvg[:], kvg_d, w=["kvg"])
            S.dma(pools[:], pools_d, w=["pools"])
            S.dma(invw[:], invw_d, w=["invw"])
            S.dma(invc0[:], invc0_d[:, :, 0:T1], w=["invc0"])
            nstg = [0]

            def staged(src_ap, width, fn, rk=(), wk=()):
                i = nstg[0] % 2
                nstg[0] += 1
                S.dma(stg[i][:, 0:width], src_ap, w=[f"stg{i}"])
                S.op("dve" if i == 0 else "pool", lambda e: fn(e, stg[i][:, 0:width]), r=[f"stg{i}"] + list(rk), w=list(wk))

            win_v = win_d.rearrange("(k p) c -> p k c", p=128)
            for k in range(8):
                staged(win_v[:, k, :], 1024, lambda e, s, k=k: e.tensor_copy(out=w_in_bf[:, k, :], in_=s), wk=["w_in_bf"])
            wuq_v = wuq_d.rearrange("(k p) c -> p k c", p=128)
            for k in range(3):
                S.dma(stg[0][:, 0:1536], wuq_v[:, k, :], w=["stg0"])
                S.op("dve", lambda e, k=k: e.tensor_scalar(out=w_uq_bf[:, k, :], in0=stg[0][:, 0:1536], scalar1=qg[:, k:k + 1],
                                                           scalar2=QSCALE, op0=ALU.mult, op1=ALU.mult),
                     r=["stg0", "qg"], w=["w_uq_bf"])
            wukv_v = wukv_d.rearrange("(k p) c -> p k c", p=128)
            for k in range(2):
                S.dma(stg[1][:, 0:1536], wukv_v[:, k, :], w=["stg1"])
                S.op("dve", lambda e, k=k: e.tensor_scalar(out=w_ukv_bf[:, k, :], in0=stg[1][:, 0:1536],
                                                           scalar1=kvg[:, k:k + 1], scalar2=None, op0=ALU.mult),
                     r=["stg1", "kvg"], w=["w_ukv_bf"])
            staged(mask_d, 2048, lambda e, s: e.tensor_copy(out=maskb[:].rearrange("p a b -> p (a b)"), in_=s), wk=["maskb"])
            S.op("dve", lambda e: e.memset(wbdf[:], 0.0), w=["wbdf"])
            for g in range(4):
                S.dma(wbdf[(g % 2) * 64:(g % 2) * 64 + 64, g // 2, (g % 2) * 64:(g % 2) * 64 + 64], poolw_d[g],
                      w=["wbdf"])
            S.op("dve", lambda e: e.tensor_copy(out=wbd[:], in_=wbdf[:]), r=["wbdf"], w=["wbd"])

            snap = S.snapshot()
            n = 0
            for k in range(8):
                for ec in range(32):
                    i = n % 2
                    n += 1
                    S.dma(cst[i][:], uT_d[k * 128:(k + 1) * 128, ec * 512:(ec + 1) * 512], w=[f"cst{i}"], q="pool")
                    S.op("pool", lambda e, i=i: e.tensor_copy(out=cbf[i][:], in_=cst[i][:]), r=[f"cst{i}"], w=[f"cbf{i}"])
                    S.dma(ubf_d[:, k, ec * 512:(ec + 1) * 512], cbf[i][:], r=[f"cbf{i}"], w=["ubf_d"], q="pool")
            v_v = v_d.rearrange("(a p) f -> p a f", p=128)
            for a2 in range(128):
                for hf in range(2):
                    i = n % 2
                    n += 1
                    fs = slice(hf * 512, (hf + 1) * 512)
                    S.dma(cst[i][:], v_v[:, a2, fs], w=[f"cst{i}"], q="pool")
                    S.op("pool", lambda e, i=i, fs=fs: e.tensor_tensor(out=cbf[i][:], in0=cst[i][:], in1=g2_bc[:, fs], op=ALU.mult),
                         r=[f"cst{i}", "g2_bc"], w=[f"cbf{i}"])
                    S.dma(vbf_d[:, a2, fs], cbf[i][:], r=[f"cbf{i}"], w=["vbf_d"], q="pool")
            S.barrier_on(snap)

        with ExitStack() as s1:
            xTt = [sb(f"xTt{i}", [128, 8, T1], stack=s1) for i in range(2)]
            hT = [sb(f"hT{i}", [128, 8, T1], BF16, stack=s1) for i in range(2)]
            pbuf = [sb(f"pbuf{i}", [128, 2, T1 + 16], stack=s1) for i in range(2)]
            t2 = sb("t2", [128, 2, T1 + 16], stack=s1)
            t4 = sb("t4", [128, 2, T1 + 16], stack=s1)
            t8 = sb("t8", [128, T1 + 16], stack=s1)
            t16 = sb("t16", [128, T1 + 16], stack=s1)
            tmpf = sb("tmpf", [128, 2, T1], stack=s1)
            mixed = sb("mixed", [128, 2, T1], BF16, stack=s1)
            yT = [sb(f"yT{i}", [128, T1], BF16, stack=s1) for i in range(2)]
            sqb = [sb(f"sqb{i}", [128, T1], BF16, stack=s1) for i in range(2)]
            sqr = sb("sqr", [128, T1], stack=s1)
            rbc = sb("rbc", [128, T1], stack=s1)
            cst_ = [sb(f"cs{i}", [64, 2, T1], stack=s1) for i in range(2)]
            r1 = sb("r1", [64, T1], stack=s1)
            r2 = sb("r2", [64, T1], stack=s1)
            xT_v = xT_d.rearrange("(k p) t -> p k t", p=128)
            S.op("dve", lambda e: e.memset(pbuf[0][:, :, 0:16], 0.0), w=["pbuf0"])
            for tt in range(SEQ // T1):
                i = tt % 2
                ts = slice(tt * T1, (tt + 1) * T1)
                S.dma(xTt[i][:], xT_v[:, :, ts], w=[f"xTt{i}"])
                S.dma(cst_[i][:, 0, :], cos_d[:, ts], w=[f"cs{i}"])
                S.dma(cst_[i][:, 1, :], sin_d[:, ts], w=[f"cs{i}"])
                for k in range(8):
                    if k % 2 == 0:
                        S.op("act", lambda e, k=k: e.activation(out=hT[i][:, k, :], in_=xTt[i][:, k, :], func=AF.Identity,
                                                                scale=sc1p[:, k:k + 1], bias=modT[:, k:k + 1]),
                             r=[f"xTt{i}", "sc1p", "modT"], w=[f"hT{i}"])
                    else:
                        S.op("dve", lambda e, k=k: e.tensor_scalar(out=hT[i][:, k, :], in0=xTt[i][:, k, :],
                                                                   scalar1=sc1p[:, k:k + 1], scalar2=modT[:, k:k + 1],
                                                                   op0=ALU.mult, op1=ALU.add),
                             r=[f"xTt{i}", "sc1p", "modT"], w=[f"hT{i}"])

                def zmm(pb, col0, m, i=i):
                    for k in range(8):
                        S.mm(bank(pb)[0:m, 0:T1], w_in_bf[:, k, col0:col0 + m], hT[i][:, k, :], k == 0, k == 7,
                             r=[f"hT{i}", "w_in_bf"], w=bk(pb))
                pb_ = pbuf[i]
                for oc in range(2):
                    zmm(7, oc * 128, 128)
                    S.op("act", lambda e, oc=oc: e.copy(out=pb_[:, oc, 16:T1 + 16], in_=bank(7)[:, 0:T1]), r=bk(7), w=[f"pbuf{i}"])
                S.op("dve", lambda e: e.tensor_tensor(out=t2[:, :, 1:T1 + 16], in0=pb_[:, :, 1:T1 + 16], in1=pb_[:, :, 0:T1 + 15], op=ALU.add),
                     r=[f"pbuf{i}"], w=["t2"])
                S.op("dve", lambda e: e.tensor_tensor(out=t4[:, :, 3:T1 + 16], in0=t2[:, :, 3:T1 + 16], in1=t2[:, :, 1:T1 + 14], op=ALU.add),
                     r=["t2"], w=["t4"])
                S.op("dve", lambda e: e.tensor_tensor(out=t8[:, 7:T1 + 16], in0=t4[:, 1, 7:T1 + 16], in1=t4[:, 1, 3:T1 + 12], op=ALU.add),
                     r=["t4"], w=["t8"])
                S.op("dve", lambda e: e.tensor_tensor(out=t16[:, 15:T1 + 16], in0=t8[:, 15:T1 + 16], in1=t8[:, 7:T1 + 8], op=ALU.add),
                     r=["t8"], w=["t16"])
                srcs = [(t2, 0, 0), (t4, 0, 64), (None, 1, 0), (None, 1, 64)]
                for (tsrc, ch, p0) in srcs:
                    if tsrc is None:
                        win_ap = (t8 if p0 == 0 else t16)[p0:p0 + 64, 16:T1 + 16]
                    else:
                        win_ap = tsrc[p0:p0 + 64, ch, 16:T1 + 16]
                    if tt == 0:
                        S.op("dve", lambda e, win_ap=win_ap, ch=ch, p0=p0: e.tensor_tensor(
                            out=tmpf[p0:p0 + 64, ch, :], in0=win_ap, in1=invc0[p0:p0 + 64, ch, :], op=ALU.mult),
                            r=["t2", "t4", "t8", "t16", "invc0"], w=["tmpf"])
                        S.op("dve", lambda e, ch=ch, p0=p0: e.tensor_tensor(
                            out=mixed[p0:p0 + 64, ch, :], in0=tmpf[p0:p0 + 64, ch, :], in1=pb_[p0:p0 + 64, ch, 16:T1 + 16],
                            op=ALU.subtract), r=["tmpf", f"pbuf{i}"], w=["mixed"])
                    else:
                        S.op("dve", lambda e, win_ap=win_ap, ch=ch, p0=p0: e.scalar_tensor_tensor(
                            out=mixed[p0:p0 + 64, ch, :], in0=win_ap, scalar=invw[p0:p0 + 64, ch:ch + 1],
                            in1=pb_[p0:p0 + 64, ch, 16:T1 + 16], op0=ALU.mult, op1=ALU.subtract),
                            r=["t2", "t4", "t8", "t16", "invw", f"pbuf{i}"], w=["mixed"])
                S.op("dve", lambda e: e.tensor_copy(out=pbuf[1 - i][:, :, 0:16], in_=pb_[:, :, T1:T1 + 16]),
                     r=[f"pbuf{i}"], w=[f"pbuf{1 - i}"])
                for ch in range(2):
                    S.mm(bank(7)[:, 0:T1], wbd[:, ch, :], mixed[:, ch, :], True, True, r=["wbd", "mixed"], w=bk(7))
                    S.op("act", lambda e, ch=ch: e.activation(out=yT[ch][:], in_=bank(7)[:, 0:T1], func=AF.Identity,
                                                              scale=pools[:, ch:ch + 1]), r=bk(7) + ["pools"], w=[f"yT{ch}"])
                    S.dma(apT_d[ch * 128:(ch + 1) * 128, ts], yT[ch][:], r=[f"yT{ch}"], w=["apT_d"])
                for (name, dst, nchunk, col0, pb0, pss, nfeat) in (("q", cqn, 3, 256, 0, 3, 384.0),
                                                                    ("kv", ckvn, 2, 640, 4, 6, 256.0)):
                    for j in range(nchunk):
                        zmm(pb0 + j, col0 + j * 128, 128)
                        S.op("act", lambda e, j=j, pb0=pb0: e.activation(out=sqb[j % 2][:], in_=bank(pb0 + j)[:, 0:T1], func=AF.Square),
                             r=bk(pb0 + j), w=[f"sqb{j % 2}"])
                        S.mm(bank(pss)[:, 0:T1], ones_bf[:], sqb[j % 2][:], j == 0, j == nchunk - 1, r=["ones_bf", f"sqb{j % 2}"],
                             w=bk(pss))
                    S.op("dve", lambda e, pss=pss, nfeat=nfeat: e.tensor_scalar(out=sqr[:], in0=bank(pss)[:, 0:T1], scalar1=1.0 / nfeat,
                                                                               scalar2=RMS_EPS, op0=ALU.mult, op1=ALU.add),
                         r=bk(pss), w=["sqr"])
                    S.op("act", lambda e: e.activation(out=sqr[:], in_=sqr[:], func=AF.Sqrt), r=["sqr"], w=["sqr"])
                    S.op("dve", lambda e: e.reciprocal(out=rbc[:], in_=sqr[:]), r=["sqr"], w=["rbc"])
                    for j in range(nchunk):
                        S.op("dve", lambda e, j=j, dst=dst, pb0=pb0: e.tensor_tensor(out=dst[:, j, ts], in0=bank(pb0 + j)[:, 0:T1],
                                                                                     in1=rbc[:], op=ALU.mult),
                             r=bk(pb0 + j) + ["rbc"], w=[name + "n"])
                zmm(0, 896, 64)
                zmm(1, 960, 64)
                S.op("dve", lambda e: e.tensor_tensor(out=r1[:], in0=bank(0)[0:64, 0:T1], in1=cst_[i][:, 0, :], op=ALU.mult),
                     r=bk(0) + [f"cs{i}"], w=["r1"])
                S.op("dve", lambda e: e.tensor_tensor(out=r2[:], in0=bank(1)[0:64, 0:T1], in1=cst_[i][:, 1, :], op=ALU.mult),
                     r=bk(1) + [f"cs{i}"], w=["r2"])
                S.op("dve", lambda e: e.tensor_tensor(out=krT[0:64, ts], in0=r1[:], in1=r2[:], op=ALU.add),
                     r=["r1", "r2"], w=["krT"])
            if debug:
                S.dma(dbg["lat"][:, 0:3, :], cqn[:], r=["qn"])
                S.dma(dbg["lat"][:, 3:5, :], ckvn[:], r=["kvn"])
                S.dma(dbg["lat"][0:64, 5, :], krT[0:64, :], r=["krT"])
        S.barrier_on(S.snapshot(skip=("pool",)))
        stw1.close()

        with ExitStack() as s2:
            qnT2 = [sb(f"qnT{i}", [128, 512], BF16, stack=s2) for i in range(2)]
            qrT2 = [sb(f"qrT{i}", [128, 512], BF16, stack=s2) for i in range(2)]
            for i_ in range(2):
                S.op("dve", lambda e, i_=i_: e.memset(qrT2[i_][64:128, :], 0.0), w=[f"qrT{i_}"])
            knT = sb("knT", [128, SEQ], BF16, stack=s2)
            V = sb("V", [128, 64, 128], BF16, stack=s2)
            cs2 = [sb(f"cs2{i}", [64, 2, TG], stack=s2) for i in range(2)]
            r1 = sb("r1b", [64, TG], stack=s2)
            r2 = sb("r2b", [64, TG], stack=s2)
            pT = [sb(f"pT{i}", [128, 512], BF16, stack=s2) for i in range(3)]
            acc = [[sb(f"acc{i}{p}", [128, 512], stack=s2) for p in range(2)] for i in range(2)]
            rl = sb("rl", [128, 512], stack=s2)
            att = [sb(f"att{i}", [128, 512], BF16, stack=s2) for i in range(2)]
            for h in range(NH):
                wq0 = h * 256
                for tt in range(SEQ // TG):
                    ts = slice(tt * TG, (tt + 1) * TG)
                    bkn = tt % 3
                    bv = 3 + tt % 3
                    for k in range(2):
                        S.mm(bank(bkn)[:, 0:TG], w_ukv_bf[:, k, wq0:wq0 + 128], ckvn[:, k, ts], k == 0, k == 1, r=["w_ukv_bf", "kvn"], w=bk(bkn))
                    S.op("act", lambda e, ts=ts, bkn=bkn: e.copy(out=knT[:, ts], in_=bank(bkn)[:, 0:TG]), r=bk(bkn), w=["knT"])
                    for sub in range(TG // 128):
                        tsub = slice(tt * TG + sub * 128, tt * TG + sub * 128 + 128)
                        for k in range(2):
                            S.mm(bank(bv)[:, sub * 128:(sub + 1) * 128], ckvn[:, k, tsub], w_ukv_bf[:, k, wq0 + 128:wq0 + 256],
                                 k == 0, k == 1, r=["w_ukv_bf", "kvn"], w=bk(bv))
                    S.op("dve", lambda e, tt=tt, bv=bv: e.tensor_copy(out=V[:, tt * (TG // 128):(tt + 1) * (TG // 128), :],
                                                                      in_=bank(bv)[:, 0:TG].rearrange("p (a b) -> p a b", b=128)),
                         r=bk(bv), w=["V"])
                def qgen(j):
                    qnT = qnT2[j % 2]
                    qrT = qrT2[j % 2]
                    qnk, qrk = f"qnT{j % 2}", f"qrT{j % 2}"
                    for sub in range(512 // TG):
                        i = (j * (512 // TG) + sub) % 2
                        ts = slice(j * 512 + sub * TG, j * 512 + (sub + 1) * TG)
                        so = slice(sub * TG, (sub + 1) * TG)
                        S.dma(cs2[i][:, 0, :], cos_d[:, ts], w=[f"cs2{i}"])
                        S.dma(cs2[i][:, 1, :], sin_d[:, ts], w=[f"cs2{i}"])
                        for k in range(3):
                            S.mm(bank(6)[:, 0:TG], w_uq_bf[:, k, wq0:wq0 + 128], cqn[:, k, ts], k == 0, k == 2, r=["w_uq_bf", "qn"], w=bk(6))
                        S.op("act", lambda e, so=so, qnT=qnT: e.copy(out=qnT[:, so], in_=bank(6)[:, 0:TG]), r=bk(6), w=[qnk])
                        for (pb, c0) in ((6, 128), (7, 192)):
                            for k in range(3):
                                S.mm(bank(pb)[0:64, 0:TG], w_uq_bf[:, k, wq0 + c0:wq0 + c0 + 64], cqn[:, k, ts], k == 0, k == 2,
                                     r=["w_uq_bf", "qn"], w=bk(pb))
                        S.op("dve", lambda e, i=i: e.tensor_tensor(out=r1[:], in0=bank(6)[0:64, 0:TG], in1=cs2[i][:, 0, :], op=ALU.mult),
                             r=bk(6) + [f"cs2{i}"], w=["r1"])
                        S.op("dve", lambda e, i=i: e.tensor_tensor(out=r2[:], in0=bank(7)[0:64, 0:TG], in1=cs2[i][:, 1, :], op=ALU.mult),
                             r=bk(7) + [f"cs2{i}"], w=["r2"])
                        S.op("dve", lambda e, so=so, qrT=qrT: e.tensor_tensor(out=qrT[0:64, so], in0=r1[:], in1=r2[:], op=ALU.add),
                             r=["r1", "r2"], w=[qrk])

                tiles = [(j, kt) for j in range(16) for kt in range(4 * j + 4)]

                def emit_S(n):
                    j, kt = tiles[n]
                    sbk = n % 3
                    ks = slice(kt * 128, (kt + 1) * 128)
                    S.mm(bank(sbk), knT[:, ks], qnT2[j % 2][:], True, False, r=["knT", f"qnT{j % 2}"], w=bk(sbk))
                    S.mm(bank(sbk), krT[:, ks], qrT2[j % 2][:], False, True, r=["krT", f"qrT{j % 2}"], w=bk(sbk))

                def epilogue(j):
                    ob = 3 + (j % 2)
                    for p_ in range(2):
                        S.mm(bank(5), ones_f[:], acc[j % 2][p_][:], p_ == 0, p_ == 1, r=["ones_f", f"acc{j % 2}{p_}"], w=bk(5))
                    S.op("dve", lambda e: e.reciprocal(out=rl[:], in_=bank(5)), r=bk(5), w=["rl"])
                    S.op("dve", lambda e, ob=ob, j=j: e.tensor_tensor(out=att[j % 2][:], in0=bank(ob), in1=rl[:], op=ALU.mult),
                         r=bk(ob) + ["rl"], w=[f"att{j % 2}"])
                    S.dma(apT_d[256 + h * 128:256 + (h + 1) * 128, j * 512:(j + 1) * 512], att[j % 2][:], r=[f"att{j % 2}"], w=["apT_d"])

                qgen(0)
                qgen(1)
                emit_S(0)
                emit_S(1)
                pend = None
                for n, (j, kt) in enumerate(tiles):
                    nk = 4 * j + 4
                    sbk = n % 3
                    ob = 3 + (j % 2)
                    ac = acc[j % 2][kt % 2]
                    ack = f"acc{j % 2}{kt % 2}"
                    S.op("act", lambda e, sbk=sbk: e.activation(out=pT[sbk][:], in_=bank(sbk), func=AF.Exp),
                         r=bk(sbk), w=[f"pT{sbk}"])
                    if kt >= 4 * j:
                        S.op("dve", lambda e, sbk=sbk, m=kt - 4 * j: e.tensor_tensor(
                            out=pT[sbk][:], in0=pT[sbk][:], in1=maskb[:, m, :], op=ALU.mult),
                            r=[f"pT{sbk}", "maskb"], w=[f"pT{sbk}"])
                    S.mm(bank(ob), V[:, kt, :], pT[sbk][:], kt == 0, kt == nk - 1, r=["V", f"pT{sbk}"], w=bk(ob))
                    if n + 2 < len(tiles):
                        emit_S(n + 2)
                    if kt < 2:
                        S.op("dve", lambda e, sbk=sbk, ac=ac: e.tensor_copy(out=ac[:], in_=pT[sbk][:]),
                             r=[f"pT{sbk}"], w=[ack])
                    else:
                        S.op("dve", lambda e, sbk=sbk, ac=ac: e.tensor_tensor(out=ac[:], in0=ac[:], in1=pT[sbk][:], op=ALU.add),
                             r=[f"pT{sbk}", ack], w=[ack])
                    if pend is not None and kt == 1:
                        epilogue(pend)
                        pend = None
                    if kt == 2 and j + 1 < 16 and j >= 1:
                        qgen(j + 1)
                    if kt == nk - 1:
                        pend = j
                epilogue(pend)
        S.barrier()
        st12.close()

        with ExitStack() as s3:
            wq_bf = sb("wq_bf", [128, 8, 2048], BF16, stack=s3)
            w_out_bf = sb("w_out_bf", [128, 8, D], BF16, stack=s3)
            keysT = sb("keysT", [128, 16, 128], BF16, stack=s3)
            lnb = sb("lnb", [128, 4, D], stack=s3)
            iota_n = sb("iota_n", [128, 128], I32, stack=s3)
            iota_j = sb("iota_j", [128, 256], I32, stack=s3)
            iota16 = sb("iota16", [128, 16], stack=s3)
            iota128 = sb("iota128", [128, 128], stack=s3)
            S.dma(iota_n[:], iotan_d, w=["iota_n"])
            S.dma(iota_j[:], iotaj_d, w=["iota_j"])
            S.dma(iota16[:], iota16_d, w=["iota16"])
            S.dma(iota128[:], iota128_d, w=["iota128"])
            for r_ in range(4):
                S.dma(lnb[:, r_, :], ln_rows_d[r_:r_ + 1, :].to_broadcast([128, D]), w=["lnb"])
            with ExitStack() as sp3:
                stg = [sb(f"stg3{i}", [128, 2048], stack=sp3) for i in range(2)]
                wpq_v = wpq_d.rearrange("(k p) c -> p k c", p=128)
                for k in range(8):
                    S.dma(stg[k % 2][:], wpq_v[:, k, :], w=[f"stg3{k % 2}"])
                    S.op("dve" if k % 2 == 0 else "pool", lambda e, k=k: e.tensor_copy(out=wq_bf[:, k, :], in_=stg[k % 2][:]),
                         r=[f"stg3{k % 2}"], w=["wq_bf"])
                wout_v = wout_d.rearrange("(k p) c -> p k c", p=128)
                for k in range(8):
                    S.dma(stg[k % 2][:, 0:1024], wout_v[:, k, :], w=[f"stg3{k % 2}"])
                    S.op("dve", lambda e, k=k: e.tensor_tensor(out=w_out_bf[:, k, :], in0=stg[k % 2][:, 0:1024], in1=g1_bc[:],
                                                                           op=ALU.mult), r=[f"stg3{k % 2}", "g1_bc"], w=["w_out_bf"])
                S.dma(stg[0][:], keysT_d, w=["stg30"])
                S.op("dve", lambda e: e.tensor_copy(out=keysT[:].rearrange("p a b -> p (a b)"), in_=stg[0][:]),
                     r=["stg30"], w=["keysT"])
                S.barrier()

            apT4 = [sb(f"apT4{i}", [128, 8, 128], BF16, stack=s3) for i in range(1)]
            xt = [sb(f"xt{i}", [128, D], stack=s3) for i in range(1)]
            y = sb("y", [128, D], stack=s3)
            n1 = y
            x1s = [sb(f"x1{i}", [128, D], stack=s3) for i in range(2)]
            ot = [sb(f"ot{i}", [128, D], stack=s3) for i in range(1)]
            stats = sb("stats", [128, 2, 6], stack=s3)
            mv = sb("mv", [128, 2], stack=s3)
            rstd = sb("rstd", [128, 1], stack=s3)
            h2T = sb("h2T", [128, 8, 256], BF16, stack=s3)
            qT = sb("qT", [128, 16, 128], BF16, stack=s3)
            sc = sb("sc", [128, 16, 128], stack=s3)
            scr = sb("scrx", [128, 16, 128], stack=s3)
            s1t = sb("s1t", [128, 16, 16], stack=s3)
            idx1i = sb("idx1i", [128, 16, 16], I32, stack=s3)
            idx1f = sb("idx1f", [128, 16, 16], stack=s3)
            cand = sb("cand", [128, 8, 256], stack=s3)
            stop_ = sb("stop", [128, 8, 16], stack=s3)
            ji = sb("ji", [128, 8, 16], I32, stack=s3)
            ai = sb("ai", [128, 8, 16], I32, stack=s3)
            bi = sb("bi", [128, 8, 16], I32, stack=s3)
            af = sb("af", [128, 8, 16], stack=s3)
            bf_ = sb("bf", [128, 8, 16], stack=s3)
            sel = sb("sel", [128, 3, 128], stack=s3)
            ex = sb("ex", [128, 8, 16], stack=s3)
            ssum = sb("ssum", [128, 8], stack=s3)
            selT = sb("selT", [128, 3, 256], stack=s3)
            Bh = [sb(f"Bh{i}", [128, 8, 128], BF16, stack=s3) for i in range(2)]
            Ae = [sb(f"Ae{i}", [128, 8, 64], BF16, stack=s3) for i in range(2)]
            iota_bf = sb("iota_bf", [128, 128], BF16, stack=s3)
            S.op("dve", lambda e: e.tensor_copy(out=iota_bf[:], in_=iota128[:]), r=["iota128"], w=["iota_bf"])
            G = sb("G", [128, 64, 256], BF16, stack=s3)
            ublk = [sb(f"ublk{i}", [128, 8, 256], BF16, stack=s3) for i in range(3)]
            vblk = [sb(f"vblk{i}", [128, 2, D], BF16, stack=s3) for i in range(3)]
            gel = [sb(f"gel{i}", [128, 512], BF16, stack=s3) for i in range(2)]
            W = [sb(f"W{i}", [128, 512], BF16, stack=s3) for i in range(2)]
            eq = cand[:].rearrange("p h (a b) -> p h a b", a=16)
            apT_v = apT_d.rearrange("(k p) t -> p k t", p=128)
            out_toks = []

            def layer_norm(src, dst_n, srck, dstk):
                for c_ in range(2):
                    S.op("dve", lambda e, c_=c_: e.bn_stats(out=stats[:, c_, :], in_=src[:, c_ * 512:(c_ + 1) * 512]),
                         r=[srck], w=["stats"])
                S.op("dve", lambda e: e.bn_aggr(out=mv[:], in_=stats[:].rearrange("p a b -> p (a b)")), r=["stats"], w=["mv"])
                S.op("dve", lambda e: e.tensor_scalar_add(out=rstd[:], in0=mv[:, 1:2], scalar1=LN_EPS), r=["mv"], w=["rstd"])
                S.op("act", lambda e: e.activation(out=rstd[:], in_=rstd[:], func=AF.Sqrt), r=["rstd"], w=["rstd"])
                S.op("dve", lambda e: e.reciprocal(out=rstd[:], in_=rstd[:]), r=["rstd"], w=["rstd"])
                S.op("dve", lambda e: e.tensor_scalar(out=dst_n[:], in0=src[:], scalar1=mv[:, 0:1], scalar2=rstd[:, 0:1],
                                                      op0=ALU.subtract, op1=ALU.mult), r=[srck, "mv", "rstd"], w=[dstk])

            NST = nst
            assert NST % 2 == 0
            for pr in range(NST // 2):
                for s in (2 * pr, 2 * pr + 1):
                    t0 = s * 128
                    i = 0
                    sub = s % 2
                    x1 = x1s[sub]
                    x1k = f"x1{sub}"
                    hs = slice(sub * 128, (sub + 1) * 128)
                    S.dma(apT4[0][:], apT_v[:, :, t0:t0 + 128], w=["apT40"])
                    ap4 = apT4[0]
                    ap4k = "apT40"
                    toff = 0
                    S.dma(xt[i][:], x_d[t0:t0 + 128, :], w=[f"xt{i}"])
                    for half in range(2):
                        for k in range(8):
                            S.mm(bank(half), ap4[:, k, toff:toff + 128], w_out_bf[:, k, half * 512:(half + 1) * 512], k == 0, k == 7,
                                 r=[ap4k, "w_out_bf"], w=bk(half))
                    S.op("dve", lambda e, i=i: e.scalar_tensor_tensor(out=y[:], in0=xt[i][:], scalar=ALPHA, in1=bank(0, 2),
                                                                      op0=ALU.mult, op1=ALU.add), r=[f"xt{i}"] + bk(0, 2), w=["y"])
                    layer_norm(y, y, "y", "y")
                    S.op("pool", lambda e, x1=x1: e.tensor_tensor(out=x1[:], in0=n1[:], in1=lnb[:, 0, :], op=ALU.mult), r=["y", "lnb"], w=[x1k])
                    S.op("pool", lambda e, x1=x1: e.tensor_tensor(out=x1[:], in0=x1[:], in1=lnb[:, 1, :], op=ALU.add), r=[x1k, "lnb"], w=[x1k])
                    for k in range(8):
                        S.op("pe", lambda e, k=k: e.transpose(bank(2, 2)[:, k * 128:(k + 1) * 128], n1[:, k * 128:(k + 1) * 128], ident[:]),
                             r=["y", "ident"], w=bk(2 + k // 4))
                    for k in range(8):
                        S.op("act", lambda e, k=k, hs=hs: e.activation(out=h2T[:, k, hs], in_=bank(2, 2)[:, k * 128:(k + 1) * 128],
                                                                func=AF.Identity, scale=A2[:, k:k + 1], bias=B2[:, k:k + 1]),
                             r=bk(2 + k // 4) + ["A2", "B2"], w=["h2T"])
                    for oc in range(16):
                        pb = 4 + oc // 4
                        for k in range(8):
                            S.mm(bank(pb)[:, (oc % 4) * 128:(oc % 4 + 1) * 128], wq_bf[:, k, oc * 128:(oc + 1) * 128], h2T[:, k, hs],
                                 k == 0, k == 7, r=["wq_bf", "h2T"], w=bk(pb))
                    for b4 in range(4):
                        S.op("act" if b4 % 2 == 0 else "dve",
                             (lambda e, b4=b4: e.copy(out=qT[:, b4 * 4:b4 * 4 + 4, :], in_=bank(4 + b4).rearrange("p (a b) -> p a b", a=4)))
                             if b4 % 2 == 0 else
                             (lambda e, b4=b4: e.tensor_copy(out=qT[:, b4 * 4:b4 * 4 + 4, :], in_=bank(4 + b4).rearrange("p (a b) -> p a b", a=4))),
                             r=bk(4 + b4), w=["qT"])
                    for hp in range(16):
                        pb = 4 + hp // 4
                        S.mm(bank(pb)[:, (hp % 4) * 128:(hp % 4 + 1) * 128], qT[:, hp, :], keysT[:, hp, :], True, True,
                             r=["qT", "keysT"], w=bk(pb))
                    for b4 in range(4):
                        S.op("dve", lambda e, b4=b4: e.tensor_single_scalar(
                            out=sc[:, b4 * 4:b4 * 4 + 4, :].bitcast(I32),
                            in_=bank(4 + b4).rearrange("p (a b) -> p a b", a=4).bitcast(I32), scalar=-128, op=ALU.bitwise_and),
                            r=bk(4 + b4), w=["sc"])
                    S.op("dve", lambda e: e.tensor_tensor(out=sc[:].bitcast(I32), in0=sc[:].bitcast(I32),
                                                          in1=iota_n[:].unsqueeze(1).to_broadcast([128, 16, 128]), op=ALU.bitwise_or),
                         r=["sc", "iota_n"], w=["sc"])
                    for hp in range(16):
                        S.op("dve", lambda e, hp=hp: e.max(out=s1t[:, hp, 0:8], in_=sc[:, hp, :]), r=["sc"], w=["s1t"])
                        S.op("dve", lambda e, hp=hp: e.match_replace(out=scr[:, hp, :], in_to_replace=s1t[:, hp, 0:8],
                                                                     in_values=sc[:, hp, :], imm_value=NEG), r=["sc", "s1t"], w=["scr"])
                        S.op("dve", lambda e, hp=hp: e.max(out=s1t[:, hp, 8:16], in_=scr[:, hp, :]), r=["scr"], w=["s1t"])
                    S.op("dve", lambda e: e.tensor_single_scalar(out=idx1i[:], in_=s1t[:].bitcast(I32), scalar=127, op=ALU.bitwise_and),
                         r=["s1t"], w=["idx1i"])
                    S.op("dve", lambda e: e.tensor_copy(out=idx1f[:], in_=idx1i[:]), r=["idx1i"], w=["idx1f"])
                    s1v = s1t[:].rearrange("p (h two) a -> p h two a", two=2)
                    candv = cand[:].rearrange("p h (a b) -> p h a b", a=16)
                    S.op("dve", lambda e: e.tensor_tensor(out=candv, in0=s1v[:, :, 0, :].unsqueeze(3).to_broadcast([128, 8, 16, 16]),
                                                          in1=s1v[:, :, 1, :].unsqueeze(2).to_broadcast([128, 8, 16, 16]), op=ALU.add),
                         r=["s1t"], w=["cand"])
                    S.op("dve", lambda e: e.tensor_single_scalar(out=cand[:].bitcast(I32), in_=cand[:].bitcast(I32), scalar=-256,
                                                                 op=ALU.bitwise_and), r=["cand"], w=["cand"])
                    S.op("dve", lambda e: e.tensor_tensor(out=cand[:].bitcast(I32), in0=cand[:].bitcast(I32),
                                                          in1=iota_j[:].unsqueeze(1).to_broadcast([128, 8, 256]), op=ALU.bitwise_or),
                         r=["cand", "iota_j"], w=["cand"])
                    scr2 = scr[:].rearrange("p (h two) n -> p h (two n)", two=2)
                    for h in range(8):
                        S.op("dve", lambda e, h=h: e.max(out=stop_[:, h, 0:8], in_=cand[:, h, :]), r=["cand"], w=["stop"])
                        S.op("dve", lambda e, h=h: e.match_replace(out=scr2[:, h, :], in_to_replace=stop_[:, h, 0:8],
                                                                   in_values=cand[:, h, :], imm_value=NEG), r=["cand", "stop"], w=["scr"])
                        S.op("dve", lambda e, h=h: e.max(out=stop_[:, h, 8:16], in_=scr2[:, h, :]), r=["scr"], w=["stop"])
                    S.op("dve", lambda e: e.tensor_single_scalar(out=ji[:], in_=stop_[:].bitcast(I32), scalar=255, op=ALU.bitwise_and),
                         r=["stop"], w=["ji"])
                    S.op("dve", lambda e: e.tensor_single_scalar(out=ai[:], in_=ji[:], scalar=4, op=ALU.logical_shift_right),
                         r=["ji"], w=["ai"])
                    S.op("dve", lambda e: e.tensor_single_scalar(out=bi[:], in_=ji[:], scalar=15, op=ALU.bitwise_and),
                         r=["ji"], w=["bi"])
                    S.op("dve", lambda e: e.tensor_copy(out=af[:], in_=ai[:]), r=["ai"], w=["af"])
                    S.op("dve", lambda e: e.tensor_copy(out=bf_[:], in_=bi[:]), r=["bi"], w=["bf"])
                    idxv = idx1f[:].rearrange("p (h two) a -> p h two a", two=2)
                    for which, (srcf, srck) in enumerate(((af, "af"), (bf_, "bf"))):
                        S.op("dve", lambda e, srcf=srcf: e.tensor_tensor(
                            out=eq, in0=srcf[:].unsqueeze(3).to_broadcast([128, 8, 16, 16]),
                            in1=iota16[:].unsqueeze(1).unsqueeze(1).to_broadcast([128, 8, 16, 16]), op=ALU.is_equal),
                            r=[srck, "iota16"], w=["cand"])
                        S.op("dve", lambda e, which=which: e.tensor_tensor(
                            out=eq, in0=eq, in1=idxv[:, :, which, :].unsqueeze(2).to_broadcast([128, 8, 16, 16]), op=ALU.mult),
                            r=["cand", "idx1f"], w=["cand"])
                        S.op("dve", lambda e, which=which: e.tensor_reduce(
                            out=sel[:, which, :], in_=cand[:].rearrange("p h (k a) -> p (h k) a", a=16), axis=AX.X, op=ALU.add),
                            r=["cand"], w=["sel"])
                    S.op("dve", lambda e: e.tensor_tensor(out=ex[:], in0=stop_[:], in1=stop_[:, :, 0:1].to_broadcast([128, 8, 16]),
                                                          op=ALU.subtract), r=["stop"], w=["ex"])
                    S.op("act", lambda e: e.activation(out=ex[:], in_=ex[:], func=AF.Exp), r=["ex"], w=["ex"])
                    S.op("dve", lambda e: e.tensor_reduce(out=ssum[:], in_=ex[:], axis=AX.X, op=ALU.add), r=["ex"], w=["ssum"])
                    S.op("dve", lambda e: e.reciprocal(out=ssum[:], in_=ssum[:]), r=["ssum"], w=["ssum"])
                    S.op("dve", lambda e: e.tensor_tensor(out=sel[:, 2, :].rearrange("p (h k) -> p h k", h=8), in0=ex[:],
                                                          in1=ssum[:].unsqueeze(2).to_broadcast([128, 8, 16]), op=ALU.mult),
                         r=["ex", "ssum"], w=["sel"])
                    if debug and s == 0:
                        S.dma(dbg["sel"], sel[:], r=["sel"])
                        S.dma(dbg["x1"], x1[:], r=[x1k])
                    for w_ in range(3):
                        S.op("pe", lambda e, w_=w_: e.transpose(bank(2)[:, w_ * 128:(w_ + 1) * 128], sel[:, w_, :], ident[:]),
                             r=["sel", "ident"], w=bk(2))
                    S.op("dve", lambda e, hs=hs: e.tensor_copy(out=selT[:, :, hs], in_=bank(2)[:, 0:384].rearrange("p (a b) -> p a b", a=3)), r=bk(2), w=["selT"])
                def onehot_G(half):
                    ng = 0
                    for c0 in range(0, 256, 8):
                        ob_ = (c0 // 8) % 2
                        for tt_ in range(8):
                            t_ = c0 + tt_
                            S.op("dve", lambda e, t_=t_, tt_=tt_, ob_=ob_: e.tensor_scalar(
                                out=Bh[ob_][:, tt_, :], in0=iota_bf[:], scalar1=selT[:, 1, t_:t_ + 1], scalar2=None, op0=ALU.is_equal),
                                r=["selT", "iota_bf"], w=[f"Bh{ob_}"])
                            S.op("dve", lambda e, t_=t_, tt_=tt_, ob_=ob_: e.tensor_scalar(
                                out=Ae[ob_][:, tt_, :], in0=iota_bf[:, half * 64:(half + 1) * 64], scalar1=selT[:, 0, t_:t_ + 1],
                                scalar2=selT[:, 2, t_:t_ + 1], op0=ALU.is_equal, op1=ALU.mult), r=["selT", "iota_bf"], w=[f"Ae{ob_}"])
                        pb = 6 + (ng % 2)
                        ng += 1
                        for tt_ in range(8):
                            S.mm(bank(pb)[:, tt_ * 64:(tt_ + 1) * 64], Bh[ob_][:, tt_, :], Ae[ob_][:, tt_, :], True, True,
                                 r=[f"Bh{ob_}", f"Ae{ob_}"], w=bk(pb))
                        S.op("act", lambda e, pb=pb, c0=c0: e.copy(out=G[:, :, c0:c0 + 8],
                                                                   in_=bank(pb).rearrange("p (t i) -> p i t", t=8)),
                             r=bk(pb), w=["G"])

                def dense(half):
                    def emit_dma(g):
                        gg = half * 32 + g
                        sl_ = gg % 3
                        S.dma(ublk[sl_][:], ubf_d[:, :, gg * 256:(gg + 1) * 256], w=[f"ublk{sl_}"])
                        S.dma(vblk[sl_][:], vbf_d[:, gg * 2:gg * 2 + 2, :], w=[f"vblk{sl_}"])

                    def emit_ST(g):
                        gg = half * 32 + g
                        sl_ = gg % 3
                        pst = 4 + gg % 2
                        for i4 in range(2):
                            for k in range(8):
                                S.mm(bank(pst)[:, i4 * 256:(i4 + 1) * 256], ublk[sl_][:, k, i4 * 128:(i4 + 1) * 128], h2T[:, k, :],
                                     k == 0, k == 7, r=[f"ublk{sl_}", "h2T"], w=bk(pst))
                    emit_dma(0)
                    emit_dma(1)
                    emit_ST(0)
                    for g in range(32):
                        gg = half * 32 + g
                        bi_ = gg % 2
                        sl_ = gg % 3
                        pst = 4 + bi_
                        S.op("act", lambda e, bi_=bi_, pst=pst: e.activation(out=gel[bi_][:], in_=bank(pst), func=AF.Gelu),
                             r=bk(pst), w=[f"gel{bi_}"])
                        if g + 2 < 32:
                            emit_dma(g + 2)
                        if g + 1 < 32:
                            emit_ST(g + 1)
                        S.op("dve", lambda e, bi_=bi_, g=g: e.tensor_tensor(
                            out=W[bi_][:], in0=gel[bi_][:], in1=G[:, g * 2:g * 2 + 2, :].rearrange("p a t -> p (a t)"), op=ALU.mult),
                            r=[f"gel{bi_}", "G"], w=[f"W{bi_}"])
                        for i4 in range(2):
                            for sub in range(2):
                                for hf in range(2):
                                    S.mm(bank(sub * 2 + hf), W[bi_][:, i4 * 256 + sub * 128:i4 * 256 + (sub + 1) * 128],
                                         vblk[sl_][:, i4, hf * 512:(hf + 1) * 512],
                                         half == 0 and g == 0 and i4 == 0, half == 1 and g == 31 and i4 == 1,
                                         r=[f"W{bi_}", f"vblk{sl_}"], w=bk(sub * 2 + hf))

                for half in range(2):
                    onehot_G(half)
                    dense(half)
                if debug and pr == 0:
                    S.op("dve", lambda e: e.tensor_copy(out=y[:], in_=bank(0, 2)), r=bk(0, 2), w=["y"])
                    S.dma(dbg["ffn"], y[:], r=["y"])
                for s in (2 * pr, 2 * pr + 1):
                    t0 = s * 128
                    i = 0
                    sub = s % 2
                    x1 = x1s[sub]
                    x1k = f"x1{sub}"
                    S.op("dve", lambda e, x1=x1, sub=sub: e.scalar_tensor_tensor(out=y[:], in0=x1[:], scalar=ALPHA, in1=bank(sub * 2, 2),
                                                                 op0=ALU.mult, op1=ALU.add), r=[x1k] + bk(sub * 2, 2), w=["y"])
                    layer_norm(y, y, "y", "y")
                    S.op("dve", lambda e, i=i: e.tensor_tensor(out=ot[i][:], in0=n1[:], in1=lnb[:, 2, :], op=ALU.mult),
                         r=["y", "lnb"], w=[f"ot{i}"])
                    S.op("dve", lambda e, i=i: e.tensor_tensor(out=ot[i][:], in0=ot[i][:], in1=lnb[:, 3, :], op=ALU.add),
                         r=[f"ot{i}", "lnb"], w=[f"ot{i}"])
                    out_toks.append(S.dma(out_d[t0:t0 + 128, :], ot[i][:], r=[f"ot{i}"], w=["out_d"]))
            S.barrier()
        stg1.close()
        print("bass ops:", S.nops, "sems:", S.nsem)
    return nc


def _host_inputs(inp, b):
    f32 = np.float32
    x = np.asarray(inp["x"][b], f32)
    fm = lambda v, k: np.ascontiguousarray(np.asarray(v, f32).reshape(k, 128).T)
    w_in = np.asarray(inp["w_in"][0], f32)
    w_in_ext = np.concatenate([w_in, w_in[:, 928:960], w_in[:, 896:928]], axis=1)
    wuq = np.asarray(inp["w_uq"][0], f32).reshape(384, NH, 192)
    wuq_ext = np.concatenate([wuq, wuq[:, :, 160:192], wuq[:, :, 128:160]], axis=2).reshape(384, NH * 256)
    inv_freq = (10000.0 ** (-np.arange(0, 64, 2, dtype=np.float32) / 64)).astype(f32)
    invf = np.zeros((64, 2), f32)
    invf[:, 0] = np.concatenate([inv_freq, inv_freq])
    invf[:, 1] = np.concatenate([-np.ones(32, f32), np.ones(32, f32)])
    wins = np.array([2, 4, 8, 16], f32)
    invw = np.zeros((128, 2), f32)
    invc0 = np.zeros((128, 2, 512), f32)
    t = np.arange(512, dtype=f32)
    for g in range(4):
        rows = slice((g % 2) * 64, (g % 2) * 64 + 64)
        invw[rows, g // 2] = 1.0 / wins[g]
        invc0[rows, g // 2, :] = 1.0 / np.minimum(t + 1, wins[g])
    kp = np.arange(128)[:, None]
    qf = np.arange(512)[None, :]
    mask = np.stack([(qf >= i * 128 + kp).astype(f32) for i in range(4)], axis=1).reshape(128, 2048)
    b_ada = np.asarray(inp["b_ada"][0], f32)
    ln_rows = np.stack([inp["ln1_g"][0], inp["ln1_b"][0], inp["ln2_g"][0], inp["ln2_b"][0]]).astype(f32)
    keysT = np.ascontiguousarray(np.asarray(inp["peer_keys"][0], f32).reshape(16, 128, 128).transpose(2, 0, 1)).reshape(128, 2048)
    return {
        "xT": np.ascontiguousarray(x.T), "x": x, "c": fm(inp["c"][b], 8),
        "pos": np.asarray(inp["positions"][b], np.int32).reshape(1, SEQ), "invf": invf,
        "w_ada": np.asarray(inp["w_ada"][0], f32), "b_ada_fm": fm(b_ada, 48), "b_ada_row": b_ada.reshape(1, -1),
        "w_in_ext": np.ascontiguousarray(w_in_ext), "pool_w": np.asarray(inp["pool_w"][0], f32),
        "pool_scale_fm": fm(inp["pool_scale"][0], 2), "invw": invw, "invc0": invc0,
        "q_norm_g_fm": fm(inp["q_norm_g"][0], 3), "kv_norm_g_fm": fm(inp["kv_norm_g"][0], 2),
        "w_uq_ext": np.ascontiguousarray(wuq_ext), "w_ukv": np.asarray(inp["w_ukv"][0], f32),
        "w_out": np.asarray(inp["w_out"][0], f32), "ln_rows": ln_rows,
        "ln1_fm": np.concatenate([fm(inp["ln1_g"][0], 8), fm(inp["ln1_b"][0], 8)], axis=1),
        "w_peer_q": np.asarray(inp["w_peer_q"][0], f32), "keysT": keysT,
        "uT": np.ascontiguousarray(np.asarray(inp["peer_u"][0], f32).T), "v": np.asarray(inp["peer_v"][0], f32),
        "mask": mask, "ident": np.eye(128, dtype=f32),
        "iota_n": np.tile(np.arange(128, dtype=np.int32), (128, 1)),
        "iota_j": np.tile(np.arange(256, dtype=np.int32), (128, 1)),
        "iota16": np.tile(np.arange(16, dtype=f32), (128, 1)),
        "iota128": np.tile(np.arange(128, dtype=f32), (128, 1)),
    }


def kernel(**inputs):
    nc = build(False)
    shared = None
    in_maps = []
    for b in range(8):
        m = _host_inputs(inputs, b)
        if shared is None:
            shared = m
        else:
            for k in m:
                if k not in ("xT", "x", "c", "pos"):
                    m[k] = shared[k]
        in_maps.append(m)
    res = run_bass_kernel_spmd(nc, in_maps, core_ids=list(range(8)))
    return np.stack([np.asarray(r["out"], np.float32) for r in res.results], axis=0)
```
